# Optimizing a Trainium2 kernel written in Bass

```python
import math
import jax, jax.numpy as jnp
from jax import lax
import numpy as np

D_MODEL = 1024
BATCH = 8
SEQ = 4096
DEPTH = 2

GRID_W = 64
CTX_LEN = 256
HEAD_DIM = 64
ROPE_BASE = 10000.0
EPS = 1e-6
NEG_INF = -1e30
ATTN_SCALE = HEAD_DIM ** -0.5
Q_BLOCK = 128
CHUNK = 128
A_GROUPS = 4
A_GROUP_DIM = 128
A_WIDTH = A_GROUPS * A_GROUP_DIM
B_HEADS = 4
B_QK_WIDTH = 2 * B_HEADS * HEAD_DIM
B_V_DIM = 2 * HEAD_DIM
B_V_WIDTH = B_HEADS * B_V_DIM
KV_START = 2 * A_WIDTH + B_QK_WIDTH
EVEN_IN = 2 * A_WIDTH + 2 * B_QK_WIDTH + B_V_WIDTH
EVEN_OUT = A_WIDTH + B_V_WIDTH
C_HEADS = 16
C_KV_HEADS = 4
C_GROUP = C_HEADS // C_KV_HEADS
WINDOW = 128
C_Q_WIDTH = C_HEADS * HEAD_DIM
C_KV_WIDTH = C_KV_HEADS * HEAD_DIM
C_IN = C_Q_WIDTH + 2 * C_KV_WIDTH
FFN_HIDDEN = ((8 * D_MODEL + 3 * 256 - 1) // (3 * 256)) * 256
N_EVEN = (DEPTH + 1) // 2
N_ODD = DEPTH // 2

kernel_name = 'hybrid_prefix_gmlp_diffattn_swa_block'


def rmsnorm(x, g):
    xf = x.astype(jnp.float32)
    y = xf * lax.rsqrt(jnp.mean(xf * xf, axis=-1, keepdims=True) + EPS)
    return (y * g.astype(jnp.float32)).astype(x.dtype)


def modulate(x, g, shift, scale):
    return rmsnorm(x, g) * (1 + scale) + shift


def rope_tables(s_len, dtype):
    rows = s_len // GRID_W
    row = jnp.repeat(jnp.arange(rows), GRID_W).astype(jnp.float32)
    col = jnp.tile(jnp.arange(GRID_W), rows).astype(jnp.float32)
    nf = HEAD_DIM // 4
    inv = ROPE_BASE ** (-jnp.arange(nf, dtype=jnp.float32) / nf)
    ang = jnp.stack([row[:, None] * inv, col[:, None] * inv], axis=1)
    ang = jnp.broadcast_to(ang[:, :, None, :], (s_len, 2, 2, nf)).reshape(s_len, HEAD_DIM)
    return jnp.cos(ang).astype(dtype), jnp.sin(ang).astype(dtype)


def apply_rope(x, cos, sin):
    xr = x.reshape(x.shape[:-1] + (2, 2, HEAD_DIM // 4))
    rot = jnp.stack([-xr[..., 1, :], xr[..., 0, :]], axis=-2).reshape(x.shape)
    return x * cos[None, :, None, :] + rot * sin[None, :, None, :]


def swiglu(h, w_in, w_out):
    g, u = jnp.split(h @ w_in, 2, axis=-1)
    return (jax.nn.silu(g) * u) @ w_out


def chunk_mlp(u, v, w_s, b_s):
    b, t, _ = u.shape
    shp = (b, t // CHUNK, CHUNK, A_GROUPS, A_GROUP_DIM)
    u = jax.nn.gelu(u, approximate=False).reshape(shp)
    vf = jax.nn.gelu(v.astype(jnp.float32), approximate=False).reshape(shp)
    mu = jnp.mean(vf, axis=-1, keepdims=True)
    vf = (vf - mu) * lax.rsqrt(jnp.mean(jnp.square(vf - mu), axis=-1, keepdims=True) + EPS)
    mixed = jnp.einsum('gpq,bnqgc->bnpgc', w_s.astype(jnp.float32), vf) + b_s.astype(jnp.float32).T[None, None, :, :, None]
    return (u * mixed.astype(u.dtype)).reshape(b, t, A_WIDTH)


def diff_attn(q, k, v, lam):
    b, tq = q.shape[0], q.shape[1]
    tk = k.shape[1]
    s = jnp.einsum('bqhd,bkhd->bhqk', q, k).astype(jnp.float32) * ATTN_SCALE
    p = jax.nn.softmax(s, axis=-1).reshape(b, B_HEADS, 2, tq, tk)
    a = p[:, :, 0] - lam * p[:, :, 1]
    return jnp.einsum('bhqk,bkhe->bqhe', a.astype(v.dtype), v)


def diff_out(o, g, lam_init):
    b, t = o.shape[0], o.shape[1]
    return (rmsnorm(o, g) * (1.0 - lam_init)).reshape(b, t, B_V_WIDTH)


def sink_attend(q, kvs, sink):
    scores = []
    for k, v, mask in kvs:
        s = jnp.einsum('bqhgd,bkhd->bhgqk', q, k).astype(jnp.float32) * ATTN_SCALE
        if mask is not None:
            s = jnp.where(mask, s, NEG_INF)
        scores.append(s)
    s_sink = jnp.broadcast_to(sink.astype(jnp.float32).reshape(1, C_KV_HEADS, C_GROUP, 1, 1), scores[0].shape[:-1] + (1,))
    p = jax.nn.softmax(jnp.concatenate(scores + [s_sink], axis=-1), axis=-1)
    out = None
    start = 0
    for k, v, _ in kvs:
        tk = k.shape[1]
        o = jnp.einsum('bhgqk,bkhd->bqhgd', p[..., start:start + tk].astype(v.dtype), v)
        out = o if out is None else out + o
        start += tk
    return out


def even_mixer(h, hc, w_in, w_out, w_s, b_s, lq1, lk1, lq2, lk2, subln_g, lam_init, cos, sin, need_ctx):
    b, s_len, _ = h.shape
    nb = s_len // Q_BLOCK
    f32 = jnp.float32
    lam = (jnp.exp(jnp.sum(lq1.astype(f32) * lk1.astype(f32)))
           - jnp.exp(jnp.sum(lq2.astype(f32) * lk2.astype(f32))) + lam_init)
    u, v, q, k, vv = jnp.split(h @ w_in, [A_WIDTH, 2 * A_WIDTH, KV_START, KV_START + B_QK_WIDTH], axis=-1)
    q = apply_rope(q.reshape(b, s_len, 2 * B_HEADS, HEAD_DIM), cos, sin)
    k = apply_rope(k.reshape(b, s_len, 2 * B_HEADS, HEAD_DIM), cos, sin)
    vv = vv.reshape(b, s_len, B_HEADS, B_V_DIM)
    kc, vvc = jnp.split(hc @ w_in[:, KV_START:], [B_QK_WIDTH], axis=-1)
    kc = kc.reshape(b, -1, 2 * B_HEADS, HEAD_DIM)
    vvc = vvc.reshape(b, -1, B_HEADS, B_V_DIM)
    k_all = jnp.concatenate([k, kc], axis=1)
    v_all = jnp.concatenate([vv, vvc], axis=1)
    q_blocks = q.reshape(b, nb, Q_BLOCK, 2 * B_HEADS, HEAD_DIM).transpose(1, 0, 2, 3, 4)
    o = lax.map(lambda qb: diff_attn(qb, k_all, v_all, lam), q_blocks)
    o = o.transpose(1, 0, 2, 3, 4).reshape(b, s_len, B_HEADS, B_V_DIM)
    y = jnp.concatenate([chunk_mlp(u, v, w_s, b_s), diff_out(o, subln_g, lam_init)], axis=-1) @ w_out
    yc = None
    if need_ctx:
        uc, vc_, qc = jnp.split(hc @ w_in[:, :KV_START], [A_WIDTH, 2 * A_WIDTH], axis=-1)
        qc = qc.reshape(b, -1, 2 * B_HEADS, HEAD_DIM)
        oc = diff_attn(qc, kc, vvc, lam)
        yc = jnp.concatenate([chunk_mlp(uc, vc_, w_s, b_s), diff_out(oc, subln_g, lam_init)], axis=-1) @ w_out
    return y, yc


def odd_mixer(h, hc, w_qkv, w_out, sink, cos, sin, need_ctx):
    b, s_len, _ = h.shape
    nb = s_len // Q_BLOCK
    q, k, v = jnp.split(h @ w_qkv, [C_Q_WIDTH, C_Q_WIDTH + C_KV_WIDTH], axis=-1)
    q = apply_rope(q.reshape(b, s_len, C_HEADS, HEAD_DIM), cos, sin).reshape(b, s_len, C_KV_HEADS, C_GROUP, HEAD_DIM)
    k = apply_rope(k.reshape(b, s_len, C_KV_HEADS, HEAD_DIM), cos, sin)
    v = v.reshape(b, s_len, C_KV_HEADS, HEAD_DIM)
    kc, vc = jnp.split(hc @ w_qkv[:, C_Q_WIDTH:], 2, axis=-1)
    kc = kc.reshape(b, -1, C_KV_HEADS, HEAD_DIM)
    vc = vc.reshape(b, -1, C_KV_HEADS, HEAD_DIM)
    pad = ((0, 0), (WINDOW, WINDOW), (0, 0), (0, 0))
    k_pad = jnp.pad(k, pad)
    v_pad = jnp.pad(v, pad)
    band = Q_BLOCK + 2 * WINDOW
    q_blocks = q.reshape(b, nb, Q_BLOCK, C_KV_HEADS, C_GROUP, HEAD_DIM).transpose(1, 0, 2, 3, 4, 5)

    def block(args):
        i, qb = args
        start = i * Q_BLOCK
        kb = lax.dynamic_slice_in_dim(k_pad, start, band, axis=1)
        vb = lax.dynamic_slice_in_dim(v_pad, start, band, axis=1)
        qpos = start + jnp.arange(Q_BLOCK)
        kpos = start - WINDOW + jnp.arange(band)
        mask = ((kpos >= 0)[None, :] & (kpos < s_len)[None, :]
                & (jnp.abs(qpos[:, None] - kpos[None, :]) <= WINDOW))
        return sink_attend(qb, [(kb, vb, mask), (kc, vc, None)], sink)

    o = lax.map(block, (jnp.arange(nb), q_blocks))
    y = o.transpose(1, 0, 2, 3, 4, 5).reshape(b, s_len, C_Q_WIDTH) @ w_out
    yc = None
    if need_ctx:
        qc = (hc @ w_qkv[:, :C_Q_WIDTH]).reshape(b, -1, C_KV_HEADS, C_GROUP, HEAD_DIM)
        yc = sink_attend(qc, [(kc, vc, None)], sink).reshape(b, -1, C_Q_WIDTH) @ w_out
    return y, yc


def setup_inputs(seed: int = 0) -> dict:
    key = jax.random.key(seed)
    ks = jax.random.split(key, 24)
    n = jax.random.normal
    f32 = jnp.float32
    return {
        'x': n(ks[0], (BATCH, SEQ, D_MODEL), f32),
        'c': n(ks[1], (BATCH, D_MODEL), f32),
        'ctx': n(ks[2], (BATCH, CTX_LEN, D_MODEL), f32),
        'c_ctx': n(ks[3], (D_MODEL,), f32),
        'ada_w': n(ks[4], (DEPTH, D_MODEL, 6 * D_MODEL), f32) * (0.5 * D_MODEL ** -0.5),
        'ada_b': n(ks[5], (DEPTH, 6 * D_MODEL), f32) * 0.02,
        'norm1_g': 1.0 + 0.02 * n(ks[6], (DEPTH, D_MODEL), f32),
        'norm2_g': 1.0 + 0.02 * n(ks[7], (DEPTH, D_MODEL), f32),
        'ffn_w_in': n(ks[8], (DEPTH, D_MODEL, 2 * FFN_HIDDEN), f32) * D_MODEL ** -0.5,
        'ffn_w_out': n(ks[9], (DEPTH, FFN_HIDDEN, D_MODEL), f32) * FFN_HIDDEN ** -0.5,
        'even_w_in': n(ks[10], (N_EVEN, D_MODEL, EVEN_IN), f32) * D_MODEL ** -0.5,
        'even_w_out': n(ks[11], (N_EVEN, EVEN_OUT, D_MODEL), f32) * EVEN_OUT ** -0.5,
        'sgu_w': n(ks[12], (N_EVEN, A_GROUPS, CHUNK, CHUNK), f32) * CHUNK ** -0.5,
        'sgu_b': 1.0 + 0.1 * n(ks[13], (N_EVEN, A_GROUPS, CHUNK), f32),
        'diff_lq1': 0.1 * n(ks[14], (N_EVEN, HEAD_DIM), f32),
        'diff_lk1': 0.1 * n(ks[15], (N_EVEN, HEAD_DIM), f32),
        'diff_lq2': 0.1 * n(ks[16], (N_EVEN, HEAD_DIM), f32),
        'diff_lk2': 0.1 * n(ks[17], (N_EVEN, HEAD_DIM), f32),
        'diff_subln_g': 1.0 + 0.02 * n(ks[18], (N_EVEN, B_V_DIM), f32),
        'odd_w_qkv': n(ks[19], (N_ODD, D_MODEL, C_IN), f32) * D_MODEL ** -0.5,
        'odd_w_out': n(ks[20], (N_ODD, C_Q_WIDTH, D_MODEL), f32) * C_Q_WIDTH ** -0.5,
        'odd_sink': 0.5 * n(ks[21], (N_ODD, C_HEADS), f32),
        'final_g': 1.0 + 0.02 * n(ks[22], (D_MODEL,), f32),
    }


def reference(x, c, ctx, c_ctx, ada_w, ada_b, norm1_g, norm2_g, ffn_w_in, ffn_w_out,
              even_w_in, even_w_out, sgu_w, sgu_b, diff_lq1, diff_lk1, diff_lq2, diff_lk2,
              diff_subln_g, odd_w_qkv, odd_w_out, odd_sink, final_g):
    s_len = x.shape[1]
    cos, sin = rope_tables(s_len, x.dtype)
    silu_c = jax.nn.silu(c)
    silu_cc = jax.nn.silu(c_ctx)
    h, hc = x, ctx
    for l in range(DEPTH):
        need_ctx = l < DEPTH - 1
        j = l // 2
        mod = silu_c @ ada_w[l] + ada_b[l]
        modc = silu_cc @ ada_w[l] + ada_b[l]
        sh1, sc1, g1, sh2, sc2, g2 = [m[:, None, :] for m in jnp.split(mod, 6, axis=-1)]
        csh1, csc1, cg1, csh2, csc2, cg2 = [m[None, None, :] for m in jnp.split(modc, 6, axis=-1)]
        a = modulate(h, norm1_g[l], sh1, sc1)
        ac = modulate(hc, norm1_g[l], csh1, csc1)
        if l % 2 == 0:
            lam_init = 0.8 - 0.6 * math.exp(-0.3 * l)
            y, yc = even_mixer(a, ac, even_w_in[j], even_w_out[j], sgu_w[j], sgu_b[j],
                               diff_lq1[j], diff_lk1[j], diff_lq2[j], diff_lk2[j], diff_subln_g[j],
                               lam_init, cos, sin, need_ctx)
        else:
            y, yc = odd_mixer(a, ac, odd_w_qkv[j], odd_w_out[j], odd_sink[j], cos, sin, need_ctx)
        h = h + g1 * y
        h = h + g2 * swiglu(modulate(h, norm2_g[l], sh2, sc2), ffn_w_in[l], ffn_w_out[l])
        if need_ctx:
            hc = hc + cg1 * yc
            hc = hc + cg2 * swiglu(modulate(hc, norm2_g[l], csh2, csc2), ffn_w_in[l], ffn_w_out[l])
    return rmsnorm(h, final_g)
```

```python
import contextlib
import math
import os
import numpy as np
import ml_dtypes
import concourse.bass as bass
import concourse.mybir as mybir
from concourse.bass_utils import run_bass_kernel_spmd

F32 = mybir.dt.float32
BF16 = mybir.dt.bfloat16
AF = mybir.ActivationFunctionType
ALU = mybir.AluOpType
AX = mybir.AxisListType

D = 1024
SEQ = 4096
CTX = 256
NTOK = SEQ + CTX
NT = NTOK // 128
DEPTH = 2
EPS = 1e-6
FFN = 2816
NF = FFN // 128
GRID_W = 64
HEAD_DIM = 64
SCALE = HEAD_DIM ** -0.5
LAM_INIT0 = 0.8 - 0.6 * math.exp(-0.3 * 0)


class Buf:
    __slots__ = ("name", "writer", "readers", "sem", "semval")

    def __init__(self, name):
        self.name = name
        self.writer = None
        self.readers = {}
        self.sem = None
        self.semval = 0


class Ins:
    __slots__ = ("key", "sem", "val", "is_dma")

    def __init__(self, key, sem=None, val=None, is_dma=False):
        self.key = key
        self.sem = sem
        self.val = val
        self.is_dma = is_dma


class Sched:
    CE = ("pe", "act", "dve", "pool")

    def __init__(self, nc, es):
        self.nc = nc
        self.es = es
        self.E = {"pe": nc.tensor, "act": nc.scalar, "dve": nc.vector, "pool": nc.gpsimd, "sp": nc.sync}
        self.sem = {e: es.enter_context(nc.semaphore("sem_" + e)) for e in self.CE}
        self.cnt = {e: 0 for e in self.CE}
        self.waited = {}
        self.pending = {e: [] for e in self.CE}
        self.dma_bufs = []
        self.sem_pool = []
        self.sw_bufs = []
        self.n_dsem = 0
        self.n_ins = 0
        self.n_wait = 0

    def _wait(self, stream, d):
        if d.val is None:
            raise RuntimeError("dependency on unsignalled instruction of " + d.key)
        k = (stream, d.sem.name)
        if self.waited.get(k, 0) >= d.val:
            return
        self.waited[k] = d.val
        self.E[stream].wait_ge(d.sem, d.val)
        self.n_wait += 1

    def _deps(self, stream, r, w, is_dma):
        def skip(d):
            return d.key == "pe" and stream == "pe" and not is_dma
        for b in r:
            d = b.writer
            if d is not None and not skip(d):
                self._wait(stream, d)
        for b in w:
            d = b.writer
            if d is not None and not skip(d):
                self._wait(stream, d)
            for d in b.readers.values():
                if not skip(d):
                    self._wait(stream, d)

    def op(self, eng, fn, r=(), w=(), signal=True):
        self._deps(eng, r, w, False)
        inst = fn()
        self.n_ins += 1
        ins = Ins(eng)
        if signal:
            self.cnt[eng] += 1
            inst.then_inc(self.sem[eng], 1)
            ins.sem = self.sem[eng]
            ins.val = self.cnt[eng]
            for p in self.pending[eng]:
                p.sem = ins.sem
                p.val = ins.val
            self.pending[eng] = []
        else:
            self.pending[eng].append(ins)
        for b in r:
            b.readers[eng] = ins
        for b in w:
            b.writer = ins
            b.readers = {}
        return ins

    def dma(self, q, out, in_, chan, r=(), w=()):
        self._deps(q, r, w, True)
        if q == "pool":
            chan = Buf(chan.name + "_sw%d" % self.n_dsem)
            chan.sem = self.es.enter_context(self.nc.semaphore("w%d" % self.n_dsem))
            self.n_dsem += 1
            self.sw_bufs.append(chan)
        if chan.sem is None:
            if self.sem_pool:
                chan.sem, chan.semval = self.sem_pool.pop()
            else:
                chan.sem = self.es.enter_context(self.nc.semaphore("d%d" % self.n_dsem))
                self.n_dsem += 1
            self.dma_bufs.append(chan)
        chan.semval += 16
        self.E[q].dma_start(out=out, in_=in_).then_inc(chan.sem, 16)
        self.n_ins += 1
        ins = Ins("dma_" + chan.name, chan.sem, chan.semval, True)
        for b in r:
            b.readers[ins.key] = ins
        for b in w:
            b.writer = ins
            b.readers = {}
        return ins

    def finish(self):
        for b in self.dma_bufs + self.sw_bufs:
            self._wait("sp", Ins("x", b.sem, b.semval, True))
        for e in self.CE:
            if self.cnt[e]:
                self._wait("sp", Ins(e, self.sem[e], self.cnt[e]))


class K:
    pass


def _bufs(prefix, n):
    return [Buf("%s%d" % (prefix, i)) for i in range(n)]


def build(stop_after=None, dev=False):
    nc = bass.Bass("TRN2", target_bir_lowering=False)
    k = K()
    k.nc = nc
    k.dev = dev

    def din(name, shape, dt=F32):
        return nc.dram_tensor(name, list(shape), dt, kind="ExternalInput").ap()

    def dscr(name, shape, dt):
        return nc.dram_tensor(name, list(shape), dt, kind="ExternalOutput" if dev else "Internal").ap()

    I = {}
    I["x"] = din("x", [NTOK, D])
    I["c_col"] = din("c_col", [128, 16])
    I["ada_w"] = din("ada_w", [DEPTH, D, 6 * D])
    I["ada_b_bc"] = din("ada_b_bc", [DEPTH, 128, 6 * D])
    I["ng_bc"] = din("ng_bc", [5, 128, D])
    I["ffn_w_in"] = din("ffn_w_in", [DEPTH, D, 2 * FFN])
    I["ffn_w_out"] = din("ffn_w_out", [DEPTH, FFN, D])
    I["even_w_in"] = din("even_w_in", [D, 2560])
    I["even_w_out"] = din("even_w_out", [D, D])
    I["sgu_wT"] = din("sgu_wT", [128, 512])
    I["sgu_b_bc"] = din("sgu_b_bc", [128, 512])
    I["lqk_bc"] = din("lqk_bc", [128, 256])
    I["subln_col"] = din("subln_col", [128, 1])
    I["odd_w_qkv"] = din("odd_w_qkv", [D, 1536])
    I["odd_w_out"] = din("odd_w_out", [D, D])
    I["sink_col"] = din("sink_col", [128, 8])
    I["cosT"] = din("cosT", [128, SEQ])
    I["sinT"] = din("sinT", [128, SEQ])
    I["cbf"] = din("cbf", [128, 768])
    I["cf32"] = din("cf32", [128, 960])
    k.I = I
    k.out = nc.dram_tensor("out", [SEQ, D], F32, kind="ExternalOutput").ap()

    S = {}
    S["h"] = dscr("h_scr", [NTOK, D], F32)
    S["modbc"] = dscr("modbc_scr", [6, 128, D], F32)
    S["q"] = dscr("q_scr", [8, 128, NTOK], BF16)
    S["k"] = dscr("k_scr", [4, 128, NTOK], BF16)
    S["v"] = dscr("v_scr", [NTOK, 512], BF16)
    S["o"] = dscr("o_scr", [8, 128, NTOK], BF16)
    S["hm"] = dscr("hm_scr", [NF, 128, NTOK], BF16)
    if dev:
        S["dbg"] = dscr("dbg_scr", [128, 256], F32)
    k.S = S
    k.Bh = _bufs("Dh", NT)
    k.Bmod = _bufs("Dmod", 6)
    k.Bq = [[Buf("Dq%d_%d" % (c, b)) for b in range(9)] for c in range(8)]
    k.Bk = [[Buf("Dk%d_%d" % (c, b)) for b in range(9)] for c in range(4)]
    k.Bv = _bufs("Dv", NT)
    k.Bo = [[Buf("Do%d_%d" % (c, b)) for b in range(9)] for c in range(8)]
    k.Bhm = [[Buf("Dhm%d_%d" % (f, b)) for b in range(9)] for f in range(NF)]

    with contextlib.ExitStack() as es:
        sc = Sched(nc, es)
        k.sc = sc
        k.es = es
        k.marks = []
        k.marks = []
        phases = [phase_prologue, phase_A0, phase_B0, phase_C1_l0, phase_C2_l0,
                  phase_A1, phase_B1, phase_C1_l1, phase_C2_l1]
        for ph in phases:
            with contextlib.ExitStack() as pes:
                k.pes = pes
                ph(k)
            k.marks.append((ph.__name__, dict(sc.cnt)))
            k.marks.append((ph.__name__, dict(sc.cnt)))
            if stop_after is not None and ph.__name__ == stop_after:
                break
        sc.finish()
    k.n_ins = sc.n_ins
    k.n_wait = sc.n_wait
    return nc, k


def sbt(k, st, name, shape, dt):
    return st.enter_context(k.nc.sbuf_tensor(name, list(shape), dt))


def pst(k, st, name, shape, dt=F32):
    return st.enter_context(k.nc.psum_tensor(name, list(shape), dt))


def mm_group(k, out_ap, outB, pairs, rB, signal_last=True):
    sc, nc = k.sc, k.nc
    n = len(pairs)
    for i, (l, r) in enumerate(pairs):
        sc.op("pe", lambda: nc.tensor.matmul(out_ap, lhsT=l, rhs=r, start=(i == 0), stop=(i == n - 1)),
              r=rB, w=[outB], signal=(signal_last and i == n - 1))


def barrier(k):
    sc = k.sc
    for e in sc.CE:
        assert not sc.pending[e], "unsignalled tail on " + e
    for s in ("pe", "act", "dve", "pool", "sp"):
        for e in sc.CE:
            if sc.cnt[e]:
                sc._wait(s, Ins(e, sc.sem[e], sc.cnt[e]))
        for b in sc.dma_bufs + sc.sw_bufs:
            sc._wait(s, Ins("x", b.sem, b.semval, True))
    sc.sw_bufs = []
    for b in sc.dma_bufs:
        sc.sem_pool.append((b.sem, b.semval))
        b.sem = None
    sc.dma_bufs = []


def modidx(l, s, v):
    return ((l * 2 + s) * 4 + v) * 8


class NormCtx:
    def __init__(self, k, st, tag):
        self.junks = [sbt(k, st, tag + "junk%d" % i, [128, D], BF16) for i in range(2)]
        self.junkB = _bufs(tag + "junk", 2)
        self.nj = 0
        self.ss = sbt(k, st, tag + "ss", [128, 4], F32)
        self.sq = sbt(k, st, tag + "sq", [128, 4], F32)
        self.rs = sbt(k, st, tag + "rs", [128, 4], F32)
        self.ssB, self.sqB, self.rsB = Buf(tag + "ss"), Buf(tag + "sq"), Buf(tag + "rs")
        self.xn = [sbt(k, st, tag + "xn%d" % i, [128, D], BF16) for i in range(2)]
        self.xnB = _bufs(tag + "xn", 2)
        self.psT = [pst(k, st, tag + "psT%d" % i, [128, 8, 128], BF16) for i in range(2)]
        self.psTB = _bufs(tag + "psT", 2)
        self.n = 0


def norm_stats(k, nx, tiles):
    sc, nc = k.sc, k.nc
    nt = len(tiles)
    for i, (ap, B) in enumerate(tiles):
        jj = nx.nj % 2
        nx.nj += 1
        sc.op("act", lambda: nc.scalar.activation(out=nx.junks[jj][:], in_=ap, func=AF.Square,
                                                  accum_out=nx.ss[:, i:i + 1]), r=[B], w=[nx.ssB, nx.junkB[jj]])
    sc.op("act", lambda: nc.scalar.activation(out=nx.sq[:, 0:nt], in_=nx.ss[:, 0:nt], func=AF.Sqrt,
                                              scale=1.0 / D, bias=k.epsc), r=[nx.ssB, k.Bcf], w=[nx.sqB])
    sc.op("dve", lambda: nc.vector.reciprocal(out=nx.rs[:, 0:nt], in_=nx.sq[:, 0:nt]), r=[nx.sqB], w=[nx.rsB])


def norm_apply(k, nx, ap, B, i, mbase, aT, aTB, col0):
    sc, nc = k.sc, k.nc
    r = nx.n % 2
    nx.n += 1
    sc.op("dve", lambda: nc.vector.tensor_scalar(out=nx.xn[r][:], in0=ap, scalar1=nx.rs[:, i:i + 1], scalar2=None,
                                                 op0=ALU.mult), r=[B, nx.rsB], w=[nx.xnB[r]])
    for j in range(8):
        sc.op("pe", lambda: nc.tensor.transpose(nx.psT[r][:, j, :], nx.xn[r][:, j * 128:(j + 1) * 128], k.ident),
              r=[nx.xnB[r], k.Bc], w=[nx.psTB[r]], signal=(j == 7))
    for j in range(8):
        sc.op("act", lambda: nc.scalar.activation(out=aT[:, j, col0:col0 + 128], in_=nx.psT[r][:, j, :],
                                                  func=AF.Identity, scale=k.modcol[:, mbase + j:mbase + j + 1],
                                                  bias=k.modcol[:, mbase + 8 + j:mbase + 9 + j]),
              r=[nx.psTB[r], k.Bmodcol], w=[aTB])


class RopeCtx:
    def __init__(self, k, st, tag):
        self.qb = [sbt(k, st, tag + "qb%d" % i, [128, 512], BF16) for i in range(2)]
        self.qbB = _bufs(tag + "qb", 2)
        self.psR = pst(k, st, tag + "psR", [128, 512])
        self.psRB = Buf(tag + "psR")
        self.css = [sbt(k, st, tag + "cs%d" % i, [128, 512], F32) for i in range(2)]
        self.sns = [sbt(k, st, tag + "sn%d" % i, [128, 512], F32) for i in range(2)]
        self.csBs = _bufs(tag + "cs", 2)
        self.cs, self.sn, self.csB = self.css[0], self.sns[0], self.csBs[0]
        self.t1 = [sbt(k, st, tag + "t1%d" % i, [128, 512], F32) for i in range(2)]
        self.t2 = [sbt(k, st, tag + "t2%d" % i, [128, 512], F32) for i in range(2)]
        self.t1B, self.t2B = _bufs(tag + "t1", 2), _bufs(tag + "t2", 2)
        self.qst = [sbt(k, st, tag + "qst%d" % i, [128, 512], BF16) for i in range(3)]
        self.qstB = _bufs(tag + "qst", 3)
        self.n = 0
        self.m = 0


def rope_load(k, rc, tok0, N, par=0):
    k.sc.dma("sp", rc.css[par][:, 0:N], k.I["cosT"][:, tok0:tok0 + N], rc.csBs[par], w=[rc.csBs[par]])
    k.sc.dma("sp", rc.sns[par][:, 0:N], k.I["sinT"][:, tok0:tok0 + N], rc.csBs[par], w=[rc.csBs[par]])


def rope_select(rc, par):
    rc.cs, rc.sn, rc.csB = rc.css[par], rc.sns[par], rc.csBs[par]


def rope_store(k, rc, psF, psFB, N, rope, dst_ap, dstB):
    sc, nc = k.sc, k.nc
    s3 = rc.m % 3
    rc.m += 1
    if not rope:
        sc.op("act", lambda: nc.scalar.copy(out=rc.qst[s3][:, 0:N], in_=psF), r=[psFB], w=[rc.qstB[s3]])

        def part2():
            sc.dma("sp", dst_ap, rc.qst[s3][:, 0:N], rc.qstB[s3], r=[rc.qstB[s3]], w=[dstB])
        return part2
    r = rc.n % 2
    rc.n += 1
    sc.op("act", lambda: nc.scalar.copy(out=rc.qb[r][:, 0:N], in_=psF), r=[psFB], w=[rc.qbB[r]])
    sc.op("dve", lambda: nc.vector.tensor_tensor(out=rc.t1[r][:, 0:N], in0=psF, in1=rc.cs[:, 0:N], op=ALU.mult),
          r=[psFB, rc.csB, rc.qbB[r]], w=[rc.t1B[r]])

    def part2():
        sc.op("pe", lambda: nc.tensor.matmul(rc.psR[:, 0:N], lhsT=k.rperm, rhs=rc.qb[r][:, 0:N], start=True, stop=True),
              r=[rc.qbB[r], k.Bc], w=[rc.psRB])
        sc.op("dve", lambda: nc.vector.tensor_tensor(out=rc.t2[r][:, 0:N], in0=rc.psR[:, 0:N], in1=rc.sn[:, 0:N],
                                                     op=ALU.mult), r=[rc.psRB, rc.csB], w=[rc.t2B[r]])
        sc.op("pool", lambda: nc.gpsimd.tensor_tensor(out=rc.qst[s3][:, 0:N], in0=rc.t1[r][:, 0:N],
                                                      in1=rc.t2[r][:, 0:N], op=ALU.add),
              r=[rc.t1B[r], rc.t2B[r]], w=[rc.qstB[s3]])
        sc.dma("sp", dst_ap, rc.qst[s3][:, 0:N], rc.qstB[s3], r=[rc.qstB[s3]], w=[dstB])
    return part2


def load_w(k, dst, src, B, pieces=1):
    v = src.rearrange("(kc p) n -> p kc n", p=128)
    kc = v.shape[1]
    step = (kc + pieces - 1) // pieces
    for a in range(0, kc, step):
        b = min(kc, a + step)
        k.sc.dma("pool", dst[:, a:b, :], v[:, a:b, :], B, w=[B])


def phase_prologue(k):
    nc, sc, es, st, I = k.nc, k.sc, k.es, k.pes, k.I
    cbf = sbt(k, es, "sb_cbf", [128, 768], BF16)
    cf32 = sbt(k, es, "sb_cf32", [128, 960], F32)
    k.Bc, k.Bcf = Buf("cbf"), Buf("cf32")
    sc.dma("pool", cbf[:], I["cbf"], k.Bc, w=[k.Bc])
    sc.dma("sp", cf32[:], I["cf32"], k.Bcf, w=[k.Bcf])
    k.ident, k.ones, k.rperm = cbf[:, 0:128], cbf[:, 128:256], cbf[:, 256:384]
    k.mprev, k.mnext = cbf[:, 384:512], cbf[:, 512:640]
    k.identF4, k.onesF, k.epsc = cf32[:, 0:512], cf32[:, 512:640], cf32[:, 640:641]
    k.sel = [cf32[0:64, 704:832], cf32[0:64, 832:960]]
    k.modcol = sbt(k, es, "modcol", [128, 128], F32)
    k.Bmodcol = Buf("modcol")

    ccol = sbt(k, st, "p_ccol", [128, 16], F32)
    scol = sbt(k, st, "p_scol", [128, 16], F32)
    ccolB, scolB = Buf("p_ccol"), Buf("p_scol")
    sc.dma("sp", ccol[:], I["c_col"], ccolB, w=[ccolB])
    sc.op("act", lambda: nc.scalar.activation(out=scol[:], in_=ccol[:], func=AF.Silu), r=[ccolB], w=[scolB])
    L = sbt(k, st, "p_L", [128, 16, 128], F32)
    LB = Buf("p_L")
    for j in range(16):
        sc.op("dve", lambda: nc.vector.tensor_scalar(out=L[:, j, :], in0=k.onesF,
                                                     scalar1=scol[:, j:j + 1], scalar2=None, op0=ALU.mult),
              r=[scolB, k.Bcf], w=[LB])
    stage = [sbt(k, st, "p_stage%d" % i, [128, 8, 512], F32) for i in range(2)]
    stageB = _bufs("p_stage", 2)
    bias = [sbt(k, st, "p_bias%d" % i, [128, 512], F32) for i in range(2)]
    biasB = _bufs("p_bias", 2)
    ngt = sbt(k, st, "p_ngt", [128, D], F32)
    ngtB = Buf("p_ngt")
    psm = [pst(k, st, "p_ps%d" % i, [128, 512]) for i in range(2)]
    psmB = _bufs("p_ps", 2)
    tA = [sbt(k, st, "p_tA%d" % i, [128, 512], F32) for i in range(2)]
    tAB = _bufs("p_tA", 2)
    tB_ = [sbt(k, st, "p_tB%d" % i, [128, 512], F32) for i in range(2)]
    tBB = _bufs("p_tB", 2)
    tC = [sbt(k, st, "p_tC%d" % i, [128, 4, 128], F32) for i in range(2)]
    tCB = _bufs("p_tC", 2)
    gbc = [sbt(k, st, "p_gbc%d" % i, [128, D], F32) for i in range(2)]
    gbcB = _bufs("p_gbc", 2)
    ngrow = {(0, 1): 0, (0, 4): 1, (1, 1): 2, (1, 4): 3}
    cnt = 0
    gcnt = 0
    for l in range(DEPTH):
        for n in range(12):
            v, half = n // 2, n % 2
            r = cnt % 2
            cnt += 1
            sc.dma("sp", stage[r][:], I["ada_w"][l].rearrange("(kc p) n -> p kc n", p=128)[:, :, n * 512:(n + 1) * 512],
                   stageB[r], w=[stageB[r]])
            sc.dma("sp", bias[r][:], I["ada_b_bc"][l][:, n * 512:(n + 1) * 512], biasB[r], w=[biasB[r]])
            if v in (1, 4) and half == 0:
                sc.dma("sp", ngt[:], I["ng_bc"][ngrow[(l, v)]], ngtB, w=[ngtB])
            for s in (0, 1):
                if s == 1 and l == 1 and v >= 2:
                    continue
                mm_group(k, psm[s][:], psmB[s], [(L[:, s * 8 + kc, :], stage[r][:, kc, :]) for kc in range(8)],
                         [LB, stageB[r]])
                if v in (2, 5):
                    sc.op("dve", lambda: nc.vector.tensor_tensor(out=gbc[s][:, half * 512:(half + 1) * 512], in0=psm[s][:],
                                                                 in1=bias[r][:], op=ALU.add),
                          r=[psmB[s], biasB[r]], w=[gbcB[s]])
                    if half == 1:
                        mi = {(0, 0, 2): 0, (0, 0, 5): 1, (0, 1, 2): 2, (0, 1, 5): 3, (1, 0, 2): 4, (1, 0, 5): 5}[(l, s, v)]
                        sc.dma("sp", k.S["modbc"][mi], gbc[s][:], gbcB[s], r=[gbcB[s]], w=[k.Bmod[mi]])
                    continue
                sc.op("dve", lambda: nc.vector.tensor_tensor(out=tA[s][:], in0=psm[s][:], in1=bias[r][:], op=ALU.add),
                      r=[psmB[s], biasB[r]], w=[tAB[s]])
                src, srcB = tA[s], tAB[s]
                if v in (1, 4):
                    sc.op("dve", lambda: nc.vector.scalar_tensor_tensor(out=tB_[s][:], in0=tA[s][:], scalar=1.0,
                                                                         in1=ngt[:, half * 512:(half + 1) * 512],
                                                                         op0=ALU.add, op1=ALU.mult),
                          r=[tAB[s], ngtB], w=[tBB[s]])
                    src, srcB = tB_[s], tBB[s]
                sc.op("pool", lambda: nc.gpsimd.tensor_tensor(out=tC[s][:].rearrange("p a b -> p (a b)"), in0=src[:],
                                                              in1=k.identF4, op=ALU.mult),
                      r=[srcB, k.Bcf], w=[tCB[s]])
                vi = {0: 1, 1: 0, 3: 3, 4: 2}[v]
                mb = modidx(l, s, vi) + half * 4
                sc.op("dve", lambda: nc.vector.tensor_reduce(out=k.modcol[:, mb:mb + 4], in_=tC[s][:], axis=AX.X,
                                                             op=ALU.add),
                      r=[tCB[s]], w=[k.Bmodcol])
    if k.dev:
        sc.dma("sp", k.S["dbg"][:, 0:112], k.modcol[:, 0:112], k.Bmodcol, r=[k.Bmodcol])
    barrier(k)


def block_tiles(tb):
    if tb < 8:
        return tb * 4, 4, 512, False
    return 32, 2, 256, True


def phase_A0(k):
    nc, sc, st, I, S = k.nc, k.sc, k.pes, k.I, k.S
    Win = sbt(k, st, "a0_Win", [128, 8, 2560], BF16)
    WinB = _bufs("a0_Win", 5)
    wv = I["even_w_in"].rearrange("(kc p) n -> p kc n", p=128)
    for g in (1, 0, 2, 3, 4):
        sc.dma("pool", Win[:, :, g * 512:(g + 1) * 512], wv[:, :, g * 512:(g + 1) * 512], WinB[g], w=[WinB[g]])
    WoA = sbt(k, st, "a0_WoA", [128, 4, D], BF16)
    WoAB = Buf("a0_WoA")
    load_w(k, WoA, I["even_w_out"][0:512, :], WoAB)
    wsT = sbt(k, st, "a0_wsT", [128, 512], BF16)
    wsTB = Buf("a0_wsT")
    sc.dma("pool", wsT[:], I["sgu_wT"], wsTB, w=[wsTB])
    bsbc = sbt(k, st, "a0_bsbc", [128, 512], F32)
    bsB = Buf("a0_bsbc")
    sc.dma("sp", bsbc[:], I["sgu_b_bc"], bsB, w=[bsB])
    gbc = [sbt(k, st, "a0_gbc%d" % i, [128, D], F32) for i in range(2)]
    gbcB = _bufs("a0_gbc", 2)
    sc.dma("sp", gbc[0][:], S["modbc"][0], gbcB[0], r=[k.Bmod[0]], w=[gbcB[0]])
    sc.dma("sp", gbc[1][:], S["modbc"][2], gbcB[1], r=[k.Bmod[2]], w=[gbcB[1]])

    xt = [sbt(k, st, "a0_xt%d" % i, [128, D], F32) for i in range(8)]
    xtB = _bufs("a0_xt", 8)
    nx = NormCtx(k, st, "a0_")
    rc = RopeCtx(k, st, "a0_")
    aT = sbt(k, st, "a0_aT", [128, 8, 512], BF16)
    aTB = Buf("a0_aT")
    guT = sbt(k, st, "a0_guT", [128, 4, 512], BF16)
    guB = Buf("a0_guT")
    vg = [sbt(k, st, "a0_vg%d" % i, [128, 4, 128], F32) for i in range(4)]
    vgB = _bufs("a0_vg", 4)
    lsum = [sbt(k, st, "a0_lsum%d" % i, [128, 4], F32) for i in range(4)]
    lnm = [sbt(k, st, "a0_lnm%d" % i, [128, 4], F32) for i in range(4)]
    lsq = [sbt(k, st, "a0_lsq%d" % i, [128, 4], F32) for i in range(4)]
    lsr = [sbt(k, st, "a0_lsr%d" % i, [128, 4], F32) for i in range(4)]
    lrs = [sbt(k, st, "a0_lrs%d" % i, [128, 4], F32) for i in range(4)]
    lsumB, lnmB, lsqB, lsrB, lrsB = (_bufs("a0_l%s" % n, 4) for n in "abcde")
    vf = [sbt(k, st, "a0_vf%d" % i, [128, 512], BF16) for i in range(4)]
    vfB = _bufs("a0_vf", 4)
    mx = [sbt(k, st, "a0_mx%d" % i, [128, 4, 128], F32) for i in range(2)]
    mxB = _bufs("a0_mx", 2)
    AoT = sbt(k, st, "a0_AoT", [128, 4, 512], BF16)
    AoB = _bufs("a0_AoT", 4)
    yt = [sbt(k, st, "a0_yt%d" % i, [128, 512], F32) for i in range(2)]
    ytB = _bufs("a0_yt", 2)
    vst = [sbt(k, st, "a0_vst%d" % i, [128, 512], BF16) for i in range(2)]
    vstB = _bufs("a0_vst", 2)
    psS = pst(k, st, "a0_psS", [128, 4, 128])
    psSB = Buf("a0_psS")
    psK = [pst(k, st, "a0_psK%d" % i, [128, 512]) for i in range(2)]
    psKB = _bufs("a0_psK", 2)
    psF = [pst(k, st, "a0_psF%d" % i, [128, 512]) for i in range(2)]
    psFB = _bufs("a0_psF", 2)
    cnt = {"K": 0, "F": 0}
    NB = int(os.environ.get("A0_BLOCKS", "9"))

    def xs(tb, i):
        return (tb % 2) * 4 + i

    def stage_norm(tb):
        t0, ntl, N, isctx = block_tiles(tb)
        s = 1 if isctx else 0
        for i in range(ntl):
            j = xs(tb, i)
            sc.dma("sp", xt[j][:], I["x"][(t0 + i) * 128:(t0 + i + 1) * 128, :], xtB[j], w=[xtB[j]])
        if not isctx:
            rope_load(k, rc, t0 * 128, N, tb % 2)
        norm_stats(k, nx, [(xt[xs(tb, i)][:], xtB[xs(tb, i)]) for i in range(ntl)])
        for i in range(ntl):
            j = xs(tb, i)
            norm_apply(k, nx, xt[j][:], xtB[j], i, modidx(0, s, 0), aT, aTB, i * 128)

    def stage_proj(tb):
        t0, ntl, N, isctx = block_tiles(tb)
        tok0 = t0 * 128
        rope_select(rc, tb % 2)
        for i in range(ntl):
            kk = cnt["K"] % 2
            cnt["K"] += 1
            r = i
            mm_group(k, psK[kk][:], psKB[kk], [(aT[:, kc, i * 128:(i + 1) * 128], Win[:, kc, 512:1024]) for kc in range(8)],
                     [WinB[1], aTB])
            sc.op("act", lambda: nc.scalar.activation(out=vg[r][:].rearrange("p a b -> p (a b)"), in_=psK[kk][:], func=AF.Gelu),
                  r=[psKB[kk]], w=[vgB[r]])
            sc.op("dve", lambda: nc.vector.tensor_reduce(out=lsum[r][:], in_=vg[r][:], axis=AX.X, op=ALU.add),
                  r=[vgB[r]], w=[lsumB[r]])
            sc.op("dve", lambda: nc.vector.tensor_scalar(out=lnm[r][:], in0=lsum[r][:], scalar1=-1.0 / 128, scalar2=None,
                                                         op0=ALU.mult), r=[lsumB[r]], w=[lnmB[r]])
        for i in range(ntl):
            r = i
            for g in range(4):
                jj = nx.nj % 2
                nx.nj += 1
                sc.op("act", lambda: nc.scalar.activation(out=nx.junks[jj][:, 0:128], in_=vg[r][:, g, :], func=AF.Square,
                                                          bias=lnm[r][:, g:g + 1], accum_out=lsq[r][:, g:g + 1]),
                      r=[vgB[r], lnmB[r]], w=[lsqB[r], nx.junkB[jj]])
        for g in range(4):
            f = cnt["F"] % 2
            cnt["F"] += 1
            mm_group(k, psF[f][:, 0:N], psFB[f], [(Win[:, kc, g * 128:(g + 1) * 128], aT[:, kc, 0:N]) for kc in range(8)],
                     [WinB[0], aTB])
            sc.op("act", lambda: nc.scalar.activation(out=guT[:, g, 0:N], in_=psF[f][:, 0:N], func=AF.Gelu),
                  r=[psFB[f]], w=[guB])
        for i in range(ntl):
            r = i
            sc.op("act", lambda: nc.scalar.activation(out=lsr[r][:], in_=lsq[r][:], func=AF.Sqrt, scale=1.0 / 128,
                                                      bias=k.epsc), r=[lsqB[r], k.Bcf], w=[lsrB[r]])
            sc.op("dve", lambda: nc.vector.reciprocal(out=lrs[r][:], in_=lsr[r][:]), r=[lsrB[r]], w=[lrsB[r]])
            for g in range(4):
                sc.op("dve", lambda: nc.vector.tensor_scalar(out=vf[r][:, g * 128:(g + 1) * 128], in0=vg[r][:, g, :],
                                                             scalar1=lnm[r][:, g:g + 1], scalar2=lrs[r][:, g:g + 1],
                                                             op0=ALU.add, op1=ALU.mult),
                      r=[vgB[r], lnmB[r], lrsB[r]], w=[vfB[r]])
        pend = None
        for c in range(8):
            f = cnt["F"] % 2
            cnt["F"] += 1
            mm_group(k, psF[f][:, 0:N], psFB[f],
                     [(Win[:, kc, 1024 + c * 128:1024 + (c + 1) * 128], aT[:, kc, 0:N]) for kc in range(8)],
                     [WinB[2 + c // 4], aTB])
            if pend is not None:
                pend()
            if c < 4:
                pend = rope_store(k, rc, psF[f][:, 0:N], psFB[f], N, not isctx, S["q"][c][:, tok0:tok0 + N], k.Bq[c][tb])
            else:
                pend = rope_store(k, rc, psF[f][:, 0:N], psFB[f], N, not isctx, S["k"][c - 4][:, tok0:tok0 + N], k.Bk[c - 4][tb])
        for i in range(ntl):
            kk = cnt["K"] % 2
            cnt["K"] += 1
            mm_group(k, psK[kk][:], psKB[kk], [(aT[:, kc, i * 128:(i + 1) * 128], Win[:, kc, 2048:2560]) for kc in range(8)],
                     [WinB[4], aTB])
            if pend is not None:
                pend()
                pend = None
            v2 = (t0 + i) % 2
            sc.op("dve", lambda: nc.vector.tensor_copy(out=vst[v2][:], in_=psK[kk][:]), r=[psKB[kk]], w=[vstB[v2]])
            sc.dma("sp", S["v"][(t0 + i) * 128:(t0 + i + 1) * 128, :], vst[v2][:], vstB[v2], r=[vstB[v2]], w=[k.Bv[t0 + i]])

    def stage_tail(tb):
        t0, ntl, N, isctx = block_tiles(tb)
        s = 1 if isctx else 0
        for i in range(ntl):
            r = i
            m2 = i % 2
            j = xs(tb, i)
            for g in range(4):
                sc.op("pe", lambda: nc.tensor.matmul(psS[:, g, :], lhsT=vf[r][:, g * 128:(g + 1) * 128],
                                                     rhs=wsT[:, g * 128:(g + 1) * 128], start=True, stop=True),
                      r=[vfB[r], wsTB], w=[psSB], signal=(g == 3))
            sc.op("dve", lambda: nc.vector.tensor_tensor(out=mx[m2][:].rearrange("p a b -> p (a b)"),
                                                         in0=psS[:].rearrange("p a b -> p (a b)"), in1=bsbc[:], op=ALU.add),
                  r=[psSB, bsB], w=[mxB[m2]])
            sc.op("pool", lambda: nc.gpsimd.tensor_tensor(out=AoT[:, :, i * 128:(i + 1) * 128], in0=mx[m2][:],
                                                          in1=guT[:, :, i * 128:(i + 1) * 128], op=ALU.mult),
                  r=[mxB[m2], guB], w=[AoB[i]])
        for i in range(ntl):
            j = xs(tb, i)
            for nh in range(2):
                kk = cnt["K"] % 2
                cnt["K"] += 1
                mm_group(k, psK[kk][:], psKB[kk],
                         [(AoT[:, g, i * 128:(i + 1) * 128], WoA[:, g, nh * 512:(nh + 1) * 512]) for g in range(4)],
                         [AoB[i], WoAB])
                sc.op("dve", lambda: nc.vector.tensor_tensor(out=yt[nh][:], in0=psK[kk][:],
                                                             in1=gbc[s][:, nh * 512:(nh + 1) * 512], op=ALU.mult),
                      r=[psKB[kk], gbcB[s]], w=[ytB[nh]])
                sc.op("pool", lambda: nc.gpsimd.tensor_tensor(out=xt[j][:, nh * 512:(nh + 1) * 512], in0=yt[nh][:],
                                                              in1=xt[j][:, nh * 512:(nh + 1) * 512], op=ALU.add),
                      r=[ytB[nh], xtB[j]], w=[xtB[j]])
            sc.dma("sp", S["h"][(t0 + i) * 128:(t0 + i + 1) * 128, :], xt[j][:], xtB[j], r=[xtB[j]], w=[k.Bh[t0 + i]])

    stage_norm(0)
    for tb in range(NB):
        stage_proj(tb)
        if tb + 1 < NB:
            stage_norm(tb + 1)
        stage_tail(tb)
    barrier(k)


def phase_B0(k):
    nc, sc, st, I, S = k.nc, k.sc, k.pes, k.I, k.S
    kT = sbt(k, st, "b0_kT", [128, 4, NTOK], BF16)
    kTB = _bufs("b0_kT", 4)
    for h in range(4):
        sc.dma("sp", kT[:, h, :], S["k"][h], kTB[h], r=[k.Bk[h][b] for b in range(9)], w=[kTB[h]])
    vA = sbt(k, st, "b0_vA", [128, NT, 512], BF16)
    vAB = Buf("b0_vA")
    vv = S["v"].rearrange("(t p) e -> p t e", p=128)
    for a, b in ((0, 17), (17, 34)):
        sc.dma("sp", vA[:, a:b, :], vv[:, a:b, :], vAB, r=[k.Bv[t] for t in range(a, b)], w=[vAB])
    lq = sbt(k, st, "b0_lq", [128, 256], F32)
    lqB = Buf("b0_lq")
    sc.dma("sp", lq[:], I["lqk_bc"], lqB, w=[lqB])
    sg = sbt(k, st, "b0_sg", [128, 1], F32)
    sgB = Buf("b0_sg")
    sc.dma("sp", sg[:], I["subln_col"], sgB, w=[sgB])
    prod = sbt(k, st, "b0_prod", [128, 2, 64], F32)
    s12 = sbt(k, st, "b0_s12", [128, 2], F32)
    e12 = sbt(k, st, "b0_e12", [128, 2], F32)
    dl = sbt(k, st, "b0_dl", [128, 1], F32)
    nl = sbt(k, st, "b0_nl", [128, 1], F32)
    gl = sbt(k, st, "b0_gl", [128, 1], F32)
    prodB, s12B, e12B, dlB, nlB, glB = (Buf("b0_" + n) for n in ("prod", "s12", "e12", "dl", "nl", "gl"))
    for j in range(2):
        sc.op("dve", lambda: nc.vector.tensor_tensor(out=prod[:, j, :], in0=lq[:, j * 128:j * 128 + 64],
                                                     in1=lq[:, j * 128 + 64:j * 128 + 128], op=ALU.mult),
              r=[lqB], w=[prodB])
    sc.op("dve", lambda: nc.vector.tensor_reduce(out=s12[:], in_=prod[:], axis=AX.X, op=ALU.add), r=[prodB], w=[s12B])
    sc.op("act", lambda: nc.scalar.activation(out=e12[:], in_=s12[:], func=AF.Exp), r=[s12B], w=[e12B])
    sc.op("dve", lambda: nc.vector.tensor_tensor(out=dl[:], in0=e12[:, 1:2], in1=e12[:, 0:1], op=ALU.subtract),
          r=[e12B], w=[dlB])
    sc.op("dve", lambda: nc.vector.tensor_scalar(out=nl[:], in0=dl[:], scalar1=-LAM_INIT0, scalar2=None, op0=ALU.add),
          r=[dlB], w=[nlB])
    sc.op("dve", lambda: nc.vector.tensor_scalar(out=gl[:], in0=sg[:], scalar1=1.0 - LAM_INIT0, scalar2=None, op0=ALU.mult),
          r=[sgB], w=[glB])

    psS = [pst(k, st, "b0_psS%d" % i, [128, 2, 512]) for i in range(2)]
    psSB = _bufs("b0_psS", 2)
    psO = [pst(k, st, "b0_psO%d" % i, [128, 512]) for i in range(2)]
    psOB = _bufs("b0_psO", 2)
    psL = [pst(k, st, "b0_psL%d" % i, [128, 512]) for i in range(2)]
    psLB = _bufs("b0_psL", 2)
    E = [sbt(k, st, "b0_E%d" % i, [128, 2, 512], BF16) for i in range(3)]
    EB = _bufs("b0_E", 3)
    qblk = [sbt(k, st, "b0_q%d" % i, [128, 512], BF16) for i in range(2)]
    qblkB = _bufs("b0_q", 2)
    Lsb = sbt(k, st, "b0_Lsb", [128, 512], F32)
    LsbB = Buf("b0_Lsb")
    R = [sbt(k, st, "b0_R%d" % i, [128, 512], F32) for i in range(2)]
    RB = _bufs("b0_R", 2)
    T = [sbt(k, st, "b0_T%d" % i, [128, 512], F32) for i in range(2)]
    TB = _bufs("b0_T", 2)
    ost2 = [sbt(k, st, "b0_ost%d" % i, [128, NTOK], F32) for i in range(2)]
    ostB2 = [_bufs("b0_ost%d_" % i, 9) for i in range(2)]
    sqs2 = [sbt(k, st, "b0_sqs%d" % i, [128, NTOK], BF16) for i in range(2)]
    sqsB2 = [_bufs("b0_sqs%d_" % i, 9) for i in range(2)]
    pend_p2 = []
    SD = [sbt(k, st, "b0_SD%d" % i, [128, 512], F32) for i in range(2)]
    SDB = _bufs("b0_SD", 2)
    RS = [sbt(k, st, "b0_RS%d" % i, [128, 512], F32) for i in range(2)]
    RSB = _bufs("b0_RS", 2)
    osg = [sbt(k, st, "b0_osg%d" % i, [128, 512], BF16) for i in range(2)]
    osgB = _bufs("b0_osg", 2)
    nH = int(os.environ.get("B0_HEADS", "4"))
    nQ = int(os.environ.get("B0_QB", "9"))
    its = [(h, qb) for h in range(nH) for qb in range(9 - nQ, 9)]
    cS = cE = 0
    pend_fin = []

    def qinfo(qb):
        return (512, list(range(NT))) if qb < 8 else (256, [32, 33])

    def load_q(n):
        h, qb = its[n]
        N, _ = qinfo(qb)
        sc.dma("sp", qblk[n % 2][:, 0:N], S["q"][h][:, qb * 512:qb * 512 + N], qblkB[n % 2], r=[k.Bq[h][qb]],
               w=[qblkB[n % 2]])

    load_q(0)
    for n, (h, qb) in enumerate(its):
        N, tiles = qinfo(qb)
        q0 = qb * 512
        qq, qqB = qblk[n % 2], qblkB[n % 2]
        ost, ostB, sqs, sqsB = ost2[h % 2], ostB2[h % 2], sqs2[h % 2], sqsB2[h % 2]
        if n + 1 < len(its):
            load_q(n + 1)
        nt = len(tiles)
        sl = {}

        def emit_S(i):
            nonlocal cS
            s = cS % 2
            cS += 1
            sl[i] = s
            kt = tiles[i]
            for j in range(2):
                sc.op("pe", lambda: nc.tensor.matmul(psS[s][:, j, 0:N], lhsT=kT[j * 64:(j + 1) * 64, h, kt * 128:(kt + 1) * 128],
                                                     rhs=qq[j * 64:(j + 1) * 64, 0:N], start=True, stop=True),
                      r=[kTB[h], qqB], w=[psSB[s]], signal=(j == 1))

        emit_S(0)
        for i in range(nt):
            if i + 1 < nt:
                emit_S(i + 1)
            if pend_fin and (i == 3 or i == nt - 1):
                pend_fin.pop()()
            if pend_p2 and qb < 8 and (i == 12 or (i == 24 and len(pend_p2) > 8 - qb)):
                pend_p2.pop(0)()
            s = sl[i]
            e = cE % 3
            cE += 1
            kt = tiles[i]
            sc.op("act", lambda: nc.scalar.activation(out=E[e][:, :, 0:N], in_=psS[s][:, :, 0:N], func=AF.Exp, scale=SCALE),
                  r=[psSB[s]], w=[EB[e]])
            for j in range(2):
                sc.op("pe", lambda: nc.tensor.matmul(psO[j][:, 0:N], lhsT=vA[:, kt, h * 128:(h + 1) * 128], rhs=E[e][:, j, 0:N],
                                                     start=(i == 0), stop=(i == nt - 1)),
                      r=[vAB, EB[e]], w=[psOB[j]], signal=False)
            for j in range(2):
                sc.op("pe", lambda: nc.tensor.matmul(psL[0][j * 32:(j + 1) * 32, 0:N], lhsT=k.ones[:, 0:32], rhs=E[e][:, j, 0:N],
                                                     start=(i == 0), stop=(i == nt - 1), skip_group_check=True),
                      r=[k.Bc, EB[e]], w=[psLB[0]], signal=(j == 1))
        for j in range(2):
            sc.op("dve", lambda: nc.vector.tensor_copy(out=T[j][:, 0:N], in_=psO[j][:, 0:N]), r=[psOB[j]], w=[TB[j]])
        sc.op("act", lambda: nc.scalar.copy(out=Lsb[0:64, 0:N], in_=psL[0][0:64, 0:N]), r=[psLB[0]], w=[LsbB])

        def fin(N=N, q0=q0, qb=qb, ost=ost, ostB=ostB, sqs=sqs, sqsB=sqsB):
            sc.op("dve", lambda: nc.vector.reciprocal(out=R[0][0:64, 0:N], in_=Lsb[0:64, 0:N]), r=[LsbB], w=[RB[0]])
            for j in (1, 0):
                sc.op("pe", lambda: nc.tensor.matmul(psL[1][:, 0:N], lhsT=k.sel[j], rhs=R[0][0:64, 0:N], start=True, stop=True),
                      r=[k.Bcf, RB[0]], w=[psLB[1]])
                sc.op("dve", lambda: nc.vector.tensor_tensor(out=T[j][:, 0:N], in0=T[j][:, 0:N], in1=psL[1][:, 0:N], op=ALU.mult),
                      r=[psLB[1], TB[j]], w=[TB[j]])
            sc.op("dve", lambda: nc.vector.scalar_tensor_tensor(out=ost[:, q0:q0 + N], in0=T[1][:, 0:N], scalar=nl[:, 0:1],
                                                                in1=T[0][:, 0:N], op0=ALU.mult, op1=ALU.add),
                  r=[TB[0], TB[1], nlB], w=[ostB[qb]])
            sc.op("pool", lambda: nc.gpsimd.tensor_tensor(out=sqs[:, q0:q0 + N], in0=ost[:, q0:q0 + N], in1=ost[:, q0:q0 + N],
                                                          op=ALU.mult), r=[ostB[qb]], w=[sqsB[qb]])
        pend_fin.append(fin)
        if qb == 8 or n + 1 == len(its):
            pend_fin.pop()()
        if qb == 8:
            for q2 in range(9 - nQ, 9):
                def unit(q2=q2, h=h, sqs=sqs, sqsB=sqsB):
                    N2, _ = qinfo(q2)
                    p0 = q2 * 512
                    x2 = q2 % 2
                    sc.op("pe", lambda: nc.tensor.matmul(psL[1][:, 0:N2], lhsT=k.ones, rhs=sqs[:, p0:p0 + N2], start=True, stop=True),
                          r=[k.Bc, sqsB[q2]], w=[psLB[1]])
                    sc.op("act", lambda: nc.scalar.activation(out=SD[x2][:, 0:N2], in_=psL[1][:, 0:N2], func=AF.Ln,
                                                              scale=1.0 / 128, bias=k.epsc), r=[psLB[1], k.Bcf], w=[SDB[x2]])
                    sc.op("act", lambda: nc.scalar.activation(out=RS[x2][:, 0:N2], in_=SD[x2][:, 0:N2], func=AF.Exp, scale=-0.5),
                          r=[SDB[x2]], w=[RSB[x2]])
                    sc.op("dve", lambda: nc.vector.scalar_tensor_tensor(out=osg[x2][:, 0:N2], in0=ost2[h % 2][:, p0:p0 + N2],
                                                                        scalar=gl[:, 0:1], in1=RS[x2][:, 0:N2],
                                                                        op0=ALU.mult, op1=ALU.mult),
                          r=[ostB2[h % 2][q2], glB, RSB[x2]], w=[osgB[x2]])
                    sc.dma("sp", S["o"][h][:, p0:p0 + N2], osg[x2][:, 0:N2], osgB[x2], r=[osgB[x2]], w=[k.Bo[h][q2]])
                pend_p2.append(unit)
    while pend_p2:
        pend_p2.pop(0)()
    barrier(k)


def phase_C1(k, l):
    nc, sc, st, I, S = k.nc, k.sc, k.pes, k.I, k.S
    tg = "c1%d_" % l
    nk = 4 if l == 0 else 8
    nblk = 9 if l == 0 else 8
    Wo = sbt(k, st, tg + "Wo", [128, nk, D], BF16)
    WoB = Buf(tg + "Wo")
    load_w(k, Wo, I["even_w_out"][512:1024, :] if l == 0 else I["odd_w_out"], WoB)
    Wfi = sbt(k, st, tg + "Wfi", [128, 8, 2 * FFN], BF16)
    WfiB = _bufs(tg + "Wfi", 11)
    wv = I["ffn_w_in"][l].rearrange("(kc p) n -> p kc n", p=128)
    for g in range(11):
        sc.dma("pool", Wfi[:, :, g * 512:(g + 1) * 512], wv[:, :, g * 512:(g + 1) * 512], WfiB[g], w=[WfiB[g]])
    gbc = [sbt(k, st, tg + "gbc%d" % i, [128, D], F32) for i in range(2)]
    gbcB = _bufs(tg + "gbc", 2)
    mi = 0 if l == 0 else 4
    sc.dma("sp", gbc[0][:], S["modbc"][mi], gbcB[0], r=[k.Bmod[mi]], w=[gbcB[0]])
    if l == 0:
        sc.dma("sp", gbc[1][:], S["modbc"][2], gbcB[1], r=[k.Bmod[2]], w=[gbcB[1]])
    ht = [sbt(k, st, tg + "ht%d" % i, [128, D], F32) for i in range(4)]
    htB = _bufs(tg + "ht", 4)
    ob = sbt(k, st, tg + "ob", [128, nk, 512], BF16)
    obB = Buf(tg + "ob")
    nx = NormCtx(k, st, tg)
    aT2 = [sbt(k, st, tg + "aT%d" % i, [128, 8, 512], BF16) for i in range(2)]
    aT2B = _bufs(tg + "aT", 2)
    yt = [sbt(k, st, tg + "yt%d" % i, [128, 512], F32) for i in range(2)]
    ytB = _bufs(tg + "yt", 2)
    sg = [sbt(k, st, tg + "sg%d" % i, [128, 512], F32) for i in range(2)]
    sgB = _bufs(tg + "sg", 2)
    hst = [sbt(k, st, tg + "hst%d" % i, [128, 512], BF16) for i in range(3)]
    hstB = _bufs(tg + "hst", 3)
    psK = [pst(k, st, tg + "psK%d" % i, [128, 512]) for i in range(2)]
    psKB = _bufs(tg + "psK", 2)
    psG = [pst(k, st, tg + "psG%d" % i, [128, 512]) for i in range(2)]
    psGB = _bufs(tg + "psG", 2)
    psU = [pst(k, st, tg + "psU%d" % i, [128, 512]) for i in range(2)]
    psUB = _bufs(tg + "psU", 2)
    ov = S["o"][0:nk].rearrange("c p n -> p c n")
    cnt = {"K": 0, "G": 0, "H": 0}

    def preA(tb):
        t0, ntl, N, isctx = block_tiles(tb)
        tok0 = t0 * 128
        s = 1 if isctx else 0
        for i in range(ntl):
            sc.dma("sp", ht[i][:], S["h"][(t0 + i) * 128:(t0 + i + 1) * 128, :], htB[i], r=[k.Bh[t0 + i]], w=[htB[i]])
        sc.dma("sp", ob[:, :, 0:N], ov[:, :, tok0:tok0 + N], obB, r=[k.Bo[c][tb] for c in range(nk)], w=[obB])
        for i in range(ntl):
            for c in range(nk):
                for nh in range(2):
                    sc.op("pe", lambda: nc.tensor.matmul(psK[nh][:], lhsT=ob[:, c, i * 128:(i + 1) * 128],
                                                         rhs=Wo[:, c, nh * 512:(nh + 1) * 512], start=(c == 0), stop=(c == nk - 1)),
                          r=[obB, WoB], w=[psKB[nh]], signal=(c == nk - 1))
            for nh in range(2):
                kk = nh
                sc.op("dve", lambda: nc.vector.tensor_tensor(out=yt[nh][:], in0=psK[kk][:], in1=gbc[s][:, nh * 512:(nh + 1) * 512],
                                                             op=ALU.mult), r=[psKB[kk], gbcB[s]], w=[ytB[nh]])
                sc.op("pool", lambda: nc.gpsimd.tensor_tensor(out=ht[i][:, nh * 512:(nh + 1) * 512], in0=yt[nh][:],
                                                              in1=ht[i][:, nh * 512:(nh + 1) * 512], op=ALU.add),
                      r=[ytB[nh], htB[i]], w=[htB[i]])
            sc.dma("sp", S["h"][(t0 + i) * 128:(t0 + i + 1) * 128, :], ht[i][:], htB[i], r=[htB[i]], w=[k.Bh[t0 + i]])
        norm_stats(k, nx, [(ht[i][:], htB[i]) for i in range(ntl)])

    def preB(tb):
        t0, ntl, N, isctx = block_tiles(tb)
        s = 1 if isctx else 0
        for i in range(ntl):
            norm_apply(k, nx, ht[i][:], htB[i], i, modidx(l, s, 2), aT2[tb % 2], aT2B[tb % 2], i * 128)

    preA(0)
    preB(0)
    for tb in range(nblk):
        t0, ntl, N, isctx = block_tiles(tb)
        tok0 = t0 * 128
        aT, aTB = aT2[tb % 2], aT2B[tb % 2]
        for f in range(NF):
            if tb + 1 < nblk and f == 4:
                preA(tb + 1)
            if tb + 1 < nblk and f == 13:
                preB(tb + 1)
            g2 = cnt["G"] % 2
            cnt["G"] += 1
            h3 = cnt["H"] % 3
            cnt["H"] += 1
            c0, c1 = f * 128, FFN + f * 128
            mm_group(k, psG[g2][:, 0:N], psGB[g2], [(Wfi[:, kc, c0:c0 + 128], aT[:, kc, 0:N]) for kc in range(8)],
                     [WfiB[c0 // 512], WfiB[(c0 + 127) // 512], aTB])
            mm_group(k, psU[g2][:, 0:N], psUB[g2], [(Wfi[:, kc, c1:c1 + 128], aT[:, kc, 0:N]) for kc in range(8)],
                     [WfiB[c1 // 512], WfiB[(c1 + 127) // 512], aTB])
            sc.op("act", lambda: nc.scalar.activation(out=sg[g2][:, 0:N], in_=psG[g2][:, 0:N], func=AF.Silu),
                  r=[psGB[g2]], w=[sgB[g2]])
            sc.op("dve", lambda: nc.vector.tensor_tensor(out=hst[h3][:, 0:N], in0=psU[g2][:, 0:N], in1=sg[g2][:, 0:N], op=ALU.mult),
                  r=[psUB[g2], sgB[g2]], w=[hstB[h3]])
            sc.dma("sp", S["hm"][f][:, tok0:tok0 + N], hst[h3][:, 0:N], hstB[h3], r=[hstB[h3]], w=[k.Bhm[f][tb]])
    barrier(k)


def phase_C2(k, l):
    nc, sc, st, I, S = k.nc, k.sc, k.pes, k.I, k.S
    tg = "c2%d_" % l
    nblk = 9 if l == 0 else 8
    last = l == DEPTH - 1
    Wfo = sbt(k, st, tg + "Wfo", [128, NF, D], BF16)
    WfoB = Buf(tg + "Wfo")
    load_w(k, Wfo, I["ffn_w_out"][l], WfoB, pieces=2)
    gbc = [sbt(k, st, tg + "gbc%d" % i, [128, D], F32) for i in range(2)]
    gbcB = _bufs(tg + "gbc", 2)
    mi = 1 if l == 0 else 5
    sc.dma("sp", gbc[0][:], S["modbc"][mi], gbcB[0], r=[k.Bmod[mi]], w=[gbcB[0]])
    if l == 0:
        sc.dma("sp", gbc[1][:], S["modbc"][3], gbcB[1], r=[k.Bmod[3]], w=[gbcB[1]])
    if last:
        fg = sbt(k, st, tg + "fg", [128, D], F32)
        fgB = Buf(tg + "fg")
        sc.dma("sp", fg[:], I["ng_bc"][4], fgB, w=[fgB])
        ot = [sbt(k, st, tg + "ot%d" % i, [128, D], F32) for i in range(2)]
        otB = _bufs(tg + "ot", 2)
    ht = [sbt(k, st, tg + "ht%d" % i, [128, D], F32) for i in range(4)]
    htB = _bufs(tg + "ht", 4)
    hmb = [sbt(k, st, tg + "hmb%d" % i, [128, NF, 512], BF16) for i in range(2)]
    hmbB = _bufs(tg + "hmb", 2)
    nx = NormCtx(k, st, tg)
    yt = [sbt(k, st, tg + "yt%d" % i, [128, 512], F32) for i in range(2)]
    ytB = _bufs(tg + "yt", 2)
    psK = [pst(k, st, tg + "psK%d" % i, [128, 512]) for i in range(4)]
    psKB = _bufs(tg + "psK", 4)
    hv = S["hm"].rearrange("f p n -> p f n")
    nK = nO = 0

    def loads(tb):
        t0, ntl, N, isctx = block_tiles(tb)
        hb = tb % 2
        sc.dma("sp", hmb[hb][:, :, 0:N], hv[:, :, t0 * 128:t0 * 128 + N], hmbB[hb], r=[k.Bhm[f][tb] for f in range(NF)],
               w=[hmbB[hb]])

    loads(0)
    for tb in range(nblk):
        t0, ntl, N, isctx = block_tiles(tb)
        s = 1 if isctx else 0
        hb = tb % 2
        for i in range(ntl):
            sc.dma("sp", ht[i][:], S["h"][(t0 + i) * 128:(t0 + i + 1) * 128, :], htB[i], r=[k.Bh[t0 + i]], w=[htB[i]])
        if tb + 1 < nblk:
            loads(tb + 1)
        for i in range(ntl):
            kks = [(nK + nh) % 4 for nh in range(2)]
            nK += 2
            for f in range(NF):
                for nh in range(2):
                    sc.op("pe", lambda: nc.tensor.matmul(psK[kks[nh]][:], lhsT=hmb[hb][:, f, i * 128:(i + 1) * 128],
                                                         rhs=Wfo[:, f, nh * 512:(nh + 1) * 512], start=(f == 0), stop=(f == NF - 1)),
                          r=[hmbB[hb], WfoB], w=[psKB[kks[nh]]], signal=(f == NF - 1))
            for nh in range(2):
                kk = kks[nh]
                sc.op("dve", lambda: nc.vector.tensor_tensor(out=yt[nh][:], in0=psK[kk][:], in1=gbc[s][:, nh * 512:(nh + 1) * 512],
                                                             op=ALU.mult), r=[psKB[kk], gbcB[s]], w=[ytB[nh]])
                sc.op("pool", lambda: nc.gpsimd.tensor_tensor(out=ht[i][:, nh * 512:(nh + 1) * 512], in0=yt[nh][:],
                                                              in1=ht[i][:, nh * 512:(nh + 1) * 512], op=ALU.add),
                      r=[ytB[nh], htB[i]], w=[htB[i]])
            if not last:
                sc.dma("sp", S["h"][(t0 + i) * 128:(t0 + i + 1) * 128, :], ht[i][:], htB[i], r=[htB[i]], w=[k.Bh[t0 + i]])
        if last:
            norm_stats(k, nx, [(ht[i][:], htB[i]) for i in range(ntl)])
            for i in range(ntl):
                o2 = nO % 2
                nO += 1
                sc.op("dve", lambda: nc.vector.scalar_tensor_tensor(out=ot[o2][:], in0=ht[i][:], scalar=nx.rs[:, i:i + 1],
                                                                    in1=fg[:], op0=ALU.mult, op1=ALU.mult),
                      r=[htB[i], nx.rsB, fgB], w=[otB[o2]])
                sc.dma("sp", k.out[(t0 + i) * 128:(t0 + i + 1) * 128, :], ot[o2][:], otB[o2], r=[otB[o2]])
    barrier(k)


def phase_C1_l0(k):
    phase_C1(k, 0)


def phase_C2_l0(k):
    phase_C2(k, 0)


def phase_C1_l1(k):
    phase_C1(k, 1)


def phase_C2_l1(k):
    phase_C2(k, 1)


def phase_A1(k):
    nc, sc, st, I, S = k.nc, k.sc, k.pes, k.I, k.S
    Wq = sbt(k, st, "a1_Wq", [128, 8, 1792], BF16)
    WqB = _bufs("a1_Wq", 4)
    wv = I["odd_w_qkv"].rearrange("(kc p) n -> p kc n", p=128)
    for g in range(2):
        sc.dma("pool", Wq[:, :, g * 512:(g + 1) * 512], wv[:, :, g * 512:(g + 1) * 512], WqB[g], w=[WqB[g]])
    for j in range(4):
        for e in range(2):
            sc.dma("pool", Wq[:, :, 1024 + j * 128 + e * 64:1024 + j * 128 + (e + 1) * 64],
                   wv[:, :, 1024 + j * 64:1024 + (j + 1) * 64], WqB[2], w=[WqB[2]])
    sc.dma("pool", Wq[:, :, 1536:1792], wv[:, :, 1280:1536], WqB[3], w=[WqB[3]])
    ht = [sbt(k, st, "a1_ht%d" % i, [128, D], F32) for i in range(4)]
    htB = _bufs("a1_ht", 4)
    nx = NormCtx(k, st, "a1_")
    rc = RopeCtx(k, st, "a1_")
    aT = sbt(k, st, "a1_aT", [128, 8, 512], BF16)
    aTB = Buf("a1_aT")
    vst = [sbt(k, st, "a1_vst%d" % i, [128, 256], BF16) for i in range(2)]
    vstB = _bufs("a1_vst", 2)
    psK = [pst(k, st, "a1_psK%d" % i, [128, 512]) for i in range(2)]
    psKB = _bufs("a1_psK", 2)
    psF = [pst(k, st, "a1_psF%d" % i, [128, 512]) for i in range(2)]
    psFB = _bufs("a1_psF", 2)
    cnt = {"K": 0, "F": 0}
    aT2 = [aT, sbt(k, st, "a1_aTb", [128, 8, 512], BF16)]
    aT2B = [aTB, Buf("a1_aTb")]

    def stage_norm(tb):
        t0, ntl, N, isctx = block_tiles(tb)
        s = 1 if isctx else 0
        for i in range(ntl):
            sc.dma("sp", ht[i][:], S["h"][(t0 + i) * 128:(t0 + i + 1) * 128, :], htB[i], r=[k.Bh[t0 + i]], w=[htB[i]])
        if not isctx:
            rope_load(k, rc, t0 * 128, N, tb % 2)
        norm_stats(k, nx, [(ht[i][:], htB[i]) for i in range(ntl)])
        for i in range(ntl):
            norm_apply(k, nx, ht[i][:], htB[i], i, modidx(1, s, 0), aT2[tb % 2], aT2B[tb % 2], i * 128)

    stage_norm(0)
    for tb in range(9):
        t0, ntl, N, isctx = block_tiles(tb)
        tok0 = t0 * 128
        aTc, aTcB = aT2[tb % 2], aT2B[tb % 2]
        rope_select(rc, tb % 2)
        pend = None
        for c in range(12):
            if isctx and c < 8:
                continue
            f = cnt["F"] % 2
            cnt["F"] += 1
            mm_group(k, psF[f][:, 0:N], psFB[f], [(Wq[:, kc, c * 128:(c + 1) * 128], aTc[:, kc, 0:N]) for kc in range(8)],
                     [WqB[0] if c < 4 else (WqB[1] if c < 8 else WqB[2]), aTcB])
            if pend is not None:
                pend()
            if c < 8:
                pend = rope_store(k, rc, psF[f][:, 0:N], psFB[f], N, True, S["q"][c][:, tok0:tok0 + N], k.Bq[c][tb])
            else:
                pend = rope_store(k, rc, psF[f][:, 0:N], psFB[f], N, not isctx, S["k"][c - 8][:, tok0:tok0 + N], k.Bk[c - 8][tb])
            if c == 7 and tb + 1 < 9:
                stage_norm(tb + 1)
        for i in range(ntl):
            kk = cnt["K"] % 2
            cnt["K"] += 1
            mm_group(k, psK[kk][:, 0:256], psKB[kk], [(aTc[:, kc, i * 128:(i + 1) * 128], Wq[:, kc, 1536:1792]) for kc in range(8)],
                     [WqB[3], aTcB])
            if pend is not None:
                pend()
                pend = None
            sc.op("dve", lambda: nc.vector.tensor_copy(out=vst[kk][:], in_=psK[kk][:, 0:256]), r=[psKB[kk]], w=[vstB[kk]])
            sc.dma("sp", S["v"][(t0 + i) * 128:(t0 + i + 1) * 128, 0:256], vst[kk][:], vstB[kk], r=[vstB[kk]], w=[k.Bv[t0 + i]])
    barrier(k)


def phase_B1(k):
    nc, sc, st, I, S = k.nc, k.sc, k.pes, k.I, k.S
    kT = sbt(k, st, "b1_kT", [128, 4, NTOK], BF16)
    kTB = _bufs("b1_kT", 4)
    for j in range(4):
        sc.dma("sp", kT[:, j, :], S["k"][j], kTB[j], r=[k.Bk[j][b] for b in range(9)], w=[kTB[j]])
    vA = sbt(k, st, "b1_vA", [128, NT, 256], BF16)
    vAB = Buf("b1_vA")
    vv = S["v"].rearrange("(t p) e -> p t e", p=128)
    for a, b in ((0, 17), (17, 34)):
        sc.dma("sp", vA[:, a:b, :], vv[:, a:b, 0:256], vAB, r=[k.Bv[t] for t in range(a, b)], w=[vAB])
    sk = sbt(k, st, "b1_sk", [128, 8], F32)
    es = sbt(k, st, "b1_es", [128, 8], F32)
    skB, esB = Buf("b1_sk"), Buf("b1_es")
    sc.dma("sp", sk[:], I["sink_col"], skB, w=[skB])
    sc.op("act", lambda: nc.scalar.activation(out=es[:], in_=sk[:], func=AF.Exp), r=[skB], w=[esB])
    psS = [pst(k, st, "b1_psS%d" % i, [128, 2, 512]) for i in range(2)]
    psSB = _bufs("b1_psS", 2)
    psO = [pst(k, st, "b1_psO%d" % i, [128, 512]) for i in range(2)]
    psOB = _bufs("b1_psO", 2)
    psL = [pst(k, st, "b1_psL%d" % i, [128, 512]) for i in range(2)]
    psLB = _bufs("b1_psL", 2)
    E = [sbt(k, st, "b1_E%d" % i, [128, 2, 512], BF16) for i in range(3)]
    EB8 = [[Buf("b1_E%d_%d" % (i, j)) for j in range(8)] for i in range(3)]

    def ebs(er, qa, qe):
        return [EB8[er][e * 4 + qt] for e in range(2) for qt in range(qa, qe)]
    qblk = [sbt(k, st, "b1_q%d" % i, [128, 512], BF16) for i in range(2)]
    qblkB = _bufs("b1_q", 2)
    LT = [sbt(k, st, "b1_LT%d" % i, [128, 512], F32) for i in range(2)]
    LTB = _bufs("b1_LT", 2)
    RR = [sbt(k, st, "b1_RR%d" % i, [128, 512], F32) for i in range(2)]
    RRB = _bufs("b1_RR", 2)
    osg = [sbt(k, st, "b1_osg%d" % i, [128, 512], BF16) for i in range(2)]
    osgB = _bufs("b1_osg", 2)
    its = [(qb, c) for qb in range(int(os.environ.get("B1_QB", "8"))) for c in range(8)]
    cS = cE = 0

    def load_q(n):
        qb, c = its[n]
        sc.dma("sp", qblk[n % 2][:], S["q"][c][:, qb * 512:(qb + 1) * 512], qblkB[n % 2], r=[k.Bq[c][qb]], w=[qblkB[n % 2]])

    load_q(0)
    for n, (qb, c) in enumerate(its):
        j = c // 2
        x2 = n % 2
        qq, qqB = qblk[x2], qblkB[x2]
        if n + 1 < len(its):
            load_q(n + 1)
        tl = [(32, 0, 4), (33, 0, 4)]
        for m in range(-1, 5):
            kt = 4 * qb + m
            if 0 <= kt <= 31:
                tl.append((kt, max(0, m - 1), min(3, m + 1) + 1))
        nt = len(tl)
        sl = {}

        def emit_S(i):
            nonlocal cS
            s = cS % 2
            cS += 1
            sl[i] = s
            kt, qa, qe = tl[i]
            for e in range(2):
                sc.op("pe", lambda: nc.tensor.matmul(psS[s][:, e, qa * 128:qe * 128],
                                                     lhsT=kT[e * 64:(e + 1) * 64, j, kt * 128:(kt + 1) * 128],
                                                     rhs=qq[e * 64:(e + 1) * 64, qa * 128:qe * 128], start=True, stop=True),
                      r=[kTB[j], qqB], w=[psSB[s]], signal=(e == 1))

        emit_S(0)
        for i in range(nt):
            if i + 1 < nt:
                emit_S(i + 1)
            s = sl[i]
            er = cE % 3
            cE += 1
            kt, qa, qe = tl[i]
            lo, hi = qa * 128, qe * 128
            sc.op("act", lambda: nc.scalar.activation(out=E[er][:, :, lo:hi], in_=psS[s][:, :, lo:hi], func=AF.Exp, scale=SCALE),
                  r=[psSB[s]], w=ebs(er, qa, qe))
            if kt < 32:
                for qt in range(qa, qe):
                    rel = kt - (4 * qb + qt)
                    if rel == 0:
                        continue
                    mk = k.mprev if rel == -1 else k.mnext
                    for e in range(2):
                        sc.op("dve", lambda: nc.vector.tensor_tensor(out=E[er][:, e, qt * 128:(qt + 1) * 128],
                                                                     in0=E[er][:, e, qt * 128:(qt + 1) * 128], in1=mk,
                                                                     op=ALU.mult), r=[EB8[er][e * 4 + qt], k.Bc], w=[EB8[er][e * 4 + qt]])
            for e in range(2):
                sc.op("pe", lambda: nc.tensor.matmul(psO[x2][e * 64:(e + 1) * 64, lo:hi], lhsT=vA[:, kt, j * 64:(j + 1) * 64],
                                                     rhs=E[er][:, e, lo:hi], start=(i == 0), stop=(i == nt - 1),
                                                     skip_group_check=True),
                      r=[vAB] + ebs(er, qa, qe), w=[psOB[x2]], signal=False)
            for e in range(2):
                sc.op("pe", lambda: nc.tensor.matmul(psL[x2][e * 64:(e + 1) * 64, lo:hi], lhsT=k.ones[:, 0:64],
                                                     rhs=E[er][:, e, lo:hi], start=(i == 0), stop=(i == nt - 1),
                                                     skip_group_check=True),
                      r=[k.Bc] + ebs(er, qa, qe), w=[psLB[x2]], signal=(e == 1))
        sc.op("act", lambda: nc.scalar.activation(out=LT[x2][:], in_=psL[x2][:], func=AF.Ln, bias=es[:, c:c + 1]),
              r=[psLB[x2], esB], w=[LTB[x2]])
        sc.op("act", lambda: nc.scalar.activation(out=RR[x2][:], in_=LT[x2][:], func=AF.Exp, scale=-1.0),
              r=[LTB[x2]], w=[RRB[x2]])
        sc.op("dve", lambda: nc.vector.tensor_tensor(out=osg[x2][:], in0=psO[x2][:], in1=RR[x2][:], op=ALU.mult),
              r=[psOB[x2], RRB[x2]], w=[osgB[x2]])
        sc.dma("sp", S["o"][c][:, qb * 512:(qb + 1) * 512], osg[x2][:], osgB[x2], r=[osgB[x2]], w=[k.Bo[c][qb]])
    barrier(k)


def host_consts():
    t = np.arange(SEQ)
    row = (t // GRID_W).astype(np.float32)
    col = (t % GRID_W).astype(np.float32)
    nf = HEAD_DIM // 4
    inv = (np.float32(10000.0) ** (-np.arange(nf, dtype=np.float32) / np.float32(nf))).astype(np.float32)
    p = np.arange(128)
    d = p % 64
    pos = np.where((d < 32)[:, None], row[None, :], col[None, :]).astype(np.float32)
    ang = (pos * inv[d % 16][:, None]).astype(np.float32)
    sign = np.where((d % 32) < 16, -1.0, 1.0).astype(np.float32)
    cosT = np.cos(ang).astype(np.float32)
    sinT = (np.sin(ang) * sign[:, None]).astype(np.float32)
    cbf = np.zeros((128, 768), np.float32)
    cbf[:, 0:128] = np.eye(128, dtype=np.float32)
    cbf[:, 128:256] = 1.0
    partner = np.where((p % 32) < 16, p + 16, p - 16)
    cbf[partner, 256 + p] = 1.0
    kl = np.arange(128)[:, None]
    ql = np.arange(128)[None, :]
    cbf[:, 384:512] = (kl >= ql).astype(np.float32)
    cbf[:, 512:640] = (kl <= ql).astype(np.float32)
    cf32 = np.zeros((128, 960), np.float32)
    cf32[0, 704:832] = 1.0
    cf32[32, 832:960] = 1.0
    cf32[:, 0:512] = np.tile(np.eye(128, dtype=np.float32), (1, 4))
    cf32[:, 512:640] = 1.0
    cf32[:, 640] = EPS
    return {"cosT": cosT, "sinT": sinT, "cbf": cbf, "cf32": cf32}


def host_shared(inp):
    f = lambda a: np.ascontiguousarray(np.asarray(a, dtype=np.float32))
    bc = lambda v: np.ascontiguousarray(np.broadcast_to(np.asarray(v, np.float32).reshape(1, -1), (128, np.asarray(v).size)))
    sh = dict(host_consts())
    sh["ada_w"] = f(inp["ada_w"])
    sh["ada_b_bc"] = np.stack([bc(inp["ada_b"][l]) for l in range(DEPTH)])
    sh["ng_bc"] = np.stack([bc(inp["norm1_g"][0]), bc(inp["norm2_g"][0]), bc(inp["norm1_g"][1]), bc(inp["norm2_g"][1]),
                            bc(inp["final_g"])])
    sh["ffn_w_in"] = f(inp["ffn_w_in"])
    sh["ffn_w_out"] = f(inp["ffn_w_out"])
    sh["even_w_in"] = f(inp["even_w_in"][0])
    sh["even_w_out"] = f(inp["even_w_out"][0])
    sh["sgu_wT"] = f(np.transpose(np.asarray(inp["sgu_w"][0]), (2, 0, 1)).reshape(128, 512))
    sh["sgu_b_bc"] = bc(np.asarray(inp["sgu_b"][0]).reshape(-1))
    sh["lqk_bc"] = bc(np.concatenate([np.asarray(inp[n][0]) for n in ("diff_lq1", "diff_lk1", "diff_lq2", "diff_lk2")]))
    sh["subln_col"] = f(np.asarray(inp["diff_subln_g"][0]).reshape(128, 1))
    sh["odd_w_qkv"] = f(inp["odd_w_qkv"][0])
    sh["odd_w_out"] = f(inp["odd_w_out"][0])
    sk = np.asarray(inp["odd_sink"][0], np.float32)
    sh["sink_col"] = f(np.stack([np.concatenate([np.full(64, sk[2 * c]), np.full(64, sk[2 * c + 1])]) for c in range(8)], 1))
    return sh


def host_core(inp, sh, b):
    m = dict(sh)
    m["x"] = np.ascontiguousarray(np.concatenate([np.asarray(inp["x"][b], np.float32), np.asarray(inp["ctx"][b], np.float32)], 0))
    cc = np.concatenate([np.asarray(inp["c"][b], np.float32).reshape(8, 128).T,
                         np.asarray(inp["c_ctx"], np.float32).reshape(8, 128).T], 1)
    m["c_col"] = np.ascontiguousarray(cc)
    return m


_NC_CACHE = {}


def kernel(**inputs):
    if "nc" not in _NC_CACHE:
        _NC_CACHE["nc"] = build()[0]
    nc = _NC_CACHE["nc"]
    sh = host_shared(inputs)
    n = 8
    in_maps = [host_core(inputs, sh, b) for b in range(n)]
    res = run_bass_kernel_spmd(nc, in_maps, core_ids=list(range(n)))
    return np.stack([np.asarray(r["out"], dtype=np.float32) for r in res.results], 0)
```

```python
import contextlib
import math
import os
import numpy as np
import ml_dtypes
import concourse.bass as bass
import concourse.mybir as mybir
from concourse.bass_utils import run_bass_kernel_spmd

F32 = mybir.dt.float32
BF16 = mybir.dt.bfloat16
AF = mybir.ActivationFunctionType
ALU = mybir.AluOpType
AX = mybir.AxisListType

D = 1024
SEQ = 4096
CTX = 256
NTOK = SEQ + CTX
NT = NTOK // 128
DEPTH = 2
EPS = 1e-6
FFN = 2816
NF = FFN // 128
GRID_W = 64
HEAD_DIM = 64
SCALE = HEAD_DIM ** -0.5
LAM_INIT0 = 0.8 - 0.6 * math.exp(-0.3 * 0)


class Buf:
    __slots__ = ("name", "writer", "readers", "sem", "semval")

    def __init__(self, name):
        self.name = name
        self.writer = None
        self.readers = {}
        self.sem = None
        self.semval = 0


class Ins:
    __slots__ = ("key", "sem", "val", "is_dma")

    def __init__(self, key, sem=None, val=None, is_dma=False):
        self.key = key
        self.sem = sem
        self.val = val
        self.is_dma = is_dma


class Sched:
    CE = ("pe", "act", "dve", "pool")

    def __init__(self, nc, es):
        self.nc = nc
        self.es = es
        self.E = {"pe": nc.tensor, "act": nc.scalar, "dve": nc.vector, "pool": nc.gpsimd, "sp": nc.sync}
        self.sem = {e: es.enter_context(nc.semaphore("sem_" + e)) for e in self.CE}
        self.cnt = {e: 0 for e in self.CE}
        self.waited = {}
        self.pending = {e: [] for e in self.CE}
        self.dma_bufs = []
        self.sem_pool = []
        self.sw_bufs = []
        self.n_dsem = 0
        self.n_ins = 0
        self.n_wait = 0

    def _wait(self, stream, d):
        if d.val is None:
            raise RuntimeError("dependency on unsignalled instruction of " + d.key)
        k = (stream, d.sem.name)
        if self.waited.get(k, 0) >= d.val:
            return
        self.waited[k] = d.val
        self.E[stream].wait_ge(d.sem, d.val)
        self.n_wait += 1

    def _deps(self, stream, r, w, is_dma):
        def skip(d):
            return d.key == "pe" and stream == "pe" and not is_dma
        for b in r:
            d = b.writer
            if d is not None and not skip(d):
                self._wait(stream, d)
        for b in w:
            d = b.writer
            if d is not None and not skip(d):
                self._wait(stream, d)
            for d in b.readers.values():
                if not skip(d):
                    self._wait(stream, d)

    def op(self, eng, fn, r=(), w=(), signal=True):
        self._deps(eng, r, w, False)
        inst = fn()
        self.n_ins += 1
        ins = Ins(eng)
        if signal:
            self.cnt[eng] += 1
            inst.then_inc(self.sem[eng], 1)
            ins.sem = self.sem[eng]
            ins.val = self.cnt[eng]
            for p in self.pending[eng]:
                p.sem = ins.sem
                p.val = ins.val
            self.pending[eng] = []
        else:
            self.pending[eng].append(ins)
        for b in r:
            b.readers[eng] = ins
        for b in w:
            b.writer = ins
            b.readers = {}
        return ins

    def dma(self, q, out, in_, chan, r=(), w=()):
        self._deps(q, r, w, True)
        if q == "pool":
            chan = Buf(chan.name + "_sw%d" % self.n_dsem)
            chan.sem = self.es.enter_context(self.nc.semaphore("w%d" % self.n_dsem))
            self.n_dsem += 1
            self.sw_bufs.append(chan)
        if chan.sem is None:
            if self.sem_pool:
                chan.sem, chan.semval = self.sem_pool.pop()
            else:
                chan.sem = self.es.enter_context(self.nc.semaphore("d%d" % self.n_dsem))
                self.n_dsem += 1
            self.dma_bufs.append(chan)
        chan.semval += 16
        self.E[q].dma_start(out=out, in_=in_).then_inc(chan.sem, 16)
        self.n_ins += 1
        ins = Ins("dma_" + chan.name, chan.sem, chan.semval, True)
        for b in r:
            b.readers[ins.key] = ins
        for b in w:
            b.writer = ins
            b.readers = {}
        return ins

    def finish(self):
        for b in self.dma_bufs + self.sw_bufs:
            self._wait("sp", Ins("x", b.sem, b.semval, True))
        for e in self.CE:
            if self.cnt[e]:
                self._wait("sp", Ins(e, self.sem[e], self.cnt[e]))


class K:
    pass


def _bufs(prefix, n):
    return [Buf("%s%d" % (prefix, i)) for i in range(n)]


def build(stop_after=None, dev=False):
    nc = bass.Bass("TRN2", target_bir_lowering=False)
    k = K()
    k.nc = nc
    k.dev = dev

    def din(name, shape, dt=F32):
        return nc.dram_tensor(name, list(shape), dt, kind="ExternalInput").ap()

    def dscr(name, shape, dt):
        return nc.dram_tensor(name, list(shape), dt, kind="ExternalOutput" if dev else "Internal").ap()

    I = {}
    I["x"] = din("x", [NTOK, D])
    I["c_col"] = din("c_col", [128, 16])
    I["ada_w"] = din("ada_w", [DEPTH, D, 6 * D])
    I["ada_b_bc"] = din("ada_b_bc", [DEPTH, 128, 6 * D])
    I["ng_bc"] = din("ng_bc", [5, 128, D])
    I["ffn_w_in"] = din("ffn_w_in", [DEPTH, D, 2 * FFN])
    I["ffn_w_out"] = din("ffn_w_out", [DEPTH, FFN, D])
    I["even_w_in"] = din("even_w_in", [D, 2560])
    I["even_w_out"] = din("even_w_out", [D, D])
    I["sgu_wT"] = din("sgu_wT", [128, 512])
    I["sgu_b_bc"] = din("sgu_b_bc", [128, 512])
    I["lqk_bc"] = din("lqk_bc", [128, 256])
    I["subln_col"] = din("subln_col", [128, 1])
    I["odd_w_qkv"] = din("odd_w_qkv", [D, 1536])
    I["odd_w_out"] = din("odd_w_out", [D, D])
    I["sink_col"] = din("sink_col", [128, 8])
    I["cosT"] = din("cosT", [128, SEQ])
    I["sinT"] = din("sinT", [128, SEQ])
    I["cbf"] = din("cbf", [128, 768])
    I["cf32"] = din("cf32", [128, 960])
    k.I = I
    k.out = nc.dram_tensor("out", [SEQ, D], F32, kind="ExternalOutput").ap()

    S = {}
    S["h"] = dscr("h_scr", [NTOK, D], F32)
    S["modbc"] = dscr("modbc_scr", [6, 128, D], F32)
    S["q"] = dscr("q_scr", [8, 128, NTOK], BF16)
    S["k"] = dscr("k_scr", [4, 128, NTOK], BF16)
    S["v"] = dscr("v_scr", [NTOK, 512], BF16)
    S["o"] = dscr("o_scr", [8, 128, NTOK], BF16)
    S["hm"] = dscr("hm_scr", [NF, 128, NTOK], BF16)
    if dev:
        S["dbg"] = dscr("dbg_scr", [128, 256], F32)
    k.S = S
    k.Bh = _bufs("Dh", NT)
    k.Bmod = _bufs("Dmod", 6)
    k.Bq = [[Buf("Dq%d_%d" % (c, b)) for b in range(9)] for c in range(8)]
    k.Bk = [[Buf("Dk%d_%d" % (c, b)) for b in range(9)] for c in range(4)]
    k.Bv = _bufs("Dv", NT)
    k.Bo = [[Buf("Do%d_%d" % (c, b)) for b in range(9)] for c in range(8)]
    k.Bhm = [[Buf("Dhm%d_%d" % (f, b)) for b in range(9)] for f in range(NF)]

    with contextlib.ExitStack() as es:
        sc = Sched(nc, es)
        k.sc = sc
        k.es = es
        k.marks = []
        k.marks = []
        phases = [phase_prologue, phase_A0, phase_B0, phase_C1_l0, phase_C2_l0,
                  phase_A1, phase_B1, phase_C1_l1, phase_C2_l1]
        for ph in phases:
            with contextlib.ExitStack() as pes:
                k.pes = pes
                ph(k)
            k.marks.append((ph.__name__, dict(sc.cnt)))
            k.marks.append((ph.__name__, dict(sc.cnt)))
            if stop_after is not None and ph.__name__ == stop_after:
                break
        sc.finish()
    k.n_ins = sc.n_ins
    k.n_wait = sc.n_wait
    return nc, k


def sbt(k, st, name, shape, dt):
    return st.enter_context(k.nc.sbuf_tensor(name, list(shape), dt))


def pst(k, st, name, shape, dt=F32):
    return st.enter_context(k.nc.psum_tensor(name, list(shape), dt))


def mm_group(k, out_ap, outB, pairs, rB, signal_last=True):
    sc, nc = k.sc, k.nc
    n = len(pairs)
    for i, (l, r) in enumerate(pairs):
        sc.op("pe", lambda: nc.tensor.matmul(out_ap, lhsT=l, rhs=r, start=(i == 0), stop=(i == n - 1)),
              r=rB, w=[outB], signal=(signal_last and i == n - 1))


def barrier(k):
    sc = k.sc
    for e in sc.CE:
        assert not sc.pending[e], "unsignalled tail on " + e
    for s in ("pe", "act", "dve", "pool", "sp"):
        for e in sc.CE:
            if sc.cnt[e]:
                sc._wait(s, Ins(e, sc.sem[e], sc.cnt[e]))
        for b in sc.dma_bufs + sc.sw_bufs:
            sc._wait(s, Ins("x", b.sem, b.semval, True))
    sc.sw_bufs = []
    for b in sc.dma_bufs:
        sc.sem_pool.append((b.sem, b.semval))
        b.sem = None
    sc.dma_bufs = []


def modidx(l, s, v):
    return ((l * 2 + s) * 4 + v) * 8


class NormCtx:
    def __init__(self, k, st, tag):
        self.junks = [sbt(k, st, tag + "junk%d" % i, [128, D], BF16) for i in range(2)]
        self.junkB = _bufs(tag + "junk", 2)
        self.nj = 0
        self.ss = sbt(k, st, tag + "ss", [128, 4], F32)
        self.sq = sbt(k, st, tag + "sq", [128, 4], F32)
        self.rs = sbt(k, st, tag + "rs", [128, 4], F32)
        self.ssB, self.sqB, self.rsB = Buf(tag + "ss"), Buf(tag + "sq"), Buf(tag + "rs")
        self.xn = [sbt(k, st, tag + "xn%d" % i, [128, D], BF16) for i in range(4)]
        self.xnB = _bufs(tag + "xn", 4)
        self.np_ = 0
        self.psT = [pst(k, st, tag + "psT%d" % i, [128, 8, 128], BF16) for i in range(2)]
        self.psTB = _bufs(tag + "psT", 2)
        self.n = 0


def norm_stats(k, nx, tiles):
    sc, nc = k.sc, k.nc
    nt = len(tiles)
    for i, (ap, B) in enumerate(tiles):
        jj = nx.nj % 2
        nx.nj += 1
        sc.op("act", lambda: nc.scalar.activation(out=nx.junks[jj][:], in_=ap, func=AF.Square,
                                                  accum_out=nx.ss[:, i:i + 1]), r=[B], w=[nx.ssB, nx.junkB[jj]])
    sc.op("act", lambda: nc.scalar.activation(out=nx.sq[:, 0:nt], in_=nx.ss[:, 0:nt], func=AF.Sqrt,
                                              scale=1.0 / D, bias=k.epsc), r=[nx.ssB, k.Bcf], w=[nx.sqB])
    sc.op("dve", lambda: nc.vector.reciprocal(out=nx.rs[:, 0:nt], in_=nx.sq[:, 0:nt]), r=[nx.sqB], w=[nx.rsB])


def norm_xn(k, nx, ap, B, i):
    sc, nc = k.sc, k.nc
    sc.op("dve", lambda: nc.vector.tensor_scalar(out=nx.xn[i][:], in0=ap, scalar1=nx.rs[:, i:i + 1], scalar2=None,
                                                 op0=ALU.mult), r=[B, nx.rsB], w=[nx.xnB[i]])


def norm_tr(k, nx, i, mbase, aT, aTB, col0, on_act=False):
    sc, nc = k.sc, k.nc
    r = nx.np_ % 2
    nx.np_ += 1
    for j in range(8):
        sc.op("pe", lambda: nc.tensor.transpose(nx.psT[r][:, j, :], nx.xn[i][:, j * 128:(j + 1) * 128], k.ident),
              r=[nx.xnB[i], k.Bc], w=[nx.psTB[r]], signal=(j == 7))
    for j in range(8):
        if on_act:
            sc.op("act", lambda: nc.scalar.activation(out=aT[:, j, col0:col0 + 128], in_=nx.psT[r][:, j, :],
                                                      func=AF.Identity, scale=k.modcol[:, mbase + j:mbase + j + 1],
                                                      bias=k.modcol[:, mbase + 8 + j:mbase + 9 + j]),
                  r=[nx.psTB[r], k.Bmodcol], w=[aTB])
            continue
        sc.op("dve", lambda: nc.vector.tensor_scalar(out=aT[:, j, col0:col0 + 128], in0=nx.psT[r][:, j, :],
                                                     scalar1=k.modcol[:, mbase + j:mbase + j + 1],
                                                     scalar2=k.modcol[:, mbase + 8 + j:mbase + 9 + j],
                                                     op0=ALU.mult, op1=ALU.add),
              r=[nx.psTB[r], k.Bmodcol], w=[aTB])


def norm_apply(k, nx, ap, B, i, mbase, aT, aTB, col0, on_act=False):
    norm_xn(k, nx, ap, B, i)
    norm_tr(k, nx, i, mbase, aT, aTB, col0, on_act)


class RopeCtx:
    def __init__(self, k, st, tag):
        self.qb = [sbt(k, st, tag + "qb%d" % i, [128, 512], BF16) for i in range(2)]
        self.qbB = _bufs(tag + "qb", 2)
        self.psR = pst(k, st, tag + "psR", [128, 512])
        self.psRB = Buf(tag + "psR")
        self.css = [sbt(k, st, tag + "cs%d" % i, [128, 512], F32) for i in range(2)]
        self.sns = [sbt(k, st, tag + "sn%d" % i, [128, 512], F32) for i in range(2)]
        self.csBs = _bufs(tag + "cs", 2)
        self.cs, self.sn, self.csB = self.css[0], self.sns[0], self.csBs[0]
        self.t1 = [sbt(k, st, tag + "t1%d" % i, [128, 512], F32) for i in range(2)]
        self.t2 = [sbt(k, st, tag + "t2%d" % i, [128, 512], F32) for i in range(2)]
        self.t1B, self.t2B = _bufs(tag + "t1", 2), _bufs(tag + "t2", 2)
        self.qst = [sbt(k, st, tag + "qst%d" % i, [128, 512], BF16) for i in range(3)]
        self.qstB = _bufs(tag + "qst", 3)
        self.n = 0
        self.m = 0


def rope_load(k, rc, tok0, N, par=0):
    k.sc.dma("sp", rc.css[par][:, 0:N], k.I["cosT"][:, tok0:tok0 + N], rc.csBs[par], w=[rc.csBs[par]])
    k.sc.dma("sp", rc.sns[par][:, 0:N], k.I["sinT"][:, tok0:tok0 + N], rc.csBs[par], w=[rc.csBs[par]])


def rope_select(rc, par):
    rc.cs, rc.sn, rc.csB = rc.css[par], rc.sns[par], rc.csBs[par]


def rope_store(k, rc, psF, psFB, N, rope, dst_ap, dstB):
    sc, nc = k.sc, k.nc
    s3 = rc.m % 3
    rc.m += 1
    if not rope:
        sc.op("act", lambda: nc.scalar.copy(out=rc.qst[s3][:, 0:N], in_=psF), r=[psFB], w=[rc.qstB[s3]])

        def part2():
            sc.dma("sp", dst_ap, rc.qst[s3][:, 0:N], rc.qstB[s3], r=[rc.qstB[s3]], w=[dstB])
        return part2
    r = rc.n % 2
    rc.n += 1
    sc.op("act", lambda: nc.scalar.copy(out=rc.qb[r][:, 0:N], in_=psF), r=[psFB], w=[rc.qbB[r]])
    sc.op("dve", lambda: nc.vector.tensor_tensor(out=rc.t1[r][:, 0:N], in0=psF, in1=rc.cs[:, 0:N], op=ALU.mult),
          r=[psFB, rc.csB, rc.qbB[r]], w=[rc.t1B[r]])

    def part2():
        sc.op("pe", lambda: nc.tensor.matmul(rc.psR[:, 0:N], lhsT=k.rperm, rhs=rc.qb[r][:, 0:N], start=True, stop=True),
              r=[rc.qbB[r], k.Bc], w=[rc.psRB])
        sc.op("dve", lambda: nc.vector.tensor_tensor(out=rc.t2[r][:, 0:N], in0=rc.psR[:, 0:N], in1=rc.sn[:, 0:N],
                                                     op=ALU.mult), r=[rc.psRB, rc.csB], w=[rc.t2B[r]])
        sc.op("pool", lambda: nc.gpsimd.tensor_tensor(out=rc.qst[s3][:, 0:N], in0=rc.t1[r][:, 0:N],
                                                      in1=rc.t2[r][:, 0:N], op=ALU.add),
              r=[rc.t1B[r], rc.t2B[r]], w=[rc.qstB[s3]])
        sc.dma("sp", dst_ap, rc.qst[s3][:, 0:N], rc.qstB[s3], r=[rc.qstB[s3]], w=[dstB])
    return part2


def load_w(k, dst, src, B, pieces=1):
    v = src.rearrange("(kc p) n -> p kc n", p=128)
    kc = v.shape[1]
    step = (kc + pieces - 1) // pieces
    for a in range(0, kc, step):
        b = min(kc, a + step)
        k.sc.dma("pool", dst[:, a:b, :], v[:, a:b, :], B, w=[B])


def phase_prologue(k):
    nc, sc, es, st, I = k.nc, k.sc, k.es, k.pes, k.I
    cbf = sbt(k, es, "sb_cbf", [128, 768], BF16)
    cf32 = sbt(k, es, "sb_cf32", [128, 960], F32)
    k.Bc, k.Bcf = Buf("cbf"), Buf("cf32")
    sc.dma("pool", cbf[:], I["cbf"], k.Bc, w=[k.Bc])
    sc.dma("sp", cf32[:], I["cf32"], k.Bcf, w=[k.Bcf])
    k.ident, k.ones, k.rperm = cbf[:, 0:128], cbf[:, 128:256], cbf[:, 256:384]
    k.mprev, k.mnext = cbf[:, 384:512], cbf[:, 512:640]
    k.identF4, k.onesF, k.epsc = cf32[:, 0:512], cf32[:, 512:640], cf32[:, 640:641]
    k.sel = [cf32[0:64, 704:832], cf32[0:64, 832:960]]
    k.modcol = sbt(k, es, "modcol", [128, 128], F32)
    k.Bmodcol = Buf("modcol")

    ccol = sbt(k, st, "p_ccol", [128, 16], F32)
    scol = sbt(k, st, "p_scol", [128, 16], F32)
    ccolB, scolB = Buf("p_ccol"), Buf("p_scol")
    sc.dma("sp", ccol[:], I["c_col"], ccolB, w=[ccolB])
    sc.op("act", lambda: nc.scalar.activation(out=scol[:], in_=ccol[:], func=AF.Silu), r=[ccolB], w=[scolB])
    L = sbt(k, st, "p_L", [128, 16, 128], F32)
    LB = Buf("p_L")
    for j in range(16):
        sc.op("dve", lambda: nc.vector.tensor_scalar(out=L[:, j, :], in0=k.onesF,
                                                     scalar1=scol[:, j:j + 1], scalar2=None, op0=ALU.mult),
              r=[scolB, k.Bcf], w=[LB])
    stage = [sbt(k, st, "p_stage%d" % i, [128, 8, 512], F32) for i in range(2)]
    stageB = _bufs("p_stage", 2)
    bias = [sbt(k, st, "p_bias%d" % i, [128, 512], F32) for i in range(2)]
    biasB = _bufs("p_bias", 2)
    ngt = sbt(k, st, "p_ngt", [128, D], F32)
    ngtB = Buf("p_ngt")
    psm = [pst(k, st, "p_ps%d" % i, [128, 512]) for i in range(2)]
    psmB = _bufs("p_ps", 2)
    tA = [sbt(k, st, "p_tA%d" % i, [128, 512], F32) for i in range(2)]
    tAB = _bufs("p_tA", 2)
    tB_ = [sbt(k, st, "p_tB%d" % i, [128, 512], F32) for i in range(2)]
    tBB = _bufs("p_tB", 2)
    tC = [sbt(k, st, "p_tC%d" % i, [128, 4, 128], F32) for i in range(2)]
    tCB = _bufs("p_tC", 2)
    gbc = [sbt(k, st, "p_gbc%d" % i, [128, D], F32) for i in range(2)]
    gbcB = _bufs("p_gbc", 2)
    ngrow = {(0, 1): 0, (0, 4): 1, (1, 1): 2, (1, 4): 3}
    cnt = 0
    gcnt = 0
    for l in range(DEPTH):
        for n in range(12):
            v, half = n // 2, n % 2
            r = cnt % 2
            cnt += 1
            sc.dma("sp", stage[r][:], I["ada_w"][l].rearrange("(kc p) n -> p kc n", p=128)[:, :, n * 512:(n + 1) * 512],
                   stageB[r], w=[stageB[r]])
            sc.dma("sp", bias[r][:], I["ada_b_bc"][l][:, n * 512:(n + 1) * 512], biasB[r], w=[biasB[r]])
            if v in (1, 4) and half == 0:
                sc.dma("sp", ngt[:], I["ng_bc"][ngrow[(l, v)]], ngtB, w=[ngtB])
            for s in (0, 1):
                if s == 1 and l == 1 and v >= 2:
                    continue
                mm_group(k, psm[s][:], psmB[s], [(L[:, s * 8 + kc, :], stage[r][:, kc, :]) for kc in range(8)],
                         [LB, stageB[r]])
                if v in (2, 5):
                    sc.op("dve", lambda: nc.vector.tensor_tensor(out=gbc[s][:, half * 512:(half + 1) * 512], in0=psm[s][:],
                                                                 in1=bias[r][:], op=ALU.add),
                          r=[psmB[s], biasB[r]], w=[gbcB[s]])
                    if half == 1:
                        mi = {(0, 0, 2): 0, (0, 0, 5): 1, (0, 1, 2): 2, (0, 1, 5): 3, (1, 0, 2): 4, (1, 0, 5): 5}[(l, s, v)]
                        sc.dma("sp", k.S["modbc"][mi], gbc[s][:], gbcB[s], r=[gbcB[s]], w=[k.Bmod[mi]])
                    continue
                sc.op("dve", lambda: nc.vector.tensor_tensor(out=tA[s][:], in0=psm[s][:], in1=bias[r][:], op=ALU.add),
                      r=[psmB[s], biasB[r]], w=[tAB[s]])
                src, srcB = tA[s], tAB[s]
                if v in (1, 4):
                    sc.op("dve", lambda: nc.vector.scalar_tensor_tensor(out=tB_[s][:], in0=tA[s][:], scalar=1.0,
                                                                         in1=ngt[:, half * 512:(half + 1) * 512],
                                                                         op0=ALU.add, op1=ALU.mult),
                          r=[tAB[s], ngtB], w=[tBB[s]])
                    src, srcB = tB_[s], tBB[s]
                sc.op("pool", lambda: nc.gpsimd.tensor_tensor(out=tC[s][:].rearrange("p a b -> p (a b)"), in0=src[:],
                                                              in1=k.identF4, op=ALU.mult),
                      r=[srcB, k.Bcf], w=[tCB[s]])
                vi = {0: 1, 1: 0, 3: 3, 4: 2}[v]
                mb = modidx(l, s, vi) + half * 4
                sc.op("dve", lambda: nc.vector.tensor_reduce(out=k.modcol[:, mb:mb + 4], in_=tC[s][:], axis=AX.X,
                                                             op=ALU.add),
                      r=[tCB[s]], w=[k.Bmodcol])
    if k.dev:
        sc.dma("sp", k.S["dbg"][:, 0:112], k.modcol[:, 0:112], k.Bmodcol, r=[k.Bmodcol])
    barrier(k)


def block_tiles(tb):
    if tb < 8:
        return tb * 4, 4, 512, False
    return 32, 2, 256, True


def phase_A0(k):
    nc, sc, st, I, S = k.nc, k.sc, k.pes, k.I, k.S
    Win = sbt(k, st, "a0_Win", [128, 8, 2560], BF16)
    WinB = _bufs("a0_Win", 5)
    wv = I["even_w_in"].rearrange("(kc p) n -> p kc n", p=128)
    for g in (1, 0, 2, 3, 4):
        sc.dma("pool", Win[:, :, g * 512:(g + 1) * 512], wv[:, :, g * 512:(g + 1) * 512], WinB[g], w=[WinB[g]])
    WoA = sbt(k, st, "a0_WoA", [128, 4, D], BF16)
    WoAB = Buf("a0_WoA")
    load_w(k, WoA, I["even_w_out"][0:512, :], WoAB)
    wsT = sbt(k, st, "a0_wsT", [128, 512], BF16)
    wsTB = Buf("a0_wsT")
    sc.dma("pool", wsT[:], I["sgu_wT"], wsTB, w=[wsTB])
    bsbc = sbt(k, st, "a0_bsbc", [128, 512], F32)
    bsB = Buf("a0_bsbc")
    sc.dma("sp", bsbc[:], I["sgu_b_bc"], bsB, w=[bsB])
    gbc = [sbt(k, st, "a0_gbc%d" % i, [128, D], F32) for i in range(2)]
    gbcB = _bufs("a0_gbc", 2)
    sc.dma("sp", gbc[0][:], S["modbc"][0], gbcB[0], r=[k.Bmod[0]], w=[gbcB[0]])
    sc.dma("sp", gbc[1][:], S["modbc"][2], gbcB[1], r=[k.Bmod[2]], w=[gbcB[1]])

    xt = [sbt(k, st, "a0_xt%d" % i, [128, D], F32) for i in range(8)]
    xtB = _bufs("a0_xt", 8)
    nx = NormCtx(k, st, "a0_")
    rc = RopeCtx(k, st, "a0_")
    aT = sbt(k, st, "a0_aT", [128, 8, 512], BF16)
    aTB = Buf("a0_aT")
    guT = sbt(k, st, "a0_guT", [128, 4, 512], BF16)
    guB = Buf("a0_guT")
    vg = [sbt(k, st, "a0_vg%d" % i, [128, 4, 128], F32) for i in range(4)]
    vgB = _bufs("a0_vg", 4)
    lsum = [sbt(k, st, "a0_lsum%d" % i, [128, 4], F32) for i in range(4)]
    lnm = [sbt(k, st, "a0_lnm%d" % i, [128, 4], F32) for i in range(4)]
    lsq = [sbt(k, st, "a0_lsq%d" % i, [128, 4], F32) for i in range(4)]
    lsr = [sbt(k, st, "a0_lsr%d" % i, [128, 4], F32) for i in range(4)]
    lrs = [sbt(k, st, "a0_lrs%d" % i, [128, 4], F32) for i in range(4)]
    lsumB, lnmB, lsqB, lsrB, lrsB = (_bufs("a0_l%s" % n, 4) for n in "abcde")
    vf = [sbt(k, st, "a0_vf%d" % i, [128, 512], BF16) for i in range(4)]
    vfB = _bufs("a0_vf", 4)
    mx = [sbt(k, st, "a0_mx%d" % i, [128, 4, 128], F32) for i in range(2)]
    mxB = _bufs("a0_mx", 2)
    AoT = sbt(k, st, "a0_AoT", [128, 4, 512], BF16)
    AoB = _bufs("a0_AoT", 4)
    yt = [sbt(k, st, "a0_yt%d" % i, [128, 512], F32) for i in range(2)]
    ytB = _bufs("a0_yt", 2)
    vst = [sbt(k, st, "a0_vst%d" % i, [128, 512], BF16) for i in range(2)]
    vstB = _bufs("a0_vst", 2)
    psS = pst(k, st, "a0_psS", [128, 4, 128])
    psSB = Buf("a0_psS")
    psK = [pst(k, st, "a0_psK%d" % i, [128, 512]) for i in range(2)]
    psKB = _bufs("a0_psK", 2)
    psF = [pst(k, st, "a0_psF%d" % i, [128, 512]) for i in range(2)]
    psFB = _bufs("a0_psF", 2)
    cnt = {"K": 0, "F": 0}
    NB = int(os.environ.get("A0_BLOCKS", "9"))

    def xs(tb, i):
        return (tb % 2) * 4 + i

    def stage_load(tb):
        t0, ntl, N, isctx = block_tiles(tb)
        for i in range(ntl):
            j = xs(tb, i)
            sc.dma("sp", xt[j][:], I["x"][(t0 + i) * 128:(t0 + i + 1) * 128, :], xtB[j], w=[xtB[j]])
        if not isctx:
            rope_load(k, rc, t0 * 128, N, tb % 2)

    def stage_norm(tb):
        t0, ntl, N, isctx = block_tiles(tb)
        s = 1 if isctx else 0
        norm_stats(k, nx, [(xt[xs(tb, i)][:], xtB[xs(tb, i)]) for i in range(ntl)])
        for i in range(ntl):
            j = xs(tb, i)
            norm_xn(k, nx, xt[j][:], xtB[j], i)

    def stage_norm2(tb):
        t0, ntl, N, isctx = block_tiles(tb)
        s = 1 if isctx else 0
        for i in range(ntl):
            norm_tr(k, nx, i, modidx(0, s, 0), aT, aTB, i * 128)

    def stage_proj(tb):
        t0, ntl, N, isctx = block_tiles(tb)
        tok0 = t0 * 128
        rope_select(rc, tb % 2)
        if tb + 1 < NB:
            stage_load(tb + 1)
        for i in range(ntl):
            kk = cnt["K"] % 2
            cnt["K"] += 1
            r = i
            mm_group(k, psK[kk][:], psKB[kk], [(aT[:, kc, i * 128:(i + 1) * 128], Win[:, kc, 512:1024]) for kc in range(8)],
                     [WinB[1], aTB])
            sc.op("act", lambda: nc.scalar.activation(out=vg[r][:].rearrange("p a b -> p (a b)"), in_=psK[kk][:], func=AF.Gelu),
                  r=[psKB[kk]], w=[vgB[r]])
            sc.op("dve", lambda: nc.vector.tensor_reduce(out=lsum[r][:], in_=vg[r][:], axis=AX.X, op=ALU.add),
                  r=[vgB[r]], w=[lsumB[r]])
            sc.op("dve", lambda: nc.vector.tensor_scalar(out=lnm[r][:], in0=lsum[r][:], scalar1=-1.0 / 128, scalar2=None,
                                                         op0=ALU.mult), r=[lsumB[r]], w=[lnmB[r]])
        for i in range(ntl):
            r = i
            for g in range(4):
                jj = nx.nj % 2
                nx.nj += 1
                sc.op("act", lambda: nc.scalar.activation(out=nx.junks[jj][:, 0:128], in_=vg[r][:, g, :], func=AF.Square,
                                                          bias=lnm[r][:, g:g + 1], accum_out=lsq[r][:, g:g + 1]),
                      r=[vgB[r], lnmB[r]], w=[lsqB[r], nx.junkB[jj]])
        for g in range(4):
            f = cnt["F"] % 2
            cnt["F"] += 1
            mm_group(k, psF[f][:, 0:N], psFB[f], [(Win[:, kc, g * 128:(g + 1) * 128], aT[:, kc, 0:N]) for kc in range(8)],
                     [WinB[0], aTB])
            sc.op("act", lambda: nc.scalar.activation(out=guT[:, g, 0:N], in_=psF[f][:, 0:N], func=AF.Gelu),
                  r=[psFB[f]], w=[guB])
        if tb + 1 < NB:
            stage_norm(tb + 1)
        for i in range(ntl):
            r = i
            sc.op("act", lambda: nc.scalar.activation(out=lsr[r][:], in_=lsq[r][:], func=AF.Sqrt, scale=1.0 / 128,
                                                      bias=k.epsc), r=[lsqB[r], k.Bcf], w=[lsrB[r]])
            sc.op("dve", lambda: nc.vector.reciprocal(out=lrs[r][:], in_=lsr[r][:]), r=[lsrB[r]], w=[lrsB[r]])
            for g in range(4):
                sc.op("dve", lambda: nc.vector.tensor_scalar(out=vf[r][:, g * 128:(g + 1) * 128], in0=vg[r][:, g, :],
                                                             scalar1=lnm[r][:, g:g + 1], scalar2=lrs[r][:, g:g + 1],
                                                             op0=ALU.add, op1=ALU.mult),
                      r=[vgB[r], lnmB[r], lrsB[r]], w=[vfB[r]])
        pend = None
        for c in range(8):
            f = cnt["F"] % 2
            cnt["F"] += 1
            mm_group(k, psF[f][:, 0:N], psFB[f],
                     [(Win[:, kc, 1024 + c * 128:1024 + (c + 1) * 128], aT[:, kc, 0:N]) for kc in range(8)],
                     [WinB[2 + c // 4], aTB])
            if pend is not None:
                pend()
            if c < 4:
                pend = rope_store(k, rc, psF[f][:, 0:N], psFB[f], N, not isctx, S["q"][c][:, tok0:tok0 + N], k.Bq[c][tb])
            else:
                pend = rope_store(k, rc, psF[f][:, 0:N], psFB[f], N, not isctx, S["k"][c - 4][:, tok0:tok0 + N], k.Bk[c - 4][tb])
        for i in range(ntl):
            kk = cnt["K"] % 2
            cnt["K"] += 1
            mm_group(k, psK[kk][:], psKB[kk], [(aT[:, kc, i * 128:(i + 1) * 128], Win[:, kc, 2048:2560]) for kc in range(8)],
                     [WinB[4], aTB])
            if pend is not None:
                pend()
                pend = None
            v2 = (t0 + i) % 2
            sc.op("dve", lambda: nc.vector.tensor_copy(out=vst[v2][:], in_=psK[kk][:]), r=[psKB[kk]], w=[vstB[v2]])
            sc.dma("sp", S["v"][(t0 + i) * 128:(t0 + i + 1) * 128, :], vst[v2][:], vstB[v2], r=[vstB[v2]], w=[k.Bv[t0 + i]])

    def stage_tail(tb):
        t0, ntl, N, isctx = block_tiles(tb)
        s = 1 if isctx else 0
        for i in range(ntl):
            r = i
            m2 = i % 2
            j = xs(tb, i)
            for g in range(4):
                sc.op("pe", lambda: nc.tensor.matmul(psS[:, g, :], lhsT=vf[r][:, g * 128:(g + 1) * 128],
                                                     rhs=wsT[:, g * 128:(g + 1) * 128], start=True, stop=True),
                      r=[vfB[r], wsTB], w=[psSB], signal=(g == 3))
            sc.op("dve", lambda: nc.vector.tensor_tensor(out=mx[m2][:].rearrange("p a b -> p (a b)"),
                                                         in0=psS[:].rearrange("p a b -> p (a b)"), in1=bsbc[:], op=ALU.add),
                  r=[psSB, bsB], w=[mxB[m2]])
            sc.op("pool", lambda: nc.gpsimd.tensor_tensor(out=AoT[:, :, i * 128:(i + 1) * 128], in0=mx[m2][:],
                                                          in1=guT[:, :, i * 128:(i + 1) * 128], op=ALU.mult),
                  r=[mxB[m2], guB], w=[AoB[i]])
        for i in range(ntl):
            j = xs(tb, i)
            for nh in range(2):
                kk = cnt["K"] % 2
                cnt["K"] += 1
                mm_group(k, psK[kk][:], psKB[kk],
                         [(AoT[:, g, i * 128:(i + 1) * 128], WoA[:, g, nh * 512:(nh + 1) * 512]) for g in range(4)],
                         [AoB[i], WoAB])
                sc.op("dve", lambda: nc.vector.tensor_tensor(out=yt[nh][:], in0=psK[kk][:],
                                                             in1=gbc[s][:, nh * 512:(nh + 1) * 512], op=ALU.mult),
                      r=[psKB[kk], gbcB[s]], w=[ytB[nh]])
                sc.op("pool", lambda: nc.gpsimd.tensor_tensor(out=xt[j][:, nh * 512:(nh + 1) * 512], in0=yt[nh][:],
                                                              in1=xt[j][:, nh * 512:(nh + 1) * 512], op=ALU.add),
                      r=[ytB[nh], xtB[j]], w=[xtB[j]])
            sc.dma("sp", S["h"][(t0 + i) * 128:(t0 + i + 1) * 128, :], xt[j][:], xtB[j], r=[xtB[j]], w=[k.Bh[t0 + i]])

    stage_load(0)
    stage_norm(0)
    stage_norm2(0)
    for tb in range(NB):
        stage_proj(tb)
        if tb + 1 < NB:
            stage_norm2(tb + 1)
        stage_tail(tb)
    barrier(k)


def phase_B0(k):
    nc, sc, st, I, S = k.nc, k.sc, k.pes, k.I, k.S
    kT = sbt(k, st, "b0_kT", [128, 4, NTOK], BF16)
    kTB = _bufs("b0_kT", 4)
    for h in range(4):
        sc.dma("sp", kT[:, h, :], S["k"][h], kTB[h], r=[k.Bk[h][b] for b in range(9)], w=[kTB[h]])
    vA = sbt(k, st, "b0_vA", [128, NT, 512], BF16)
    vAB = Buf("b0_vA")
    vv = S["v"].rearrange("(t p) e -> p t e", p=128)
    for a, b in ((0, 17), (17, 34)):
        sc.dma("sp", vA[:, a:b, :], vv[:, a:b, :], vAB, r=[k.Bv[t] for t in range(a, b)], w=[vAB])
    lq = sbt(k, st, "b0_lq", [128, 256], F32)
    lqB = Buf("b0_lq")
    sc.dma("sp", lq[:], I["lqk_bc"], lqB, w=[lqB])
    sg = sbt(k, st, "b0_sg", [128, 1], F32)
    sgB = Buf("b0_sg")
    sc.dma("sp", sg[:], I["subln_col"], sgB, w=[sgB])
    prod = sbt(k, st, "b0_prod", [128, 2, 64], F32)
    s12 = sbt(k, st, "b0_s12", [128, 2], F32)
    e12 = sbt(k, st, "b0_e12", [128, 2], F32)
    dl = sbt(k, st, "b0_dl", [128, 1], F32)
    nl = sbt(k, st, "b0_nl", [128, 1], F32)
    gl = sbt(k, st, "b0_gl", [128, 1], F32)
    prodB, s12B, e12B, dlB, nlB, glB = (Buf("b0_" + n) for n in ("prod", "s12", "e12", "dl", "nl", "gl"))
    for j in range(2):
        sc.op("dve", lambda: nc.vector.tensor_tensor(out=prod[:, j, :], in0=lq[:, j * 128:j * 128 + 64],
                                                     in1=lq[:, j * 128 + 64:j * 128 + 128], op=ALU.mult),
              r=[lqB], w=[prodB])
    sc.op("dve", lambda: nc.vector.tensor_reduce(out=s12[:], in_=prod[:], axis=AX.X, op=ALU.add), r=[prodB], w=[s12B])
    sc.op("act", lambda: nc.scalar.activation(out=e12[:], in_=s12[:], func=AF.Exp), r=[s12B], w=[e12B])
    sc.op("dve", lambda: nc.vector.tensor_tensor(out=dl[:], in0=e12[:, 1:2], in1=e12[:, 0:1], op=ALU.subtract),
          r=[e12B], w=[dlB])
    sc.op("dve", lambda: nc.vector.tensor_scalar(out=nl[:], in0=dl[:], scalar1=-LAM_INIT0, scalar2=None, op0=ALU.add),
          r=[dlB], w=[nlB])
    sc.op("dve", lambda: nc.vector.tensor_scalar(out=gl[:], in0=sg[:], scalar1=1.0 - LAM_INIT0, scalar2=None, op0=ALU.mult),
          r=[sgB], w=[glB])

    psS = [pst(k, st, "b0_psS%d" % i, [128, 2, 512]) for i in range(2)]
    psSB = _bufs("b0_psS", 2)
    psO = [pst(k, st, "b0_psO%d" % i, [128, 512]) for i in range(2)]
    psOB = _bufs("b0_psO", 2)
    psL = [pst(k, st, "b0_psL%d" % i, [128, 512]) for i in range(2)]
    psLB = _bufs("b0_psL", 2)
    E = [sbt(k, st, "b0_E%d" % i, [128, 2, 512], BF16) for i in range(3)]
    EB = _bufs("b0_E", 3)
    qblk = [sbt(k, st, "b0_q%d" % i, [128, 512], BF16) for i in range(2)]
    qblkB = _bufs("b0_q", 2)
    Lsb = sbt(k, st, "b0_Lsb", [128, 512], F32)
    LsbB = Buf("b0_Lsb")
    R = [sbt(k, st, "b0_R%d" % i, [128, 512], F32) for i in range(2)]
    RB = _bufs("b0_R", 2)
    T = [sbt(k, st, "b0_T%d" % i, [128, 512], F32) for i in range(2)]
    TB = _bufs("b0_T", 2)
    ost2 = [sbt(k, st, "b0_ost%d" % i, [128, NTOK], F32) for i in range(2)]
    ostB2 = [_bufs("b0_ost%d_" % i, 9) for i in range(2)]
    sqs2 = [sbt(k, st, "b0_sqs%d" % i, [128, NTOK], BF16) for i in range(2)]
    sqsB2 = [_bufs("b0_sqs%d_" % i, 9) for i in range(2)]
    pend_p2 = []
    SD = [sbt(k, st, "b0_SD%d" % i, [128, 512], F32) for i in range(2)]
    SDB = _bufs("b0_SD", 2)
    RS = [sbt(k, st, "b0_RS%d" % i, [128, 512], F32) for i in range(2)]
    RSB = _bufs("b0_RS", 2)
    osg = [sbt(k, st, "b0_osg%d" % i, [128, 512], BF16) for i in range(2)]
    osgB = _bufs("b0_osg", 2)
    nH = int(os.environ.get("B0_HEADS", "4"))
    nQ = int(os.environ.get("B0_QB", "9"))
    its = [(h, qb) for h in range(nH) for qb in range(9 - nQ, 9)]
    cS = cE = 0
    pend_fin = []

    def qinfo(qb):
        return (512, list(range(NT))) if qb < 8 else (256, [32, 33])

    def load_q(n):
        h, qb = its[n]
        N, _ = qinfo(qb)
        sc.dma("sp", qblk[n % 2][:, 0:N], S["q"][h][:, qb * 512:qb * 512 + N], qblkB[n % 2], r=[k.Bq[h][qb]],
               w=[qblkB[n % 2]])

    load_q(0)
    for n, (h, qb) in enumerate(its):
        N, tiles = qinfo(qb)
        q0 = qb * 512
        qq, qqB = qblk[n % 2], qblkB[n % 2]
        ost, ostB, sqs, sqsB = ost2[h % 2], ostB2[h % 2], sqs2[h % 2], sqsB2[h % 2]
        if n + 1 < len(its):
            load_q(n + 1)
        nt = len(tiles)
        sl = {}

        def emit_S(i):
            nonlocal cS
            s = cS % 2
            cS += 1
            sl[i] = s
            kt = tiles[i]
            for j in range(2):
                sc.op("pe", lambda: nc.tensor.matmul(psS[s][:, j, 0:N], lhsT=kT[j * 64:(j + 1) * 64, h, kt * 128:(kt + 1) * 128],
                                                     rhs=qq[j * 64:(j + 1) * 64, 0:N], start=True, stop=True),
                      r=[kTB[h], qqB], w=[psSB[s]], signal=(j == 1))

        emit_S(0)
        for i in range(nt):
            if i + 1 < nt:
                emit_S(i + 1)
            if pend_fin and (i == 3 or i == nt - 1):
                pend_fin.pop()()
            if pend_p2 and qb < 8 and (i == 12 or (i == 24 and len(pend_p2) > 8 - qb)):
                pend_p2.pop(0)()
            s = sl[i]
            e = cE % 3
            cE += 1
            kt = tiles[i]
            sc.op("act", lambda: nc.scalar.activation(out=E[e][:, :, 0:N], in_=psS[s][:, :, 0:N], func=AF.Exp, scale=SCALE),
                  r=[psSB[s]], w=[EB[e]])
            for j in range(2):
                sc.op("pe", lambda: nc.tensor.matmul(psO[j][:, 0:N], lhsT=vA[:, kt, h * 128:(h + 1) * 128], rhs=E[e][:, j, 0:N],
                                                     start=(i == 0), stop=(i == nt - 1)),
                      r=[vAB, EB[e]], w=[psOB[j]], signal=False)
            for j in range(2):
                sc.op("pe", lambda: nc.tensor.matmul(psL[0][j * 32:(j + 1) * 32, 0:N], lhsT=k.ones[:, 0:32], rhs=E[e][:, j, 0:N],
                                                     start=(i == 0), stop=(i == nt - 1), skip_group_check=True),
                      r=[k.Bc, EB[e]], w=[psLB[0]], signal=(j == 1))
        for j in range(2):
            sc.op("dve", lambda: nc.vector.tensor_copy(out=T[j][:, 0:N], in_=psO[j][:, 0:N]), r=[psOB[j]], w=[TB[j]])
        sc.op("act", lambda: nc.scalar.copy(out=Lsb[0:64, 0:N], in_=psL[0][0:64, 0:N]), r=[psLB[0]], w=[LsbB])

        def fin(N=N, q0=q0, qb=qb, ost=ost, ostB=ostB, sqs=sqs, sqsB=sqsB):
            sc.op("dve", lambda: nc.vector.reciprocal(out=R[0][0:64, 0:N], in_=Lsb[0:64, 0:N]), r=[LsbB], w=[RB[0]])
            for j in (1, 0):
                sc.op("pe", lambda: nc.tensor.matmul(psL[1][:, 0:N], lhsT=k.sel[j], rhs=R[0][0:64, 0:N], start=True, stop=True),
                      r=[k.Bcf, RB[0]], w=[psLB[1]])
                sc.op("dve", lambda: nc.vector.tensor_tensor(out=T[j][:, 0:N], in0=T[j][:, 0:N], in1=psL[1][:, 0:N], op=ALU.mult),
                      r=[psLB[1], TB[j]], w=[TB[j]])
            sc.op("dve", lambda: nc.vector.scalar_tensor_tensor(out=ost[:, q0:q0 + N], in0=T[1][:, 0:N], scalar=nl[:, 0:1],
                                                                in1=T[0][:, 0:N], op0=ALU.mult, op1=ALU.add),
                  r=[TB[0], TB[1], nlB], w=[ostB[qb]])
            sc.op("pool", lambda: nc.gpsimd.tensor_tensor(out=sqs[:, q0:q0 + N], in0=ost[:, q0:q0 + N], in1=ost[:, q0:q0 + N],
                                                          op=ALU.mult), r=[ostB[qb]], w=[sqsB[qb]])
        pend_fin.append(fin)
        if qb == 8 or n + 1 == len(its):
            pend_fin.pop()()
        if qb == 8:
            for q2 in range(9 - nQ, 9):
                def unit(q2=q2, h=h, sqs=sqs, sqsB=sqsB):
                    N2, _ = qinfo(q2)
                    p0 = q2 * 512
                    x2 = q2 % 2
                    sc.op("pe", lambda: nc.tensor.matmul(psL[1][:, 0:N2], lhsT=k.ones, rhs=sqs[:, p0:p0 + N2], start=True, stop=True),
                          r=[k.Bc, sqsB[q2]], w=[psLB[1]])
                    sc.op("act", lambda: nc.scalar.activation(out=SD[x2][:, 0:N2], in_=psL[1][:, 0:N2], func=AF.Ln,
                                                              scale=1.0 / 128, bias=k.epsc), r=[psLB[1], k.Bcf], w=[SDB[x2]])
                    sc.op("act", lambda: nc.scalar.activation(out=RS[x2][:, 0:N2], in_=SD[x2][:, 0:N2], func=AF.Exp, scale=-0.5),
                          r=[SDB[x2]], w=[RSB[x2]])
                    sc.op("dve", lambda: nc.vector.scalar_tensor_tensor(out=osg[x2][:, 0:N2], in0=ost2[h % 2][:, p0:p0 + N2],
                                                                        scalar=gl[:, 0:1], in1=RS[x2][:, 0:N2],
                                                                        op0=ALU.mult, op1=ALU.mult),
                          r=[ostB2[h % 2][q2], glB, RSB[x2]], w=[osgB[x2]])
                    sc.dma("sp", S["o"][h][:, p0:p0 + N2], osg[x2][:, 0:N2], osgB[x2], r=[osgB[x2]], w=[k.Bo[h][q2]])
                pend_p2.append(unit)
    while pend_p2:
        pend_p2.pop(0)()
    barrier(k)


def phase_C1(k, l):
    nc, sc, st, I, S = k.nc, k.sc, k.pes, k.I, k.S
    tg = "c1%d_" % l
    nk = 4 if l == 0 else 8
    nblk = 9 if l == 0 else 8
    Wo = sbt(k, st, tg + "Wo", [128, nk, D], BF16)
    WoB = Buf(tg + "Wo")
    load_w(k, Wo, I["even_w_out"][512:1024, :] if l == 0 else I["odd_w_out"], WoB)
    Wfi = sbt(k, st, tg + "Wfi", [128, 8, 2 * FFN], BF16)
    WfiB = _bufs(tg + "Wfi", 11)
    wv = I["ffn_w_in"][l].rearrange("(kc p) n -> p kc n", p=128)
    for g in range(11):
        sc.dma("pool", Wfi[:, :, g * 512:(g + 1) * 512], wv[:, :, g * 512:(g + 1) * 512], WfiB[g], w=[WfiB[g]])
    gbc = [sbt(k, st, tg + "gbc%d" % i, [128, D], F32) for i in range(2)]
    gbcB = _bufs(tg + "gbc", 2)
    mi = 0 if l == 0 else 4
    sc.dma("sp", gbc[0][:], S["modbc"][mi], gbcB[0], r=[k.Bmod[mi]], w=[gbcB[0]])
    if l == 0:
        sc.dma("sp", gbc[1][:], S["modbc"][2], gbcB[1], r=[k.Bmod[2]], w=[gbcB[1]])
    ht = [sbt(k, st, tg + "ht%d" % i, [128, D], F32) for i in range(4)]
    htB = _bufs(tg + "ht", 4)
    ob = sbt(k, st, tg + "ob", [128, nk, 512], BF16)
    obB = Buf(tg + "ob")
    nx = NormCtx(k, st, tg)
    aT2 = [sbt(k, st, tg + "aT%d" % i, [128, 8, 512], BF16) for i in range(2)]
    aT2B = _bufs(tg + "aT", 2)
    yt = [sbt(k, st, tg + "yt%d" % i, [128, 512], F32) for i in range(2)]
    ytB = _bufs(tg + "yt", 2)
    sg = [sbt(k, st, tg + "sg%d" % i, [128, 512], F32) for i in range(2)]
    sgB = _bufs(tg + "sg", 2)
    hst = [sbt(k, st, tg + "hst%d" % i, [128, 512], BF16) for i in range(3)]
    hstB = _bufs(tg + "hst", 3)
    psK = [pst(k, st, tg + "psK%d" % i, [128, 512]) for i in range(2)]
    psKB = _bufs(tg + "psK", 2)
    psG = [pst(k, st, tg + "psG%d" % i, [128, 512]) for i in range(2)]
    psGB = _bufs(tg + "psG", 2)
    psU = [pst(k, st, tg + "psU%d" % i, [128, 512]) for i in range(2)]
    psUB = _bufs(tg + "psU", 2)
    ov = S["o"][0:nk].rearrange("c p n -> p c n")
    cnt = {"K": 0, "G": 0, "H": 0}

    def preA(tb):
        t0, ntl, N, isctx = block_tiles(tb)
        tok0 = t0 * 128
        s = 1 if isctx else 0
        for i in range(ntl):
            sc.dma("sp", ht[i][:], S["h"][(t0 + i) * 128:(t0 + i + 1) * 128, :], htB[i], r=[k.Bh[t0 + i]], w=[htB[i]])
        sc.dma("sp", ob[:, :, 0:N], ov[:, :, tok0:tok0 + N], obB, r=[k.Bo[c][tb] for c in range(nk)], w=[obB])
        for i in range(ntl):
            for nh in range(2):
                kk = cnt["K"] % 2
                cnt["K"] += 1
                mm_group(k, psK[kk][:], psKB[kk],
                         [(ob[:, c, i * 128:(i + 1) * 128], Wo[:, c, nh * 512:(nh + 1) * 512]) for c in range(nk)], [obB, WoB])
                sc.op("dve", lambda: nc.vector.tensor_tensor(out=yt[nh][:], in0=psK[kk][:], in1=gbc[s][:, nh * 512:(nh + 1) * 512],
                                                             op=ALU.mult), r=[psKB[kk], gbcB[s]], w=[ytB[nh]])
                sc.op("pool", lambda: nc.gpsimd.tensor_tensor(out=ht[i][:, nh * 512:(nh + 1) * 512], in0=yt[nh][:],
                                                              in1=ht[i][:, nh * 512:(nh + 1) * 512], op=ALU.add),
                      r=[ytB[nh], htB[i]], w=[htB[i]])
            sc.dma("sp", S["h"][(t0 + i) * 128:(t0 + i + 1) * 128, :], ht[i][:], htB[i], r=[htB[i]], w=[k.Bh[t0 + i]])
        norm_stats(k, nx, [(ht[i][:], htB[i]) for i in range(ntl)])

    def preB(tb):
        t0, ntl, N, isctx = block_tiles(tb)
        s = 1 if isctx else 0
        for i in range(ntl):
            norm_apply(k, nx, ht[i][:], htB[i], i, modidx(l, s, 2), aT2[tb % 2], aT2B[tb % 2], i * 128, on_act=True)

    preA(0)
    preB(0)
    for tb in range(nblk):
        t0, ntl, N, isctx = block_tiles(tb)
        tok0 = t0 * 128
        aT, aTB = aT2[tb % 2], aT2B[tb % 2]
        for f in range(NF):
            if tb + 1 < nblk and f == 4:
                preA(tb + 1)
            if tb + 1 < nblk and f == 13:
                preB(tb + 1)
            g2 = cnt["G"] % 2
            cnt["G"] += 1
            h3 = cnt["H"] % 3
            cnt["H"] += 1
            c0, c1 = f * 128, FFN + f * 128
            mm_group(k, psG[g2][:, 0:N], psGB[g2], [(Wfi[:, kc, c0:c0 + 128], aT[:, kc, 0:N]) for kc in range(8)],
                     [WfiB[c0 // 512], WfiB[(c0 + 127) // 512], aTB])
            mm_group(k, psU[g2][:, 0:N], psUB[g2], [(Wfi[:, kc, c1:c1 + 128], aT[:, kc, 0:N]) for kc in range(8)],
                     [WfiB[c1 // 512], WfiB[(c1 + 127) // 512], aTB])
            sc.op("act", lambda: nc.scalar.activation(out=sg[g2][:, 0:N], in_=psG[g2][:, 0:N], func=AF.Silu),
                  r=[psGB[g2]], w=[sgB[g2]])
            sc.op("dve", lambda: nc.vector.tensor_tensor(out=hst[h3][:, 0:N], in0=psU[g2][:, 0:N], in1=sg[g2][:, 0:N], op=ALU.mult),
                  r=[psUB[g2], sgB[g2]], w=[hstB[h3]])
            sc.dma("sp", S["hm"][f][:, tok0:tok0 + N], hst[h3][:, 0:N], hstB[h3], r=[hstB[h3]], w=[k.Bhm[f][tb]])
    barrier(k)


def phase_C2(k, l):
    nc, sc, st, I, S = k.nc, k.sc, k.pes, k.I, k.S
    tg = "c2%d_" % l
    nblk = 9 if l == 0 else 8
    last = l == DEPTH - 1
    Wfo = sbt(k, st, tg + "Wfo", [128, NF, D], BF16)
    WfoB = Buf(tg + "Wfo")
    load_w(k, Wfo, I["ffn_w_out"][l], WfoB, pieces=2)
    gbc = [sbt(k, st, tg + "gbc%d" % i, [128, D], F32) for i in range(2)]
    gbcB = _bufs(tg + "gbc", 2)
    mi = 1 if l == 0 else 5
    sc.dma("sp", gbc[0][:], S["modbc"][mi], gbcB[0], r=[k.Bmod[mi]], w=[gbcB[0]])
    if l == 0:
        sc.dma("sp", gbc[1][:], S["modbc"][3], gbcB[1], r=[k.Bmod[3]], w=[gbcB[1]])
    if last:
        fg = sbt(k, st, tg + "fg", [128, D], F32)
        fgB = Buf(tg + "fg")
        sc.dma("sp", fg[:], I["ng_bc"][4], fgB, w=[fgB])
        ot = [sbt(k, st, tg + "ot%d" % i, [128, D], F32) for i in range(2)]
        otB = _bufs(tg + "ot", 2)
    ht = [sbt(k, st, tg + "ht%d" % i, [128, D], F32) for i in range(4)]
    htB = _bufs(tg + "ht", 4)
    hmb = [sbt(k, st, tg + "hmb%d" % i, [128, NF, 512], BF16) for i in range(2)]
    hmbB = _bufs(tg + "hmb", 2)
    nx = NormCtx(k, st, tg)
    yt = [sbt(k, st, tg + "yt%d" % i, [128, 512], F32) for i in range(2)]
    ytB = _bufs(tg + "yt", 2)
    psK = [pst(k, st, tg + "psK%d" % i, [128, 512]) for i in range(4)]
    psKB = _bufs(tg + "psK", 4)
    hv = S["hm"].rearrange("f p n -> p f n")
    nK = nO = 0

    def loads(tb):
        t0, ntl, N, isctx = block_tiles(tb)
        hb = tb % 2
        sc.dma("sp", hmb[hb][:, :, 0:N], hv[:, :, t0 * 128:t0 * 128 + N], hmbB[hb], r=[k.Bhm[f][tb] for f in range(NF)],
               w=[hmbB[hb]])

    loads(0)
    for tb in range(nblk):
        t0, ntl, N, isctx = block_tiles(tb)
        s = 1 if isctx else 0
        hb = tb % 2
        for i in range(ntl):
            sc.dma("sp", ht[i][:], S["h"][(t0 + i) * 128:(t0 + i + 1) * 128, :], htB[i], r=[k.Bh[t0 + i]], w=[htB[i]])
        if tb + 1 < nblk:
            loads(tb + 1)
        for i in range(ntl):
            for nh in range(2):
                kk = nK % 4
                nK += 1
                mm_group(k, psK[kk][:], psKB[kk],
                         [(hmb[hb][:, f, i * 128:(i + 1) * 128], Wfo[:, f, nh * 512:(nh + 1) * 512]) for f in range(NF)],
                         [hmbB[hb], WfoB])
                sc.op("dve", lambda: nc.vector.tensor_tensor(out=yt[nh][:], in0=psK[kk][:], in1=gbc[s][:, nh * 512:(nh + 1) * 512],
                                                             op=ALU.mult), r=[psKB[kk], gbcB[s]], w=[ytB[nh]])
                sc.op("pool", lambda: nc.gpsimd.tensor_tensor(out=ht[i][:, nh * 512:(nh + 1) * 512], in0=yt[nh][:],
                                                              in1=ht[i][:, nh * 512:(nh + 1) * 512], op=ALU.add),
                      r=[ytB[nh], htB[i]], w=[htB[i]])
            if not last:
                sc.dma("sp", S["h"][(t0 + i) * 128:(t0 + i + 1) * 128, :], ht[i][:], htB[i], r=[htB[i]], w=[k.Bh[t0 + i]])
        if last:
            norm_stats(k, nx, [(ht[i][:], htB[i]) for i in range(ntl)])
            for i in range(ntl):
                o2 = nO % 2
                nO += 1
                sc.op("dve", lambda: nc.vector.scalar_tensor_tensor(out=ot[o2][:], in0=ht[i][:], scalar=nx.rs[:, i:i + 1],
                                                                    in1=fg[:], op0=ALU.mult, op1=ALU.mult),
                      r=[htB[i], nx.rsB, fgB], w=[otB[o2]])
                sc.dma("sp", k.out[(t0 + i) * 128:(t0 + i + 1) * 128, :], ot[o2][:], otB[o2], r=[otB[o2]])
    barrier(k)


def phase_C1_l0(k):
    phase_C1(k, 0)


def phase_C2_l0(k):
    phase_C2(k, 0)


def phase_C1_l1(k):
    phase_C1(k, 1)


def phase_C2_l1(k):
    phase_C2(k, 1)


def phase_A1(k):
    nc, sc, st, I, S = k.nc, k.sc, k.pes, k.I, k.S
    Wq = sbt(k, st, "a1_Wq", [128, 8, 1792], BF16)
    WqB = _bufs("a1_Wq", 4)
    wv = I["odd_w_qkv"].rearrange("(kc p) n -> p kc n", p=128)
    for g in range(2):
        sc.dma("pool", Wq[:, :, g * 512:(g + 1) * 512], wv[:, :, g * 512:(g + 1) * 512], WqB[g], w=[WqB[g]])
    for j in range(4):
        for e in range(2):
            sc.dma("pool", Wq[:, :, 1024 + j * 128 + e * 64:1024 + j * 128 + (e + 1) * 64],
                   wv[:, :, 1024 + j * 64:1024 + (j + 1) * 64], WqB[2], w=[WqB[2]])
    sc.dma("pool", Wq[:, :, 1536:1792], wv[:, :, 1280:1536], WqB[3], w=[WqB[3]])
    ht = [sbt(k, st, "a1_ht%d" % i, [128, D], F32) for i in range(4)]
    htB = _bufs("a1_ht", 4)
    nx = NormCtx(k, st, "a1_")
    rc = RopeCtx(k, st, "a1_")
    aT = sbt(k, st, "a1_aT", [128, 8, 512], BF16)
    aTB = Buf("a1_aT")
    vst = [sbt(k, st, "a1_vst%d" % i, [128, 256], BF16) for i in range(2)]
    vstB = _bufs("a1_vst", 2)
    psK = [pst(k, st, "a1_psK%d" % i, [128, 512]) for i in range(2)]
    psKB = _bufs("a1_psK", 2)
    psF = [pst(k, st, "a1_psF%d" % i, [128, 512]) for i in range(2)]
    psFB = _bufs("a1_psF", 2)
    cnt = {"K": 0, "F": 0}
    aT2 = [aT, sbt(k, st, "a1_aTb", [128, 8, 512], BF16)]
    aT2B = [aTB, Buf("a1_aTb")]

    def stage_load(tb):
        t0, ntl, N, isctx = block_tiles(tb)
        for i in range(ntl):
            sc.dma("sp", ht[i][:], S["h"][(t0 + i) * 128:(t0 + i + 1) * 128, :], htB[i], r=[k.Bh[t0 + i]], w=[htB[i]])
        if not isctx:
            rope_load(k, rc, t0 * 128, N, tb % 2)

    def stage_norm(tb):
        t0, ntl, N, isctx = block_tiles(tb)
        s = 1 if isctx else 0
        norm_stats(k, nx, [(ht[i][:], htB[i]) for i in range(ntl)])
        for i in range(ntl):
            norm_apply(k, nx, ht[i][:], htB[i], i, modidx(1, s, 0), aT2[tb % 2], aT2B[tb % 2], i * 128)

    stage_load(0)
    stage_norm(0)
    for tb in range(9):
        t0, ntl, N, isctx = block_tiles(tb)
        tok0 = t0 * 128
        aTc, aTcB = aT2[tb % 2], aT2B[tb % 2]
        rope_select(rc, tb % 2)
        if tb + 1 < 9:
            stage_load(tb + 1)
        pend = None
        for c in range(12):
            if isctx and c < 8:
                continue
            f = cnt["F"] % 2
            cnt["F"] += 1
            mm_group(k, psF[f][:, 0:N], psFB[f], [(Wq[:, kc, c * 128:(c + 1) * 128], aTc[:, kc, 0:N]) for kc in range(8)],
                     [WqB[0] if c < 4 else (WqB[1] if c < 8 else WqB[2]), aTcB])
            if pend is not None:
                pend()
            if c < 8:
                pend = rope_store(k, rc, psF[f][:, 0:N], psFB[f], N, True, S["q"][c][:, tok0:tok0 + N], k.Bq[c][tb])
            else:
                pend = rope_store(k, rc, psF[f][:, 0:N], psFB[f], N, not isctx, S["k"][c - 8][:, tok0:tok0 + N], k.Bk[c - 8][tb])
            if c == 7 and tb + 1 < 9:
                stage_norm(tb + 1)
        for i in range(ntl):
            kk = cnt["K"] % 2
            cnt["K"] += 1
            mm_group(k, psK[kk][:, 0:256], psKB[kk], [(aTc[:, kc, i * 128:(i + 1) * 128], Wq[:, kc, 1536:1792]) for kc in range(8)],
                     [WqB[3], aTcB])
            if pend is not None:
                pend()
                pend = None
            sc.op("dve", lambda: nc.vector.tensor_copy(out=vst[kk][:], in_=psK[kk][:, 0:256]), r=[psKB[kk]], w=[vstB[kk]])
            sc.dma("sp", S["v"][(t0 + i) * 128:(t0 + i + 1) * 128, 0:256], vst[kk][:], vstB[kk], r=[vstB[kk]], w=[k.Bv[t0 + i]])
    barrier(k)


def phase_B1(k):
    nc, sc, st, I, S = k.nc, k.sc, k.pes, k.I, k.S
    kT = sbt(k, st, "b1_kT", [128, 4, NTOK], BF16)
    kTB = _bufs("b1_kT", 4)
    for j in range(4):
        sc.dma("sp", kT[:, j, :], S["k"][j], kTB[j], r=[k.Bk[j][b] for b in range(9)], w=[kTB[j]])
    vA = sbt(k, st, "b1_vA", [128, NT, 256], BF16)
    vAB = Buf("b1_vA")
    vv = S["v"].rearrange("(t p) e -> p t e", p=128)
    for a, b in ((0, 17), (17, 34)):
        sc.dma("sp", vA[:, a:b, :], vv[:, a:b, 0:256], vAB, r=[k.Bv[t] for t in range(a, b)], w=[vAB])
    sk = sbt(k, st, "b1_sk", [128, 8], F32)
    es = sbt(k, st, "b1_es", [128, 8], F32)
    skB, esB = Buf("b1_sk"), Buf("b1_es")
    sc.dma("sp", sk[:], I["sink_col"], skB, w=[skB])
    sc.op("act", lambda: nc.scalar.activation(out=es[:], in_=sk[:], func=AF.Exp), r=[skB], w=[esB])
    psS = [pst(k, st, "b1_psS%d" % i, [128, 2, 512]) for i in range(2)]
    psSB = _bufs("b1_psS", 2)
    psO = [pst(k, st, "b1_psO%d" % i, [128, 512]) for i in range(2)]
    psOB = _bufs("b1_psO", 2)
    psL = [pst(k, st, "b1_psL%d" % i, [128, 512]) for i in range(2)]
    psLB = _bufs("b1_psL", 2)
    E = [sbt(k, st, "b1_E%d" % i, [128, 2, 512], BF16) for i in range(3)]
    EB8 = [[Buf("b1_E%d_%d" % (i, j)) for j in range(8)] for i in range(3)]

    def ebs(er, qa, qe):
        return [EB8[er][e * 4 + qt] for e in range(2) for qt in range(qa, qe)]
    qblk = [sbt(k, st, "b1_q%d" % i, [128, 512], BF16) for i in range(2)]
    qblkB = _bufs("b1_q", 2)
    LT = [sbt(k, st, "b1_LT%d" % i, [128, 512], F32) for i in range(2)]
    LTB = _bufs("b1_LT", 2)
    RR = [sbt(k, st, "b1_RR%d" % i, [128, 512], F32) for i in range(2)]
    RRB = _bufs("b1_RR", 2)
    osg = [sbt(k, st, "b1_osg%d" % i, [128, 512], BF16) for i in range(2)]
    osgB = _bufs("b1_osg", 2)
    its = [(qb, c) for qb in range(int(os.environ.get("B1_QB", "8"))) for c in range(8)]
    cS = cE = 0

    def load_q(n):
        qb, c = its[n]
        sc.dma("sp", qblk[n % 2][:], S["q"][c][:, qb * 512:(qb + 1) * 512], qblkB[n % 2], r=[k.Bq[c][qb]], w=[qblkB[n % 2]])

    load_q(0)
    for n, (qb, c) in enumerate(its):
        j = c // 2
        x2 = n % 2
        qq, qqB = qblk[x2], qblkB[x2]
        if n + 1 < len(its):
            load_q(n + 1)
        tl = [(32, 0, 4), (33, 0, 4)]
        for m in range(-1, 5):
            kt = 4 * qb + m
            if 0 <= kt <= 31:
                tl.append((kt, max(0, m - 1), min(3, m + 1) + 1))
        nt = len(tl)
        sl = {}

        def emit_S(i):
            nonlocal cS
            s = cS % 2
            cS += 1
            sl[i] = s
            kt, qa, qe = tl[i]
            for e in range(2):
                sc.op("pe", lambda: nc.tensor.matmul(psS[s][:, e, qa * 128:qe * 128],
                                                     lhsT=kT[e * 64:(e + 1) * 64, j, kt * 128:(kt + 1) * 128],
                                                     rhs=qq[e * 64:(e + 1) * 64, qa * 128:qe * 128], start=True, stop=True),
                      r=[kTB[j], qqB], w=[psSB[s]], signal=(e == 1))

        emit_S(0)
        for i in range(nt):
            if i + 1 < nt:
                emit_S(i + 1)
            s = sl[i]
            er = cE % 3
            cE += 1
            kt, qa, qe = tl[i]
            lo, hi = qa * 128, qe * 128
            sc.op("act", lambda: nc.scalar.activation(out=E[er][:, :, lo:hi], in_=psS[s][:, :, lo:hi], func=AF.Exp, scale=SCALE),
                  r=[psSB[s]], w=ebs(er, qa, qe))
            if kt < 32:
                for qt in range(qa, qe):
                    rel = kt - (4 * qb + qt)
                    if rel == 0:
                        continue
                    mk = k.mprev if rel == -1 else k.mnext
                    for e in range(2):
                        sc.op("dve", lambda: nc.vector.tensor_tensor(out=E[er][:, e, qt * 128:(qt + 1) * 128],
                                                                     in0=E[er][:, e, qt * 128:(qt + 1) * 128], in1=mk,
                                                                     op=ALU.mult), r=[EB8[er][e * 4 + qt], k.Bc], w=[EB8[er][e * 4 + qt]])
            for e in range(2):
                sc.op("pe", lambda: nc.tensor.matmul(psO[x2][e * 64:(e + 1) * 64, lo:hi], lhsT=vA[:, kt, j * 64:(j + 1) * 64],
                                                     rhs=E[er][:, e, lo:hi], start=(i == 0), stop=(i == nt - 1),
                                                     skip_group_check=True),
                      r=[vAB] + ebs(er, qa, qe), w=[psOB[x2]], signal=False)
            for e in range(2):
                sc.op("pe", lambda: nc.tensor.matmul(psL[x2][e * 64:(e + 1) * 64, lo:hi], lhsT=k.ones[:, 0:64],
                                                     rhs=E[er][:, e, lo:hi], start=(i == 0), stop=(i == nt - 1),
                                                     skip_group_check=True),
                      r=[k.Bc] + ebs(er, qa, qe), w=[psLB[x2]], signal=(e == 1))
        sc.op("act", lambda: nc.scalar.activation(out=LT[x2][:], in_=psL[x2][:], func=AF.Ln, bias=es[:, c:c + 1]),
              r=[psLB[x2], esB], w=[LTB[x2]])
        sc.op("act", lambda: nc.scalar.activation(out=RR[x2][:], in_=LT[x2][:], func=AF.Exp, scale=-1.0),
              r=[LTB[x2]], w=[RRB[x2]])
        sc.op("dve", lambda: nc.vector.tensor_tensor(out=osg[x2][:], in0=psO[x2][:], in1=RR[x2][:], op=ALU.mult),
              r=[psOB[x2], RRB[x2]], w=[osgB[x2]])
        sc.dma("sp", S["o"][c][:, qb * 512:(qb + 1) * 512], osg[x2][:], osgB[x2], r=[osgB[x2]], w=[k.Bo[c][qb]])
    barrier(k)


def host_consts():
    t = np.arange(SEQ)
    row = (t // GRID_W).astype(np.float32)
    col = (t % GRID_W).astype(np.float32)
    nf = HEAD_DIM // 4
    inv = (np.float32(10000.0) ** (-np.arange(nf, dtype=np.float32) / np.float32(nf))).astype(np.float32)
    p = np.arange(128)
    d = p % 64
    pos = np.where((d < 32)[:, None], row[None, :], col[None, :]).astype(np.float32)
    ang = (pos * inv[d % 16][:, None]).astype(np.float32)
    sign = np.where((d % 32) < 16, -1.0, 1.0).astype(np.float32)
    cosT = np.cos(ang).astype(np.float32)
    sinT = (np.sin(ang) * sign[:, None]).astype(np.float32)
    cbf = np.zeros((128, 768), np.float32)
    cbf[:, 0:128] = np.eye(128, dtype=np.float32)
    cbf[:, 128:256] = 1.0
    partner = np.where((p % 32) < 16, p + 16, p - 16)
    cbf[partner, 256 + p] = 1.0
    kl = np.arange(128)[:, None]
    ql = np.arange(128)[None, :]
    cbf[:, 384:512] = (kl >= ql).astype(np.float32)
    cbf[:, 512:640] = (kl <= ql).astype(np.float32)
    cf32 = np.zeros((128, 960), np.float32)
    cf32[0, 704:832] = 1.0
    cf32[32, 832:960] = 1.0
    cf32[:, 0:512] = np.tile(np.eye(128, dtype=np.float32), (1, 4))
    cf32[:, 512:640] = 1.0
    cf32[:, 640] = EPS
    return {"cosT": cosT, "sinT": sinT, "cbf": cbf, "cf32": cf32}


def host_shared(inp):
    f = lambda a: np.ascontiguousarray(np.asarray(a, dtype=np.float32))
    bc = lambda v: np.ascontiguousarray(np.broadcast_to(np.asarray(v, np.float32).reshape(1, -1), (128, np.asarray(v).size)))
    sh = dict(host_consts())
    sh["ada_w"] = f(inp["ada_w"])
    sh["ada_b_bc"] = np.stack([bc(inp["ada_b"][l]) for l in range(DEPTH)])
    sh["ng_bc"] = np.stack([bc(inp["norm1_g"][0]), bc(inp["norm2_g"][0]), bc(inp["norm1_g"][1]), bc(inp["norm2_g"][1]),
                            bc(inp["final_g"])])
    sh["ffn_w_in"] = f(inp["ffn_w_in"])
    sh["ffn_w_out"] = f(inp["ffn_w_out"])
    sh["even_w_in"] = f(inp["even_w_in"][0])
    sh["even_w_out"] = f(inp["even_w_out"][0])
    sh["sgu_wT"] = f(np.transpose(np.asarray(inp["sgu_w"][0]), (2, 0, 1)).reshape(128, 512))
    sh["sgu_b_bc"] = bc(np.asarray(inp["sgu_b"][0]).reshape(-1))
    sh["lqk_bc"] = bc(np.concatenate([np.asarray(inp[n][0]) for n in ("diff_lq1", "diff_lk1", "diff_lq2", "diff_lk2")]))
    sh["subln_col"] = f(np.asarray(inp["diff_subln_g"][0]).reshape(128, 1))
    sh["odd_w_qkv"] = f(inp["odd_w_qkv"][0])
    sh["odd_w_out"] = f(inp["odd_w_out"][0])
    sk = np.asarray(inp["odd_sink"][0], np.float32)
    sh["sink_col"] = f(np.stack([np.concatenate([np.full(64, sk[2 * c]), np.full(64, sk[2 * c + 1])]) for c in range(8)], 1))
    return sh


def host_core(inp, sh, b):
    m = dict(sh)
    m["x"] = np.ascontiguousarray(np.concatenate([np.asarray(inp["x"][b], np.float32), np.asarray(inp["ctx"][b], np.float32)], 0))
    cc = np.concatenate([np.asarray(inp["c"][b], np.float32).reshape(8, 128).T,
                         np.asarray(inp["c_ctx"], np.float32).reshape(8, 128).T], 1)
    m["c_col"] = np.ascontiguousarray(cc)
    return m


_NC_CACHE = {}


def kernel(**inputs):
    if "nc" not in _NC_CACHE:
        _NC_CACHE["nc"] = build()[0]
    nc = _NC_CACHE["nc"]
    sh = host_shared(inputs)
    n = 8
    in_maps = [host_core(inputs, sh, b) for b in range(n)]
    res = run_bass_kernel_spmd(nc, in_maps, core_ids=list(range(n)))
    return np.stack([np.asarray(r["out"], dtype=np.float32) for r in res.results], 0)
```

```python
import contextlib
import math
import os
import numpy as np
import ml_dtypes
import concourse.bass as bass
import concourse.mybir as mybir
from concourse.bass_utils import run_bass_kernel_spmd

F32 = mybir.dt.float32
BF16 = mybir.dt.bfloat16
AF = mybir.ActivationFunctionType
ALU = mybir.AluOpType
AX = mybir.AxisListType

D = 1024
SEQ = 4096
CTX = 256
NTOK = SEQ + CTX
NT = NTOK // 128
DEPTH = 2
EPS = 1e-6
FFN = 2816
NF = FFN // 128
GRID_W = 64
HEAD_DIM = 64
SCALE = HEAD_DIM ** -0.5
LAM_INIT0 = 0.8 - 0.6 * math.exp(-0.3 * 0)


class Buf:
    __slots__ = ("name", "writer", "readers", "sem", "semval")

    def __init__(self, name):
        self.name = name
        self.writer = None
        self.readers = {}
        self.sem = None
        self.semval = 0


class Ins:
    __slots__ = ("key", "sem", "val", "is_dma")

    def __init__(self, key, sem=None, val=None, is_dma=False):
        self.key = key
        self.sem = sem
        self.val = val
        self.is_dma = is_dma


class Sched:
    CE = ("pe", "act", "dve", "pool")

    def __init__(self, nc, es):
        self.nc = nc
        self.es = es
        self.E = {"pe": nc.tensor, "act": nc.scalar, "dve": nc.vector, "pool": nc.gpsimd, "sp": nc.sync}
        self.sem = {e: es.enter_context(nc.semaphore("sem_" + e)) for e in self.CE}
        self.cnt = {e: 0 for e in self.CE}
        self.waited = {}
        self.pending = {e: [] for e in self.CE}
        self.dma_bufs = []
        self.sem_pool = []
        self.sw_bufs = []
        self.n_dsem = 0
        self.n_ins = 0
        self.n_wait = 0

    def _wait(self, stream, d):
        if d.val is None:
            raise RuntimeError("dependency on unsignalled instruction of " + d.key)
        k = (stream, d.sem.name)
        if self.waited.get(k, 0) >= d.val:
            return
        self.waited[k] = d.val
        self.E[stream].wait_ge(d.sem, d.val)
        self.n_wait += 1

    def _deps(self, stream, r, w, is_dma):
        def skip(d):
            return d.key == "pe" and stream == "pe" and not is_dma
        for b in r:
            d = b.writer
            if d is not None and not skip(d):
                self._wait(stream, d)
        for b in w:
            d = b.writer
            if d is not None and not skip(d):
                self._wait(stream, d)
            for d in b.readers.values():
                if not skip(d):
                    self._wait(stream, d)

    def op(self, eng, fn, r=(), w=(), signal=True):
        self._deps(eng, r, w, False)
        inst = fn()
        self.n_ins += 1
        ins = Ins(eng)
        if signal:
            self.cnt[eng] += 1
            inst.then_inc(self.sem[eng], 1)
            ins.sem = self.sem[eng]
            ins.val = self.cnt[eng]
            for p in self.pending[eng]:
                p.sem = ins.sem
                p.val = ins.val
            self.pending[eng] = []
        else:
            self.pending[eng].append(ins)
        for b in r:
            b.readers[eng] = ins
        for b in w:
            b.writer = ins
            b.readers = {}
        return ins

    def dma(self, q, out, in_, chan, r=(), w=()):
        self._deps(q, r, w, True)
        if q == "pool":
            chan = Buf(chan.name + "_sw%d" % self.n_dsem)
            chan.sem = self.es.enter_context(self.nc.semaphore("w%d" % self.n_dsem))
            self.n_dsem += 1
            self.sw_bufs.append(chan)
        if chan.sem is None:
            if self.sem_pool:
                chan.sem, chan.semval = self.sem_pool.pop()
            else:
                chan.sem = self.es.enter_context(self.nc.semaphore("d%d" % self.n_dsem))
                self.n_dsem += 1
            self.dma_bufs.append(chan)
        chan.semval += 16
        self.E[q].dma_start(out=out, in_=in_).then_inc(chan.sem, 16)
        self.n_ins += 1
        ins = Ins("dma_" + chan.name, chan.sem, chan.semval, True)
        for b in r:
            b.readers[ins.key] = ins
        for b in w:
            b.writer = ins
            b.readers = {}
        return ins

    def finish(self):
        for b in self.dma_bufs + self.sw_bufs:
            self._wait("sp", Ins("x", b.sem, b.semval, True))
        for e in self.CE:
            if self.cnt[e]:
                self._wait("sp", Ins(e, self.sem[e], self.cnt[e]))


class K:
    pass


def _bufs(prefix, n):
    return [Buf("%s%d" % (prefix, i)) for i in range(n)]


def build(stop_after=None, dev=False):
    nc = bass.Bass("TRN2", target_bir_lowering=False)
    k = K()
    k.nc = nc
    k.dev = dev

    def din(name, shape, dt=F32):
        return nc.dram_tensor(name, list(shape), dt, kind="ExternalInput").ap()

    def dscr(name, shape, dt):
        return nc.dram_tensor(name, list(shape), dt, kind="ExternalOutput" if dev else "Internal").ap()

    I = {}
    I["x"] = din("x", [NTOK, D])
    I["c_col"] = din("c_col", [128, 16])
    I["ada_w"] = din("ada_w", [DEPTH, D, 6 * D])
    I["ada_b_bc"] = din("ada_b_bc", [DEPTH, 128, 6 * D])
    I["ng_bc"] = din("ng_bc", [5, 128, D])
    I["ffn_w_in"] = din("ffn_w_in", [DEPTH, D, 2 * FFN])
    I["ffn_w_out"] = din("ffn_w_out", [DEPTH, FFN, D])
    I["even_w_in"] = din("even_w_in", [D, 2560])
    I["even_w_out"] = din("even_w_out", [D, D])
    I["sgu_wT"] = din("sgu_wT", [128, 512])
    I["sgu_b_bc"] = din("sgu_b_bc", [128, 512])
    I["lqk_bc"] = din("lqk_bc", [128, 256])
    I["subln_col"] = din("subln_col", [128, 1])
    I["odd_w_qkv"] = din("odd_w_qkv", [D, 1536])
    I["odd_w_out"] = din("odd_w_out", [D, D])
    I["sink_col"] = din("sink_col", [128, 8])
    I["cosT"] = din("cosT", [128, SEQ])
    I["sinT"] = din("sinT", [128, SEQ])
    I["cbf"] = din("cbf", [128, 768])
    I["cf32"] = din("cf32", [128, 960])
    k.I = I
    k.out = nc.dram_tensor("out", [SEQ, D], F32, kind="ExternalOutput").ap()

    S = {}
    S["h"] = dscr("h_scr", [NTOK, D], F32)
    S["modbc"] = dscr("modbc_scr", [6, 128, D], F32)
    S["q"] = dscr("q_scr", [8, 128, NTOK], BF16)
    S["k"] = dscr("k_scr", [4, 128, NTOK], BF16)
    S["v"] = dscr("v_scr", [NTOK, 512], BF16)
    S["o"] = dscr("o_scr", [8, 128, NTOK], BF16)
    S["hm"] = dscr("hm_scr", [NF, 128, NTOK], BF16)
    if dev:
        S["dbg"] = dscr("dbg_scr", [128, 256], F32)
    k.S = S
    k.Bh = _bufs("Dh", NT)
    k.Bmod = _bufs("Dmod", 6)
    k.Bq = [[Buf("Dq%d_%d" % (c, b)) for b in range(9)] for c in range(8)]
    k.Bk = [[Buf("Dk%d_%d" % (c, b)) for b in range(9)] for c in range(4)]
    k.Bv = _bufs("Dv", NT)
    k.Bo = [[Buf("Do%d_%d" % (c, b)) for b in range(9)] for c in range(8)]
    k.Bhm = [[Buf("Dhm%d_%d" % (f, b)) for b in range(9)] for f in range(NF)]

    with contextlib.ExitStack() as es:
        sc = Sched(nc, es)
        k.sc = sc
        k.es = es
        k.marks = []
        k.marks = []
        phases = [phase_prologue, phase_A0, phase_B0, phase_C1_l0, phase_C2_l0,
                  phase_A1, phase_B1, phase_C1_l1, phase_C2_l1]
        for ph in phases:
            with contextlib.ExitStack() as pes:
                k.pes = pes
                ph(k)
            k.marks.append((ph.__name__, dict(sc.cnt)))
            k.marks.append((ph.__name__, dict(sc.cnt)))
            if stop_after is not None and ph.__name__ == stop_after:
                break
        sc.finish()
    k.n_ins = sc.n_ins
    k.n_wait = sc.n_wait
    return nc, k


def sbt(k, st, name, shape, dt):
    return st.enter_context(k.nc.sbuf_tensor(name, list(shape), dt))


def pst(k, st, name, shape, dt=F32):
    return st.enter_context(k.nc.psum_tensor(name, list(shape), dt))


def mm_group(k, out_ap, outB, pairs, rB, signal_last=True):
    sc, nc = k.sc, k.nc
    n = len(pairs)
    for i, (l, r) in enumerate(pairs):
        sc.op("pe", lambda: nc.tensor.matmul(out_ap, lhsT=l, rhs=r, start=(i == 0), stop=(i == n - 1)),
              r=rB, w=[outB], signal=(signal_last and i == n - 1))


def barrier(k):
    sc = k.sc
    for e in sc.CE:
        assert not sc.pending[e], "unsignalled tail on " + e
    for s in ("pe", "act", "dve", "pool", "sp"):
        for e in sc.CE:
            if sc.cnt[e]:
                sc._wait(s, Ins(e, sc.sem[e], sc.cnt[e]))
        for b in sc.dma_bufs + sc.sw_bufs:
            sc._wait(s, Ins("x", b.sem, b.semval, True))
    sc.sw_bufs = []
    for b in sc.dma_bufs:
        sc.sem_pool.append((b.sem, b.semval))
        b.sem = None
    sc.dma_bufs = []


def modidx(l, s, v):
    return ((l * 2 + s) * 4 + v) * 8


class NormCtx:
    def __init__(self, k, st, tag):
        self.junks = [sbt(k, st, tag + "junk%d" % i, [128, D], BF16) for i in range(2)]
        self.junkB = _bufs(tag + "junk", 2)
        self.nj = 0
        self.ss = sbt(k, st, tag + "ss", [128, 4], F32)
        self.sq = sbt(k, st, tag + "sq", [128, 4], F32)
        self.rs = sbt(k, st, tag + "rs", [128, 4], F32)
        self.ssB, self.sqB, self.rsB = Buf(tag + "ss"), Buf(tag + "sq"), Buf(tag + "rs")
        self.xn = [sbt(k, st, tag + "xn%d" % i, [128, D], BF16) for i in range(4)]
        self.xnB = _bufs(tag + "xn", 4)
        self.np_ = 0
        self.psT = [pst(k, st, tag + "psT%d" % i, [128, 8, 128], BF16) for i in range(2)]
        self.psTB = _bufs(tag + "psT", 2)
        self.n = 0


def norm_stats(k, nx, tiles):
    sc, nc = k.sc, k.nc
    nt = len(tiles)
    for i, (ap, B) in enumerate(tiles):
        jj = nx.nj % 2
        nx.nj += 1
        sc.op("act", lambda: nc.scalar.activation(out=nx.junks[jj][:], in_=ap, func=AF.Square,
                                                  accum_out=nx.ss[:, i:i + 1]), r=[B], w=[nx.ssB, nx.junkB[jj]])
    sc.op("act", lambda: nc.scalar.activation(out=nx.sq[:, 0:nt], in_=nx.ss[:, 0:nt], func=AF.Sqrt,
                                              scale=1.0 / D, bias=k.epsc), r=[nx.ssB, k.Bcf], w=[nx.sqB])
    sc.op("dve", lambda: nc.vector.reciprocal(out=nx.rs[:, 0:nt], in_=nx.sq[:, 0:nt]), r=[nx.sqB], w=[nx.rsB])


def norm_xn(k, nx, ap, B, i):
    sc, nc = k.sc, k.nc
    sc.op("dve", lambda: nc.vector.tensor_scalar(out=nx.xn[i][:], in0=ap, scalar1=nx.rs[:, i:i + 1], scalar2=None,
                                                 op0=ALU.mult), r=[B, nx.rsB], w=[nx.xnB[i]])


def norm_tr(k, nx, i, mbase, aT, aTB, col0, on_act=False):
    sc, nc = k.sc, k.nc
    r = nx.np_ % 2
    nx.np_ += 1
    for j in range(8):
        sc.op("pe", lambda: nc.tensor.transpose(nx.psT[r][:, j, :], nx.xn[i][:, j * 128:(j + 1) * 128], k.ident),
              r=[nx.xnB[i], k.Bc], w=[nx.psTB[r]], signal=(j == 7))
    for j in range(8):
        if on_act:
            sc.op("act", lambda: nc.scalar.activation(out=aT[:, j, col0:col0 + 128], in_=nx.psT[r][:, j, :],
                                                      func=AF.Identity, scale=k.modcol[:, mbase + j:mbase + j + 1],
                                                      bias=k.modcol[:, mbase + 8 + j:mbase + 9 + j]),
                  r=[nx.psTB[r], k.Bmodcol], w=[aTB])
            continue
        sc.op("dve", lambda: nc.vector.tensor_scalar(out=aT[:, j, col0:col0 + 128], in0=nx.psT[r][:, j, :],
                                                     scalar1=k.modcol[:, mbase + j:mbase + j + 1],
                                                     scalar2=k.modcol[:, mbase + 8 + j:mbase + 9 + j],
                                                     op0=ALU.mult, op1=ALU.add),
              r=[nx.psTB[r], k.Bmodcol], w=[aTB])


def norm_apply(k, nx, ap, B, i, mbase, aT, aTB, col0, on_act=False):
    norm_xn(k, nx, ap, B, i)
    norm_tr(k, nx, i, mbase, aT, aTB, col0, on_act)


class RopeCtx:
    def __init__(self, k, st, tag):
        self.qb = [sbt(k, st, tag + "qb%d" % i, [128, 512], BF16) for i in range(2)]
        self.qbB = _bufs(tag + "qb", 2)
        self.psR = pst(k, st, tag + "psR", [128, 512])
        self.psRB = Buf(tag + "psR")
        self.css = [sbt(k, st, tag + "cs%d" % i, [128, 512], F32) for i in range(2)]
        self.sns = [sbt(k, st, tag + "sn%d" % i, [128, 512], F32) for i in range(2)]
        self.csBs = _bufs(tag + "cs", 2)
        self.cs, self.sn, self.csB = self.css[0], self.sns[0], self.csBs[0]
        self.t1 = [sbt(k, st, tag + "t1%d" % i, [128, 512], F32) for i in range(2)]
        self.t2 = [sbt(k, st, tag + "t2%d" % i, [128, 512], F32) for i in range(2)]
        self.t1B, self.t2B = _bufs(tag + "t1", 2), _bufs(tag + "t2", 2)
        self.qst = [sbt(k, st, tag + "qst%d" % i, [128, 512], BF16) for i in range(3)]
        self.qstB = _bufs(tag + "qst", 3)
        self.n = 0
        self.m = 0


def rope_load(k, rc, tok0, N, par=0):
    k.sc.dma("sp", rc.css[par][:, 0:N], k.I["cosT"][:, tok0:tok0 + N], rc.csBs[par], w=[rc.csBs[par]])
    k.sc.dma("sp", rc.sns[par][:, 0:N], k.I["sinT"][:, tok0:tok0 + N], rc.csBs[par], w=[rc.csBs[par]])


def rope_select(rc, par):
    rc.cs, rc.sn, rc.csB = rc.css[par], rc.sns[par], rc.csBs[par]


def rope_store(k, rc, psF, psFB, N, rope, dst_ap, dstB):
    sc, nc = k.sc, k.nc
    s3 = rc.m % 3
    rc.m += 1
    if not rope:
        sc.op("act", lambda: nc.scalar.copy(out=rc.qst[s3][:, 0:N], in_=psF), r=[psFB], w=[rc.qstB[s3]])

        def part2():
            sc.dma("sp", dst_ap, rc.qst[s3][:, 0:N], rc.qstB[s3], r=[rc.qstB[s3]], w=[dstB])
        return part2
    r = rc.n % 2
    rc.n += 1
    sc.op("act", lambda: nc.scalar.copy(out=rc.qb[r][:, 0:N], in_=psF), r=[psFB], w=[rc.qbB[r]])
    sc.op("dve", lambda: nc.vector.tensor_tensor(out=rc.t1[r][:, 0:N], in0=psF, in1=rc.cs[:, 0:N], op=ALU.mult),
          r=[psFB, rc.csB, rc.qbB[r]], w=[rc.t1B[r]])

    def part2():
        sc.op("pe", lambda: nc.tensor.matmul(rc.psR[:, 0:N], lhsT=k.rperm, rhs=rc.qb[r][:, 0:N], start=True, stop=True),
              r=[rc.qbB[r], k.Bc], w=[rc.psRB])
        sc.op("dve", lambda: nc.vector.tensor_tensor(out=rc.t2[r][:, 0:N], in0=rc.psR[:, 0:N], in1=rc.sn[:, 0:N],
                                                     op=ALU.mult), r=[rc.psRB, rc.csB], w=[rc.t2B[r]])
        sc.op("pool", lambda: nc.gpsimd.tensor_tensor(out=rc.qst[s3][:, 0:N], in0=rc.t1[r][:, 0:N],
                                                      in1=rc.t2[r][:, 0:N], op=ALU.add),
              r=[rc.t1B[r], rc.t2B[r]], w=[rc.qstB[s3]])
        sc.dma("sp", dst_ap, rc.qst[s3][:, 0:N], rc.qstB[s3], r=[rc.qstB[s3]], w=[dstB])
    return part2


def load_w(k, dst, src, B, pieces=1):
    v = src.rearrange("(kc p) n -> p kc n", p=128)
    kc = v.shape[1]
    step = (kc + pieces - 1) // pieces
    for a in range(0, kc, step):
        b = min(kc, a + step)
        k.sc.dma("pool", dst[:, a:b, :], v[:, a:b, :], B, w=[B])


def phase_prologue(k):
    nc, sc, es, st, I = k.nc, k.sc, k.es, k.pes, k.I
    cbf = sbt(k, es, "sb_cbf", [128, 768], BF16)
    cf32 = sbt(k, es, "sb_cf32", [128, 960], F32)
    k.Bc, k.Bcf = Buf("cbf"), Buf("cf32")
    sc.dma("pool", cbf[:], I["cbf"], k.Bc, w=[k.Bc])
    sc.dma("sp", cf32[:], I["cf32"], k.Bcf, w=[k.Bcf])
    k.ident, k.ones, k.rperm = cbf[:, 0:128], cbf[:, 128:256], cbf[:, 256:384]
    k.mprev, k.mnext = cbf[:, 384:512], cbf[:, 512:640]
    k.identF4, k.onesF, k.epsc = cf32[:, 0:512], cf32[:, 512:640], cf32[:, 640:641]
    k.sel = [cf32[0:64, 704:832], cf32[0:64, 832:960]]
    k.modcol = sbt(k, es, "modcol", [128, 128], F32)
    k.Bmodcol = Buf("modcol")

    ccol = sbt(k, st, "p_ccol", [128, 16], F32)
    scol = sbt(k, st, "p_scol", [128, 16], F32)
    ccolB, scolB = Buf("p_ccol"), Buf("p_scol")
    sc.dma("sp", ccol[:], I["c_col"], ccolB, w=[ccolB])
    sc.op("act", lambda: nc.scalar.activation(out=scol[:], in_=ccol[:], func=AF.Silu), r=[ccolB], w=[scolB])
    L = sbt(k, st, "p_L", [128, 16, 128], F32)
    LB = Buf("p_L")
    for j in range(16):
        sc.op("dve", lambda: nc.vector.tensor_scalar(out=L[:, j, :], in0=k.onesF,
                                                     scalar1=scol[:, j:j + 1], scalar2=None, op0=ALU.mult),
              r=[scolB, k.Bcf], w=[LB])
    stage = [sbt(k, st, "p_stage%d" % i, [128, 8, 512], F32) for i in range(2)]
    stageB = _bufs("p_stage", 2)
    bias = [sbt(k, st, "p_bias%d" % i, [128, 512], F32) for i in range(2)]
    biasB = _bufs("p_bias", 2)
    ngt = sbt(k, st, "p_ngt", [128, D], F32)
    ngtB = Buf("p_ngt")
    psm = [pst(k, st, "p_ps%d" % i, [128, 512]) for i in range(2)]
    psmB = _bufs("p_ps", 2)
    tA = [sbt(k, st, "p_tA%d" % i, [128, 512], F32) for i in range(2)]
    tAB = _bufs("p_tA", 2)
    tB_ = [sbt(k, st, "p_tB%d" % i, [128, 512], F32) for i in range(2)]
    tBB = _bufs("p_tB", 2)
    tC = [sbt(k, st, "p_tC%d" % i, [128, 4, 128], F32) for i in range(2)]
    tCB = _bufs("p_tC", 2)
    gbc = [sbt(k, st, "p_gbc%d" % i, [128, D], F32) for i in range(2)]
    gbcB = _bufs("p_gbc", 2)
    ngrow = {(0, 1): 0, (0, 4): 1, (1, 1): 2, (1, 4): 3}
    cnt = 0
    gcnt = 0
    for l in range(DEPTH):
        for n in range(12):
            v, half = n // 2, n % 2
            r = cnt % 2
            cnt += 1
            sc.dma("sp", stage[r][:], I["ada_w"][l].rearrange("(kc p) n -> p kc n", p=128)[:, :, n * 512:(n + 1) * 512],
                   stageB[r], w=[stageB[r]])
            sc.dma("sp", bias[r][:], I["ada_b_bc"][l][:, n * 512:(n + 1) * 512], biasB[r], w=[biasB[r]])
            if v in (1, 4) and half == 0:
                sc.dma("sp", ngt[:], I["ng_bc"][ngrow[(l, v)]], ngtB, w=[ngtB])
            for s in (0, 1):
                if s == 1 and l == 1 and v >= 2:
                    continue
                mm_group(k, psm[s][:], psmB[s], [(L[:, s * 8 + kc, :], stage[r][:, kc, :]) for kc in range(8)],
                         [LB, stageB[r]])
                if v in (2, 5):
                    sc.op("dve", lambda: nc.vector.tensor_tensor(out=gbc[s][:, half * 512:(half + 1) * 512], in0=psm[s][:],
                                                                 in1=bias[r][:], op=ALU.add),
                          r=[psmB[s], biasB[r]], w=[gbcB[s]])
                    if half == 1:
                        mi = {(0, 0, 2): 0, (0, 0, 5): 1, (0, 1, 2): 2, (0, 1, 5): 3, (1, 0, 2): 4, (1, 0, 5): 5}[(l, s, v)]
                        sc.dma("sp", k.S["modbc"][mi], gbc[s][:], gbcB[s], r=[gbcB[s]], w=[k.Bmod[mi]])
                    continue
                sc.op("dve", lambda: nc.vector.tensor_tensor(out=tA[s][:], in0=psm[s][:], in1=bias[r][:], op=ALU.add),
                      r=[psmB[s], biasB[r]], w=[tAB[s]])
                src, srcB = tA[s], tAB[s]
                if v in (1, 4):
                    sc.op("dve", lambda: nc.vector.scalar_tensor_tensor(out=tB_[s][:], in0=tA[s][:], scalar=1.0,
                                                                         in1=ngt[:, half * 512:(half + 1) * 512],
                                                                         op0=ALU.add, op1=ALU.mult),
                          r=[tAB[s], ngtB], w=[tBB[s]])
                    src, srcB = tB_[s], tBB[s]
                sc.op("pool", lambda: nc.gpsimd.tensor_tensor(out=tC[s][:].rearrange("p a b -> p (a b)"), in0=src[:],
                                                              in1=k.identF4, op=ALU.mult),
                      r=[srcB, k.Bcf], w=[tCB[s]])
                vi = {0: 1, 1: 0, 3: 3, 4: 2}[v]
                mb = modidx(l, s, vi) + half * 4
                sc.op("dve", lambda: nc.vector.tensor_reduce(out=k.modcol[:, mb:mb + 4], in_=tC[s][:], axis=AX.X,
                                                             op=ALU.add),
                      r=[tCB[s]], w=[k.Bmodcol])
    if k.dev:
        sc.dma("sp", k.S["dbg"][:, 0:112], k.modcol[:, 0:112], k.Bmodcol, r=[k.Bmodcol])
    barrier(k)


def block_tiles(tb):
    if tb < 8:
        return tb * 4, 4, 512, False
    return 32, 2, 256, True


def phase_A0(k):
    nc, sc, st, I, S = k.nc, k.sc, k.pes, k.I, k.S
    Win = sbt(k, st, "a0_Win", [128, 8, 2560], BF16)
    WinB = _bufs("a0_Win", 5)
    wv = I["even_w_in"].rearrange("(kc p) n -> p kc n", p=128)
    for g in (1, 0, 2, 3, 4):
        sc.dma("pool", Win[:, :, g * 512:(g + 1) * 512], wv[:, :, g * 512:(g + 1) * 512], WinB[g], w=[WinB[g]])
    WoA = sbt(k, st, "a0_WoA", [128, 4, D], BF16)
    WoAB = Buf("a0_WoA")
    load_w(k, WoA, I["even_w_out"][0:512, :], WoAB)
    wsT = sbt(k, st, "a0_wsT", [128, 512], BF16)
    wsTB = Buf("a0_wsT")
    sc.dma("pool", wsT[:], I["sgu_wT"], wsTB, w=[wsTB])
    bsbc = sbt(k, st, "a0_bsbc", [128, 512], F32)
    bsB = Buf("a0_bsbc")
    sc.dma("sp", bsbc[:], I["sgu_b_bc"], bsB, w=[bsB])
    gbc = [sbt(k, st, "a0_gbc%d" % i, [128, D], F32) for i in range(2)]
    gbcB = _bufs("a0_gbc", 2)
    sc.dma("sp", gbc[0][:], S["modbc"][0], gbcB[0], r=[k.Bmod[0]], w=[gbcB[0]])
    sc.dma("sp", gbc[1][:], S["modbc"][2], gbcB[1], r=[k.Bmod[2]], w=[gbcB[1]])

    xt = [sbt(k, st, "a0_xt%d" % i, [128, D], F32) for i in range(8)]
    xtB = _bufs("a0_xt", 8)
    nx = NormCtx(k, st, "a0_")
    rc = RopeCtx(k, st, "a0_")
    aT2 = [sbt(k, st, "a0_aT%d" % i, [128, 8, 512], BF16) for i in range(2)]
    aT2B = _bufs("a0_aT", 2)
    guT = sbt(k, st, "a0_guT", [128, 4, 512], BF16)
    guB = Buf("a0_guT")
    vg = [sbt(k, st, "a0_vg%d" % i, [128, 4, 128], F32) for i in range(4)]
    vgB = _bufs("a0_vg", 4)
    lsum = [sbt(k, st, "a0_lsum%d" % i, [128, 4], F32) for i in range(4)]
    lnm = [sbt(k, st, "a0_lnm%d" % i, [128, 4], F32) for i in range(4)]
    lsq = [sbt(k, st, "a0_lsq%d" % i, [128, 4], F32) for i in range(4)]
    lsr = [sbt(k, st, "a0_lsr%d" % i, [128, 4], F32) for i in range(4)]
    lrs = [sbt(k, st, "a0_lrs%d" % i, [128, 4], F32) for i in range(4)]
    lsumB, lnmB, lsqB, lsrB, lrsB = (_bufs("a0_l%s" % n, 4) for n in "abcde")
    vf = [sbt(k, st, "a0_vf%d" % i, [128, 512], BF16) for i in range(4)]
    vfB = _bufs("a0_vf", 4)
    mx = [sbt(k, st, "a0_mx%d" % i, [128, 4, 128], F32) for i in range(2)]
    mxB = _bufs("a0_mx", 2)
    AoT = sbt(k, st, "a0_AoT", [128, 4, 512], BF16)
    AoB = _bufs("a0_AoT", 4)
    yt = [sbt(k, st, "a0_yt%d" % i, [128, 512], F32) for i in range(2)]
    ytB = _bufs("a0_yt", 2)
    vst = [sbt(k, st, "a0_vst%d" % i, [128, 512], BF16) for i in range(2)]
    vstB = _bufs("a0_vst", 2)
    psS = pst(k, st, "a0_psS", [128, 4, 128])
    psSB = Buf("a0_psS")
    psK = [pst(k, st, "a0_psK%d" % i, [128, 512]) for i in range(2)]
    psKB = _bufs("a0_psK", 2)
    psF = [pst(k, st, "a0_psF%d" % i, [128, 512]) for i in range(2)]
    psFB = _bufs("a0_psF", 2)
    cnt = {"K": 0, "F": 0}
    NB = int(os.environ.get("A0_BLOCKS", "9"))

    def xs(tb, i):
        return (tb % 2) * 4 + i

    def stage_load(tb):
        t0, ntl, N, isctx = block_tiles(tb)
        for i in range(ntl):
            j = xs(tb, i)
            sc.dma("sp", xt[j][:], I["x"][(t0 + i) * 128:(t0 + i + 1) * 128, :], xtB[j], w=[xtB[j]])
        if not isctx:
            rope_load(k, rc, t0 * 128, N, tb % 2)

    def stage_norm(tb):
        t0, ntl, N, isctx = block_tiles(tb)
        s = 1 if isctx else 0
        norm_stats(k, nx, [(xt[xs(tb, i)][:], xtB[xs(tb, i)]) for i in range(ntl)])
        for i in range(ntl):
            j = xs(tb, i)
            norm_xn(k, nx, xt[j][:], xtB[j], i)

    def stage_norm2(tb):
        t0, ntl, N, isctx = block_tiles(tb)
        s = 1 if isctx else 0
        for i in range(ntl):
            norm_tr(k, nx, i, modidx(0, s, 0), aT2[tb % 2], aT2B[tb % 2], i * 128)

    def stage_proj(tb):
        t0, ntl, N, isctx = block_tiles(tb)
        tok0 = t0 * 128
        aT, aTB = aT2[tb % 2], aT2B[tb % 2]
        rope_select(rc, tb % 2)
        if tb + 1 < NB:
            stage_load(tb + 1)
        for i in range(ntl):
            kk = cnt["K"] % 2
            cnt["K"] += 1
            r = i
            mm_group(k, psK[kk][:], psKB[kk], [(aT[:, kc, i * 128:(i + 1) * 128], Win[:, kc, 512:1024]) for kc in range(8)],
                     [WinB[1], aTB])
            sc.op("act", lambda: nc.scalar.activation(out=vg[r][:].rearrange("p a b -> p (a b)"), in_=psK[kk][:], func=AF.Gelu),
                  r=[psKB[kk]], w=[vgB[r]])
            sc.op("dve", lambda: nc.vector.tensor_reduce(out=lsum[r][:], in_=vg[r][:], axis=AX.X, op=ALU.add),
                  r=[vgB[r]], w=[lsumB[r]])
            sc.op("dve", lambda: nc.vector.tensor_scalar(out=lnm[r][:], in0=lsum[r][:], scalar1=-1.0 / 128, scalar2=None,
                                                         op0=ALU.mult), r=[lsumB[r]], w=[lnmB[r]])
        for i in range(ntl):
            r = i
            for g in range(4):
                jj = nx.nj % 2
                nx.nj += 1
                sc.op("act", lambda: nc.scalar.activation(out=nx.junks[jj][:, 0:128], in_=vg[r][:, g, :], func=AF.Square,
                                                          bias=lnm[r][:, g:g + 1], accum_out=lsq[r][:, g:g + 1]),
                      r=[vgB[r], lnmB[r]], w=[lsqB[r], nx.junkB[jj]])
        for g in range(4):
            f = cnt["F"] % 2
            cnt["F"] += 1
            mm_group(k, psF[f][:, 0:N], psFB[f], [(Win[:, kc, g * 128:(g + 1) * 128], aT[:, kc, 0:N]) for kc in range(8)],
                     [WinB[0], aTB])
            sc.op("act", lambda: nc.scalar.activation(out=guT[:, g, 0:N], in_=psF[f][:, 0:N], func=AF.Gelu),
                  r=[psFB[f]], w=[guB])
        if tb + 1 < NB:
            stage_norm(tb + 1)
        for i in range(ntl):
            r = i
            sc.op("act", lambda: nc.scalar.activation(out=lsr[r][:], in_=lsq[r][:], func=AF.Sqrt, scale=1.0 / 128,
                                                      bias=k.epsc), r=[lsqB[r], k.Bcf], w=[lsrB[r]])
            sc.op("dve", lambda: nc.vector.reciprocal(out=lrs[r][:], in_=lsr[r][:]), r=[lsrB[r]], w=[lrsB[r]])
            for g in range(4):
                sc.op("dve", lambda: nc.vector.tensor_scalar(out=vf[r][:, g * 128:(g + 1) * 128], in0=vg[r][:, g, :],
                                                             scalar1=lnm[r][:, g:g + 1], scalar2=lrs[r][:, g:g + 1],
                                                             op0=ALU.add, op1=ALU.mult),
                      r=[vgB[r], lnmB[r], lrsB[r]], w=[vfB[r]])
        pend = None
        for c in range(8):
            f = cnt["F"] % 2
            cnt["F"] += 1
            mm_group(k, psF[f][:, 0:N], psFB[f],
                     [(Win[:, kc, 1024 + c * 128:1024 + (c + 1) * 128], aT[:, kc, 0:N]) for kc in range(8)],
                     [WinB[2 + c // 4], aTB])
            if pend is not None:
                pend()
            if c < 4:
                pend = rope_store(k, rc, psF[f][:, 0:N], psFB[f], N, not isctx, S["q"][c][:, tok0:tok0 + N], k.Bq[c][tb])
            else:
                pend = rope_store(k, rc, psF[f][:, 0:N], psFB[f], N, not isctx, S["k"][c - 4][:, tok0:tok0 + N], k.Bk[c - 4][tb])
            if c == 4 and tb + 1 < NB:
                stage_norm2(tb + 1)
        for i in range(ntl):
            kk = cnt["K"] % 2
            cnt["K"] += 1
            mm_group(k, psK[kk][:], psKB[kk], [(aT[:, kc, i * 128:(i + 1) * 128], Win[:, kc, 2048:2560]) for kc in range(8)],
                     [WinB[4], aTB])
            if pend is not None:
                pend()
                pend = None
            v2 = (t0 + i) % 2
            sc.op("dve", lambda: nc.vector.tensor_copy(out=vst[v2][:], in_=psK[kk][:]), r=[psKB[kk]], w=[vstB[v2]])
            sc.dma("sp", S["v"][(t0 + i) * 128:(t0 + i + 1) * 128, :], vst[v2][:], vstB[v2], r=[vstB[v2]], w=[k.Bv[t0 + i]])

    def stage_tail(tb):
        t0, ntl, N, isctx = block_tiles(tb)
        s = 1 if isctx else 0
        for i in range(ntl):
            r = i
            m2 = i % 2
            j = xs(tb, i)
            for g in range(4):
                sc.op("pe", lambda: nc.tensor.matmul(psS[:, g, :], lhsT=vf[r][:, g * 128:(g + 1) * 128],
                                                     rhs=wsT[:, g * 128:(g + 1) * 128], start=True, stop=True),
                      r=[vfB[r], wsTB], w=[psSB], signal=(g == 3))
            sc.op("dve", lambda: nc.vector.tensor_tensor(out=mx[m2][:].rearrange("p a b -> p (a b)"),
                                                         in0=psS[:].rearrange("p a b -> p (a b)"), in1=bsbc[:], op=ALU.add),
                  r=[psSB, bsB], w=[mxB[m2]])
            sc.op("pool", lambda: nc.gpsimd.tensor_tensor(out=AoT[:, :, i * 128:(i + 1) * 128], in0=mx[m2][:],
                                                          in1=guT[:, :, i * 128:(i + 1) * 128], op=ALU.mult),
                  r=[mxB[m2], guB], w=[AoB[i]])
        for i in range(ntl):
            j = xs(tb, i)
            for nh in range(2):
                kk = cnt["K"] % 2
                cnt["K"] += 1
                mm_group(k, psK[kk][:], psKB[kk],
                         [(AoT[:, g, i * 128:(i + 1) * 128], WoA[:, g, nh * 512:(nh + 1) * 512]) for g in range(4)],
                         [AoB[i], WoAB])
                sc.op("dve", lambda: nc.vector.tensor_tensor(out=yt[nh][:], in0=psK[kk][:],
                                                             in1=gbc[s][:, nh * 512:(nh + 1) * 512], op=ALU.mult),
                      r=[psKB[kk], gbcB[s]], w=[ytB[nh]])
                sc.op("pool", lambda: nc.gpsimd.tensor_tensor(out=xt[j][:, nh * 512:(nh + 1) * 512], in0=yt[nh][:],
                                                              in1=xt[j][:, nh * 512:(nh + 1) * 512], op=ALU.add),
                      r=[ytB[nh], xtB[j]], w=[xtB[j]])
            sc.dma("sp", S["h"][(t0 + i) * 128:(t0 + i + 1) * 128, :], xt[j][:], xtB[j], r=[xtB[j]], w=[k.Bh[t0 + i]])

    stage_load(0)
    stage_norm(0)
    stage_norm2(0)
    for tb in range(NB):
        stage_proj(tb)
        stage_tail(tb)
    barrier(k)


def phase_B0(k):
    nc, sc, st, I, S = k.nc, k.sc, k.pes, k.I, k.S
    kT = sbt(k, st, "b0_kT", [128, 4, NTOK], BF16)
    kTB = _bufs("b0_kT", 4)
    for h in range(4):
        sc.dma("sp", kT[:, h, :], S["k"][h], kTB[h], r=[k.Bk[h][b] for b in range(9)], w=[kTB[h]])
    vA = sbt(k, st, "b0_vA", [128, NT, 512], BF16)
    vAB = Buf("b0_vA")
    vv = S["v"].rearrange("(t p) e -> p t e", p=128)
    for a, b in ((0, 17), (17, 34)):
        sc.dma("sp", vA[:, a:b, :], vv[:, a:b, :], vAB, r=[k.Bv[t] for t in range(a, b)], w=[vAB])
    lq = sbt(k, st, "b0_lq", [128, 256], F32)
    lqB = Buf("b0_lq")
    sc.dma("sp", lq[:], I["lqk_bc"], lqB, w=[lqB])
    sg = sbt(k, st, "b0_sg", [128, 1], F32)
    sgB = Buf("b0_sg")
    sc.dma("sp", sg[:], I["subln_col"], sgB, w=[sgB])
    prod = sbt(k, st, "b0_prod", [128, 2, 64], F32)
    s12 = sbt(k, st, "b0_s12", [128, 2], F32)
    e12 = sbt(k, st, "b0_e12", [128, 2], F32)
    dl = sbt(k, st, "b0_dl", [128, 1], F32)
    nl = sbt(k, st, "b0_nl", [128, 1], F32)
    gl = sbt(k, st, "b0_gl", [128, 1], F32)
    prodB, s12B, e12B, dlB, nlB, glB = (Buf("b0_" + n) for n in ("prod", "s12", "e12", "dl", "nl", "gl"))
    for j in range(2):
        sc.op("dve", lambda: nc.vector.tensor_tensor(out=prod[:, j, :], in0=lq[:, j * 128:j * 128 + 64],
                                                     in1=lq[:, j * 128 + 64:j * 128 + 128], op=ALU.mult),
              r=[lqB], w=[prodB])
    sc.op("dve", lambda: nc.vector.tensor_reduce(out=s12[:], in_=prod[:], axis=AX.X, op=ALU.add), r=[prodB], w=[s12B])
    sc.op("act", lambda: nc.scalar.activation(out=e12[:], in_=s12[:], func=AF.Exp), r=[s12B], w=[e12B])
    sc.op("dve", lambda: nc.vector.tensor_tensor(out=dl[:], in0=e12[:, 1:2], in1=e12[:, 0:1], op=ALU.subtract),
          r=[e12B], w=[dlB])
    sc.op("dve", lambda: nc.vector.tensor_scalar(out=nl[:], in0=dl[:], scalar1=-LAM_INIT0, scalar2=None, op0=ALU.add),
          r=[dlB], w=[nlB])
    sc.op("dve", lambda: nc.vector.tensor_scalar(out=gl[:], in0=sg[:], scalar1=1.0 - LAM_INIT0, scalar2=None, op0=ALU.mult),
          r=[sgB], w=[glB])

    psS = [pst(k, st, "b0_psS%d" % i, [128, 2, 512]) for i in range(2)]
    psSB = _bufs("b0_psS", 2)
    psO = [pst(k, st, "b0_psO%d" % i, [128, 512]) for i in range(2)]
    psOB = _bufs("b0_psO", 2)
    psL = [pst(k, st, "b0_psL%d" % i, [128, 512]) for i in range(2)]
    psLB = _bufs("b0_psL", 2)
    E = [sbt(k, st, "b0_E%d" % i, [128, 2, 512], BF16) for i in range(3)]
    EB = _bufs("b0_E", 3)
    qblk = [sbt(k, st, "b0_q%d" % i, [128, 512], BF16) for i in range(2)]
    qblkB = _bufs("b0_q", 2)
    Lsb = sbt(k, st, "b0_Lsb", [128, 512], F32)
    LsbB = Buf("b0_Lsb")
    R = [sbt(k, st, "b0_R%d" % i, [128, 512], F32) for i in range(2)]
    RB = _bufs("b0_R", 2)
    T = [sbt(k, st, "b0_T%d" % i, [128, 512], F32) for i in range(2)]
    TB = _bufs("b0_T", 2)
    ost2 = [sbt(k, st, "b0_ost%d" % i, [128, NTOK], F32) for i in range(2)]
    ostB2 = [_bufs("b0_ost%d_" % i, 9) for i in range(2)]
    sqs2 = [sbt(k, st, "b0_sqs%d" % i, [128, NTOK], BF16) for i in range(2)]
    sqsB2 = [_bufs("b0_sqs%d_" % i, 9) for i in range(2)]
    pend_p2 = []
    SD = [sbt(k, st, "b0_SD%d" % i, [128, 512], F32) for i in range(2)]
    SDB = _bufs("b0_SD", 2)
    RS = [sbt(k, st, "b0_RS%d" % i, [128, 512], F32) for i in range(2)]
    RSB = _bufs("b0_RS", 2)
    osg = [sbt(k, st, "b0_osg%d" % i, [128, 512], BF16) for i in range(2)]
    osgB = _bufs("b0_osg", 2)
    nH = int(os.environ.get("B0_HEADS", "4"))
    nQ = int(os.environ.get("B0_QB", "9"))
    its = [(h, qb) for h in range(nH) for qb in range(9 - nQ, 9)]
    cS = cE = 0
    pend_fin = []

    def qinfo(qb):
        return (512, list(range(NT))) if qb < 8 else (256, [32, 33])

    def load_q(n):
        h, qb = its[n]
        N, _ = qinfo(qb)
        sc.dma("sp", qblk[n % 2][:, 0:N], S["q"][h][:, qb * 512:qb * 512 + N], qblkB[n % 2], r=[k.Bq[h][qb]],
               w=[qblkB[n % 2]])

    load_q(0)
    for n, (h, qb) in enumerate(its):
        N, tiles = qinfo(qb)
        q0 = qb * 512
        qq, qqB = qblk[n % 2], qblkB[n % 2]
        ost, ostB, sqs, sqsB = ost2[h % 2], ostB2[h % 2], sqs2[h % 2], sqsB2[h % 2]
        if n + 1 < len(its):
            load_q(n + 1)
        nt = len(tiles)
        sl = {}

        def emit_S(i):
            nonlocal cS
            s = cS % 2
            cS += 1
            sl[i] = s
            kt = tiles[i]
            for j in range(2):
                sc.op("pe", lambda: nc.tensor.matmul(psS[s][:, j, 0:N], lhsT=kT[j * 64:(j + 1) * 64, h, kt * 128:(kt + 1) * 128],
                                                     rhs=qq[j * 64:(j + 1) * 64, 0:N], start=True, stop=True),
                      r=[kTB[h], qqB], w=[psSB[s]], signal=(j == 1))

        emit_S(0)
        for i in range(nt):
            if i + 1 < nt:
                emit_S(i + 1)
            if pend_fin and (i == 3 or i == nt - 1):
                pend_fin.pop()()
            if pend_p2 and qb < 8 and (i == 12 or (i == 24 and len(pend_p2) > 8 - qb)):
                pend_p2.pop(0)()
            s = sl[i]
            e = cE % 3
            cE += 1
            kt = tiles[i]
            sc.op("act", lambda: nc.scalar.activation(out=E[e][:, :, 0:N], in_=psS[s][:, :, 0:N], func=AF.Exp, scale=SCALE),
                  r=[psSB[s]], w=[EB[e]])
            for j in range(2):
                sc.op("pe", lambda: nc.tensor.matmul(psO[j][:, 0:N], lhsT=vA[:, kt, h * 128:(h + 1) * 128], rhs=E[e][:, j, 0:N],
                                                     start=(i == 0), stop=(i == nt - 1)),
                      r=[vAB, EB[e]], w=[psOB[j]], signal=False)
            for j in range(2):
                sc.op("pe", lambda: nc.tensor.matmul(psL[0][j * 32:(j + 1) * 32, 0:N], lhsT=k.ones[:, 0:32], rhs=E[e][:, j, 0:N],
                                                     start=(i == 0), stop=(i == nt - 1), skip_group_check=True),
                      r=[k.Bc, EB[e]], w=[psLB[0]], signal=(j == 1))
        for j in range(2):
            sc.op("dve", lambda: nc.vector.tensor_copy(out=T[j][:, 0:N], in_=psO[j][:, 0:N]), r=[psOB[j]], w=[TB[j]])
        sc.op("act", lambda: nc.scalar.copy(out=Lsb[0:64, 0:N], in_=psL[0][0:64, 0:N]), r=[psLB[0]], w=[LsbB])

        def fin(N=N, q0=q0, qb=qb, ost=ost, ostB=ostB, sqs=sqs, sqsB=sqsB):
            sc.op("dve", lambda: nc.vector.reciprocal(out=R[0][0:64, 0:N], in_=Lsb[0:64, 0:N]), r=[LsbB], w=[RB[0]])
            for j in (1, 0):
                sc.op("pe", lambda: nc.tensor.matmul(psL[1][:, 0:N], lhsT=k.sel[j], rhs=R[0][0:64, 0:N], start=True, stop=True),
                      r=[k.Bcf, RB[0]], w=[psLB[1]])
                sc.op("dve", lambda: nc.vector.tensor_tensor(out=T[j][:, 0:N], in0=T[j][:, 0:N], in1=psL[1][:, 0:N], op=ALU.mult),
                      r=[psLB[1], TB[j]], w=[TB[j]])
            sc.op("dve", lambda: nc.vector.scalar_tensor_tensor(out=ost[:, q0:q0 + N], in0=T[1][:, 0:N], scalar=nl[:, 0:1],
                                                                in1=T[0][:, 0:N], op0=ALU.mult, op1=ALU.add),
                  r=[TB[0], TB[1], nlB], w=[ostB[qb]])
            sc.op("pool", lambda: nc.gpsimd.tensor_tensor(out=sqs[:, q0:q0 + N], in0=ost[:, q0:q0 + N], in1=ost[:, q0:q0 + N],
                                                          op=ALU.mult), r=[ostB[qb]], w=[sqsB[qb]])
        pend_fin.append(fin)
        if qb == 8 or n + 1 == len(its):
            pend_fin.pop()()
        if qb == 8:
            for q2 in range(9 - nQ, 9):
                def unit(q2=q2, h=h, sqs=sqs, sqsB=sqsB):
                    N2, _ = qinfo(q2)
                    p0 = q2 * 512
                    x2 = q2 % 2
                    sc.op("pe", lambda: nc.tensor.matmul(psL[1][:, 0:N2], lhsT=k.ones, rhs=sqs[:, p0:p0 + N2], start=True, stop=True),
                          r=[k.Bc, sqsB[q2]], w=[psLB[1]])
                    sc.op("act", lambda: nc.scalar.activation(out=SD[x2][:, 0:N2], in_=psL[1][:, 0:N2], func=AF.Ln,
                                                              scale=1.0 / 128, bias=k.epsc), r=[psLB[1], k.Bcf], w=[SDB[x2]])
                    sc.op("act", lambda: nc.scalar.activation(out=RS[x2][:, 0:N2], in_=SD[x2][:, 0:N2], func=AF.Exp, scale=-0.5),
                          r=[SDB[x2]], w=[RSB[x2]])
                    sc.op("dve", lambda: nc.vector.scalar_tensor_tensor(out=osg[x2][:, 0:N2], in0=ost2[h % 2][:, p0:p0 + N2],
                                                                        scalar=gl[:, 0:1], in1=RS[x2][:, 0:N2],
                                                                        op0=ALU.mult, op1=ALU.mult),
                          r=[ostB2[h % 2][q2], glB, RSB[x2]], w=[osgB[x2]])
                    sc.dma("sp", S["o"][h][:, p0:p0 + N2], osg[x2][:, 0:N2], osgB[x2], r=[osgB[x2]], w=[k.Bo[h][q2]])
                pend_p2.append(unit)
    while pend_p2:
        pend_p2.pop(0)()
    barrier(k)


def phase_C1(k, l):
    nc, sc, st, I, S = k.nc, k.sc, k.pes, k.I, k.S
    tg = "c1%d_" % l
    nk = 4 if l == 0 else 8
    nblk = 9 if l == 0 else 8
    Wo = sbt(k, st, tg + "Wo", [128, nk, D], BF16)
    WoB = Buf(tg + "Wo")
    load_w(k, Wo, I["even_w_out"][512:1024, :] if l == 0 else I["odd_w_out"], WoB)
    Wfi = sbt(k, st, tg + "Wfi", [128, 8, 2 * FFN], BF16)
    WfiB = _bufs(tg + "Wfi", 11)
    wv = I["ffn_w_in"][l].rearrange("(kc p) n -> p kc n", p=128)
    for g in range(11):
        sc.dma("pool", Wfi[:, :, g * 512:(g + 1) * 512], wv[:, :, g * 512:(g + 1) * 512], WfiB[g], w=[WfiB[g]])
    gbc = [sbt(k, st, tg + "gbc%d" % i, [128, D], F32) for i in range(2)]
    gbcB = _bufs(tg + "gbc", 2)
    mi = 0 if l == 0 else 4
    sc.dma("sp", gbc[0][:], S["modbc"][mi], gbcB[0], r=[k.Bmod[mi]], w=[gbcB[0]])
    if l == 0:
        sc.dma("sp", gbc[1][:], S["modbc"][2], gbcB[1], r=[k.Bmod[2]], w=[gbcB[1]])
    ht = [sbt(k, st, tg + "ht%d" % i, [128, D], F32) for i in range(4)]
    htB = _bufs(tg + "ht", 4)
    ob = sbt(k, st, tg + "ob", [128, nk, 512], BF16)
    obB = Buf(tg + "ob")
    nx = NormCtx(k, st, tg)
    aT2 = [sbt(k, st, tg + "aT%d" % i, [128, 8, 512], BF16) for i in range(2)]
    aT2B = _bufs(tg + "aT", 2)
    yt = [sbt(k, st, tg + "yt%d" % i, [128, 512], F32) for i in range(2)]
    ytB = _bufs(tg + "yt", 2)
    sg = [sbt(k, st, tg + "sg%d" % i, [128, 512], F32) for i in range(2)]
    sgB = _bufs(tg + "sg", 2)
    hst = [sbt(k, st, tg + "hst%d" % i, [128, 512], BF16) for i in range(3)]
    hstB = _bufs(tg + "hst", 3)
    psK = [pst(k, st, tg + "psK%d" % i, [128, 512]) for i in range(2)]
    psKB = _bufs(tg + "psK", 2)
    psG = [pst(k, st, tg + "psG%d" % i, [128, 512]) for i in range(2)]
    psGB = _bufs(tg + "psG", 2)
    psU = [pst(k, st, tg + "psU%d" % i, [128, 512]) for i in range(2)]
    psUB = _bufs(tg + "psU", 2)
    ov = S["o"][0:nk].rearrange("c p n -> p c n")
    cnt = {"K": 0, "G": 0, "H": 0}

    def preA(tb):
        t0, ntl, N, isctx = block_tiles(tb)
        tok0 = t0 * 128
        s = 1 if isctx else 0
        for i in range(ntl):
            sc.dma("sp", ht[i][:], S["h"][(t0 + i) * 128:(t0 + i + 1) * 128, :], htB[i], r=[k.Bh[t0 + i]], w=[htB[i]])
        sc.dma("sp", ob[:, :, 0:N], ov[:, :, tok0:tok0 + N], obB, r=[k.Bo[c][tb] for c in range(nk)], w=[obB])
        for i in range(ntl):
            for nh in range(2):
                kk = cnt["K"] % 2
                cnt["K"] += 1
                mm_group(k, psK[kk][:], psKB[kk],
                         [(ob[:, c, i * 128:(i + 1) * 128], Wo[:, c, nh * 512:(nh + 1) * 512]) for c in range(nk)], [obB, WoB])
                sc.op("dve", lambda: nc.vector.tensor_tensor(out=yt[nh][:], in0=psK[kk][:], in1=gbc[s][:, nh * 512:(nh + 1) * 512],
                                                             op=ALU.mult), r=[psKB[kk], gbcB[s]], w=[ytB[nh]])
                sc.op("pool", lambda: nc.gpsimd.tensor_tensor(out=ht[i][:, nh * 512:(nh + 1) * 512], in0=yt[nh][:],
                                                              in1=ht[i][:, nh * 512:(nh + 1) * 512], op=ALU.add),
                      r=[ytB[nh], htB[i]], w=[htB[i]])
            sc.dma("sp", S["h"][(t0 + i) * 128:(t0 + i + 1) * 128, :], ht[i][:], htB[i], r=[htB[i]], w=[k.Bh[t0 + i]])
        norm_stats(k, nx, [(ht[i][:], htB[i]) for i in range(ntl)])

    def preB(tb):
        t0, ntl, N, isctx = block_tiles(tb)
        s = 1 if isctx else 0
        for i in range(ntl):
            norm_apply(k, nx, ht[i][:], htB[i], i, modidx(l, s, 2), aT2[tb % 2], aT2B[tb % 2], i * 128, on_act=True)

    preA(0)
    preB(0)
    for tb in range(nblk):
        t0, ntl, N, isctx = block_tiles(tb)
        tok0 = t0 * 128
        aT, aTB = aT2[tb % 2], aT2B[tb % 2]
        for f in range(NF):
            if tb + 1 < nblk and f == 4:
                preA(tb + 1)
            if tb + 1 < nblk and f == 13:
                preB(tb + 1)
            g2 = cnt["G"] % 2
            cnt["G"] += 1
            h3 = cnt["H"] % 3
            cnt["H"] += 1
            c0, c1 = f * 128, FFN + f * 128
            mm_group(k, psG[g2][:, 0:N], psGB[g2], [(Wfi[:, kc, c0:c0 + 128], aT[:, kc, 0:N]) for kc in range(8)],
                     [WfiB[c0 // 512], WfiB[(c0 + 127) // 512], aTB])
            mm_group(k, psU[g2][:, 0:N], psUB[g2], [(Wfi[:, kc, c1:c1 + 128], aT[:, kc, 0:N]) for kc in range(8)],
                     [WfiB[c1 // 512], WfiB[(c1 + 127) // 512], aTB])
            sc.op("act", lambda: nc.scalar.activation(out=sg[g2][:, 0:N], in_=psG[g2][:, 0:N], func=AF.Silu),
                  r=[psGB[g2]], w=[sgB[g2]])
            sc.op("dve", lambda: nc.vector.tensor_tensor(out=hst[h3][:, 0:N], in0=psU[g2][:, 0:N], in1=sg[g2][:, 0:N], op=ALU.mult),
                  r=[psUB[g2], sgB[g2]], w=[hstB[h3]])
            sc.dma("sp", S["hm"][f][:, tok0:tok0 + N], hst[h3][:, 0:N], hstB[h3], r=[hstB[h3]], w=[k.Bhm[f][tb]])
    barrier(k)


def phase_C2(k, l):
    nc, sc, st, I, S = k.nc, k.sc, k.pes, k.I, k.S
    tg = "c2%d_" % l
    nblk = 9 if l == 0 else 8
    last = l == DEPTH - 1
    Wfo = sbt(k, st, tg + "Wfo", [128, NF, D], BF16)
    WfoB = Buf(tg + "Wfo")
    load_w(k, Wfo, I["ffn_w_out"][l], WfoB, pieces=2)
    gbc = [sbt(k, st, tg + "gbc%d" % i, [128, D], F32) for i in range(2)]
    gbcB = _bufs(tg + "gbc", 2)
    mi = 1 if l == 0 else 5
    sc.dma("sp", gbc[0][:], S["modbc"][mi], gbcB[0], r=[k.Bmod[mi]], w=[gbcB[0]])
    if l == 0:
        sc.dma("sp", gbc[1][:], S["modbc"][3], gbcB[1], r=[k.Bmod[3]], w=[gbcB[1]])
    if last:
        fg = sbt(k, st, tg + "fg", [128, D], F32)
        fgB = Buf(tg + "fg")
        sc.dma("sp", fg[:], I["ng_bc"][4], fgB, w=[fgB])
        ot = [sbt(k, st, tg + "ot%d" % i, [128, D], F32) for i in range(2)]
        otB = _bufs(tg + "ot", 2)
    ht = [sbt(k, st, tg + "ht%d" % i, [128, D], F32) for i in range(4)]
    htB = _bufs(tg + "ht", 4)
    hmb = [sbt(k, st, tg + "hmb%d" % i, [128, NF, 512], BF16) for i in range(2)]
    hmbB = _bufs(tg + "hmb", 2)
    nx = NormCtx(k, st, tg)
    yt = [sbt(k, st, tg + "yt%d" % i, [128, 512], F32) for i in range(2)]
    ytB = _bufs(tg + "yt", 2)
    psK = [pst(k, st, tg + "psK%d" % i, [128, 512]) for i in range(4)]
    psKB = _bufs(tg + "psK", 4)
    hv = S["hm"].rearrange("f p n -> p f n")
    nK = nO = 0

    def loads(tb):
        t0, ntl, N, isctx = block_tiles(tb)
        hb = tb % 2
        sc.dma("sp", hmb[hb][:, :, 0:N], hv[:, :, t0 * 128:t0 * 128 + N], hmbB[hb], r=[k.Bhm[f][tb] for f in range(NF)],
               w=[hmbB[hb]])

    loads(0)
    for tb in range(nblk):
        t0, ntl, N, isctx = block_tiles(tb)
        s = 1 if isctx else 0
        hb = tb % 2
        for i in range(ntl):
            sc.dma("sp", ht[i][:], S["h"][(t0 + i) * 128:(t0 + i + 1) * 128, :], htB[i], r=[k.Bh[t0 + i]], w=[htB[i]])
        if tb + 1 < nblk:
            loads(tb + 1)
        for i in range(ntl):
            for nh in range(2):
                kk = nK % 4
                nK += 1
                mm_group(k, psK[kk][:], psKB[kk],
                         [(hmb[hb][:, f, i * 128:(i + 1) * 128], Wfo[:, f, nh * 512:(nh + 1) * 512]) for f in range(NF)],
                         [hmbB[hb], WfoB])
                sc.op("dve", lambda: nc.vector.tensor_tensor(out=yt[nh][:], in0=psK[kk][:], in1=gbc[s][:, nh * 512:(nh + 1) * 512],
                                                             op=ALU.mult), r=[psKB[kk], gbcB[s]], w=[ytB[nh]])
                sc.op("pool", lambda: nc.gpsimd.tensor_tensor(out=ht[i][:, nh * 512:(nh + 1) * 512], in0=yt[nh][:],
                                                              in1=ht[i][:, nh * 512:(nh + 1) * 512], op=ALU.add),
                      r=[ytB[nh], htB[i]], w=[htB[i]])
            if not last:
                sc.dma("sp", S["h"][(t0 + i) * 128:(t0 + i + 1) * 128, :], ht[i][:], htB[i], r=[htB[i]], w=[k.Bh[t0 + i]])
        if last:
            norm_stats(k, nx, [(ht[i][:], htB[i]) for i in range(ntl)])
            for i in range(ntl):
                o2 = nO % 2
                nO += 1
                sc.op("dve", lambda: nc.vector.scalar_tensor_tensor(out=ot[o2][:], in0=ht[i][:], scalar=nx.rs[:, i:i + 1],
                                                                    in1=fg[:], op0=ALU.mult, op1=ALU.mult),
                      r=[htB[i], nx.rsB, fgB], w=[otB[o2]])
                sc.dma("sp", k.out[(t0 + i) * 128:(t0 + i + 1) * 128, :], ot[o2][:], otB[o2], r=[otB[o2]])
    barrier(k)


def phase_C1_l0(k):
    phase_C1(k, 0)


def phase_C2_l0(k):
    phase_C2(k, 0)


def phase_C1_l1(k):
    phase_C1(k, 1)


def phase_C2_l1(k):
    phase_C2(k, 1)


def phase_A1(k):
    nc, sc, st, I, S = k.nc, k.sc, k.pes, k.I, k.S
    Wq = sbt(k, st, "a1_Wq", [128, 8, 1792], BF16)
    WqB = _bufs("a1_Wq", 4)
    wv = I["odd_w_qkv"].rearrange("(kc p) n -> p kc n", p=128)
    for g in range(2):
        sc.dma("pool", Wq[:, :, g * 512:(g + 1) * 512], wv[:, :, g * 512:(g + 1) * 512], WqB[g], w=[WqB[g]])
    for j in range(4):
        for e in range(2):
            sc.dma("pool", Wq[:, :, 1024 + j * 128 + e * 64:1024 + j * 128 + (e + 1) * 64],
                   wv[:, :, 1024 + j * 64:1024 + (j + 1) * 64], WqB[2], w=[WqB[2]])
    sc.dma("pool", Wq[:, :, 1536:1792], wv[:, :, 1280:1536], WqB[3], w=[WqB[3]])
    ht = [sbt(k, st, "a1_ht%d" % i, [128, D], F32) for i in range(4)]
    htB = _bufs("a1_ht", 4)
    nx = NormCtx(k, st, "a1_")
    rc = RopeCtx(k, st, "a1_")
    aT = sbt(k, st, "a1_aT", [128, 8, 512], BF16)
    aTB = Buf("a1_aT")
    vst = [sbt(k, st, "a1_vst%d" % i, [128, 256], BF16) for i in range(2)]
    vstB = _bufs("a1_vst", 2)
    psK = [pst(k, st, "a1_psK%d" % i, [128, 512]) for i in range(2)]
    psKB = _bufs("a1_psK", 2)
    psF = [pst(k, st, "a1_psF%d" % i, [128, 512]) for i in range(2)]
    psFB = _bufs("a1_psF", 2)
    cnt = {"K": 0, "F": 0}
    aT2 = [aT, sbt(k, st, "a1_aTb", [128, 8, 512], BF16)]
    aT2B = [aTB, Buf("a1_aTb")]

    def stage_load(tb):
        t0, ntl, N, isctx = block_tiles(tb)
        for i in range(ntl):
            sc.dma("sp", ht[i][:], S["h"][(t0 + i) * 128:(t0 + i + 1) * 128, :], htB[i], r=[k.Bh[t0 + i]], w=[htB[i]])
        if not isctx:
            rope_load(k, rc, t0 * 128, N, tb % 2)

    def stage_norm(tb):
        t0, ntl, N, isctx = block_tiles(tb)
        norm_stats(k, nx, [(ht[i][:], htB[i]) for i in range(ntl)])
        for i in range(ntl):
            norm_xn(k, nx, ht[i][:], htB[i], i)

    def stage_norm2(tb):
        t0, ntl, N, isctx = block_tiles(tb)
        s = 1 if isctx else 0
        for i in range(ntl):
            norm_tr(k, nx, i, modidx(1, s, 0), aT2[tb % 2], aT2B[tb % 2], i * 128)

    stage_load(0)
    stage_norm(0)
    stage_norm2(0)
    for tb in range(9):
        t0, ntl, N, isctx = block_tiles(tb)
        tok0 = t0 * 128
        aTc, aTcB = aT2[tb % 2], aT2B[tb % 2]
        rope_select(rc, tb % 2)
        if tb + 1 < 9:
            stage_load(tb + 1)
        pend = None
        for c in range(12):
            if isctx and c < 8:
                continue
            f = cnt["F"] % 2
            cnt["F"] += 1
            mm_group(k, psF[f][:, 0:N], psFB[f], [(Wq[:, kc, c * 128:(c + 1) * 128], aTc[:, kc, 0:N]) for kc in range(8)],
                     [WqB[0] if c < 4 else (WqB[1] if c < 8 else WqB[2]), aTcB])
            if pend is not None:
                pend()
            if c < 8:
                pend = rope_store(k, rc, psF[f][:, 0:N], psFB[f], N, True, S["q"][c][:, tok0:tok0 + N], k.Bq[c][tb])
            else:
                pend = rope_store(k, rc, psF[f][:, 0:N], psFB[f], N, not isctx, S["k"][c - 8][:, tok0:tok0 + N], k.Bk[c - 8][tb])
            if c == 1 and tb + 1 < 9:
                stage_norm(tb + 1)
            if c == 7 and tb + 1 < 9:
                stage_norm2(tb + 1)
        for i in range(ntl):
            kk = cnt["K"] % 2
            cnt["K"] += 1
            mm_group(k, psK[kk][:, 0:256], psKB[kk], [(aTc[:, kc, i * 128:(i + 1) * 128], Wq[:, kc, 1536:1792]) for kc in range(8)],
                     [WqB[3], aTcB])
            if pend is not None:
                pend()
                pend = None
            sc.op("dve", lambda: nc.vector.tensor_copy(out=vst[kk][:], in_=psK[kk][:, 0:256]), r=[psKB[kk]], w=[vstB[kk]])
            sc.dma("sp", S["v"][(t0 + i) * 128:(t0 + i + 1) * 128, 0:256], vst[kk][:], vstB[kk], r=[vstB[kk]], w=[k.Bv[t0 + i]])
    barrier(k)


def phase_B1(k):
    nc, sc, st, I, S = k.nc, k.sc, k.pes, k.I, k.S
    kT = sbt(k, st, "b1_kT", [128, 4, NTOK], BF16)
    kTB = _bufs("b1_kT", 4)
    for j in range(4):
        sc.dma("sp", kT[:, j, :], S["k"][j], kTB[j], r=[k.Bk[j][b] for b in range(9)], w=[kTB[j]])
    vA = sbt(k, st, "b1_vA", [128, NT, 256], BF16)
    vAB = Buf("b1_vA")
    vv = S["v"].rearrange("(t p) e -> p t e", p=128)
    for a, b in ((0, 17), (17, 34)):
        sc.dma("sp", vA[:, a:b, :], vv[:, a:b, 0:256], vAB, r=[k.Bv[t] for t in range(a, b)], w=[vAB])
    sk = sbt(k, st, "b1_sk", [128, 8], F32)
    es = sbt(k, st, "b1_es", [128, 8], F32)
    skB, esB = Buf("b1_sk"), Buf("b1_es")
    sc.dma("sp", sk[:], I["sink_col"], skB, w=[skB])
    sc.op("act", lambda: nc.scalar.activation(out=es[:], in_=sk[:], func=AF.Exp), r=[skB], w=[esB])
    psS = [pst(k, st, "b1_psS%d" % i, [128, 2, 512]) for i in range(2)]
    psSB = _bufs("b1_psS", 2)
    psO = [pst(k, st, "b1_psO%d" % i, [128, 512]) for i in range(2)]
    psOB = _bufs("b1_psO", 2)
    psL = [pst(k, st, "b1_psL%d" % i, [128, 512]) for i in range(2)]
    psLB = _bufs("b1_psL", 2)
    E = [sbt(k, st, "b1_E%d" % i, [128, 2, 512], BF16) for i in range(3)]
    EB8 = [[Buf("b1_E%d_%d" % (i, j)) for j in range(8)] for i in range(3)]

    def ebs(er, qa, qe):
        return [EB8[er][e * 4 + qt] for e in range(2) for qt in range(qa, qe)]
    qblk = [sbt(k, st, "b1_q%d" % i, [128, 512], BF16) for i in range(2)]
    qblkB = _bufs("b1_q", 2)
    LT = [sbt(k, st, "b1_LT%d" % i, [128, 512], F32) for i in range(2)]
    LTB = _bufs("b1_LT", 2)
    RR = [sbt(k, st, "b1_RR%d" % i, [128, 512], F32) for i in range(2)]
    RRB = _bufs("b1_RR", 2)
    osg = [sbt(k, st, "b1_osg%d" % i, [128, 512], BF16) for i in range(2)]
    osgB = _bufs("b1_osg", 2)
    its = [(qb, c) for qb in range(int(os.environ.get("B1_QB", "8"))) for c in range(8)]
    cS = cE = 0

    def load_q(n):
        qb, c = its[n]
        sc.dma("sp", qblk[n % 2][:], S["q"][c][:, qb * 512:(qb + 1) * 512], qblkB[n % 2], r=[k.Bq[c][qb]], w=[qblkB[n % 2]])

    load_q(0)
    for n, (qb, c) in enumerate(its):
        j = c // 2
        x2 = n % 2
        qq, qqB = qblk[x2], qblkB[x2]
        if n + 1 < len(its):
            load_q(n + 1)
        tl = [(32, 0, 4), (33, 0, 4)]
        for m in range(-1, 5):
            kt = 4 * qb + m
            if 0 <= kt <= 31:
                tl.append((kt, max(0, m - 1), min(3, m + 1) + 1))
        nt = len(tl)
        sl = {}

        def emit_S(i):
            nonlocal cS
            s = cS % 2
            cS += 1
            sl[i] = s
            kt, qa, qe = tl[i]
            for e in range(2):
                sc.op("pe", lambda: nc.tensor.matmul(psS[s][:, e, qa * 128:qe * 128],
                                                     lhsT=kT[e * 64:(e + 1) * 64, j, kt * 128:(kt + 1) * 128],
                                                     rhs=qq[e * 64:(e + 1) * 64, qa * 128:qe * 128], start=True, stop=True),
                      r=[kTB[j], qqB], w=[psSB[s]], signal=(e == 1))

        emit_S(0)
        for i in range(nt):
            if i + 1 < nt:
                emit_S(i + 1)
            s = sl[i]
            er = cE % 3
            cE += 1
            kt, qa, qe = tl[i]
            lo, hi = qa * 128, qe * 128
            sc.op("act", lambda: nc.scalar.activation(out=E[er][:, :, lo:hi], in_=psS[s][:, :, lo:hi], func=AF.Exp, scale=SCALE),
                  r=[psSB[s]], w=ebs(er, qa, qe))
            if kt < 32:
                for qt in range(qa, qe):
                    rel = kt - (4 * qb + qt)
                    if rel == 0:
                        continue
                    mk = k.mprev if rel == -1 else k.mnext
                    for e in range(2):
                        sc.op("dve", lambda: nc.vector.tensor_tensor(out=E[er][:, e, qt * 128:(qt + 1) * 128],
                                                                     in0=E[er][:, e, qt * 128:(qt + 1) * 128], in1=mk,
                                                                     op=ALU.mult), r=[EB8[er][e * 4 + qt], k.Bc], w=[EB8[er][e * 4 + qt]])
            for e in range(2):
                sc.op("pe", lambda: nc.tensor.matmul(psO[x2][e * 64:(e + 1) * 64, lo:hi], lhsT=vA[:, kt, j * 64:(j + 1) * 64],
                                                     rhs=E[er][:, e, lo:hi], start=(i == 0), stop=(i == nt - 1),
                                                     skip_group_check=True),
                      r=[vAB] + ebs(er, qa, qe), w=[psOB[x2]], signal=False)
            for e in range(2):
                sc.op("pe", lambda: nc.tensor.matmul(psL[x2][e * 64:(e + 1) * 64, lo:hi], lhsT=k.ones[:, 0:64],
                                                     rhs=E[er][:, e, lo:hi], start=(i == 0), stop=(i == nt - 1),
                                                     skip_group_check=True),
                      r=[k.Bc] + ebs(er, qa, qe), w=[psLB[x2]], signal=(e == 1))
        sc.op("act", lambda: nc.scalar.activation(out=LT[x2][:], in_=psL[x2][:], func=AF.Ln, bias=es[:, c:c + 1]),
              r=[psLB[x2], esB], w=[LTB[x2]])
        sc.op("act", lambda: nc.scalar.activation(out=RR[x2][:], in_=LT[x2][:], func=AF.Exp, scale=-1.0),
              r=[LTB[x2]], w=[RRB[x2]])
        sc.op("dve", lambda: nc.vector.tensor_tensor(out=osg[x2][:], in0=psO[x2][:], in1=RR[x2][:], op=ALU.mult),
              r=[psOB[x2], RRB[x2]], w=[osgB[x2]])
        sc.dma("sp", S["o"][c][:, qb * 512:(qb + 1) * 512], osg[x2][:], osgB[x2], r=[osgB[x2]], w=[k.Bo[c][qb]])
    barrier(k)


def host_consts():
    t = np.arange(SEQ)
    row = (t // GRID_W).astype(np.float32)
    col = (t % GRID_W).astype(np.float32)
    nf = HEAD_DIM // 4
    inv = (np.float32(10000.0) ** (-np.arange(nf, dtype=np.float32) / np.float32(nf))).astype(np.float32)
    p = np.arange(128)
    d = p % 64
    pos = np.where((d < 32)[:, None], row[None, :], col[None, :]).astype(np.float32)
    ang = (pos * inv[d % 16][:, None]).astype(np.float32)
    sign = np.where((d % 32) < 16, -1.0, 1.0).astype(np.float32)
    cosT = np.cos(ang).astype(np.float32)
    sinT = (np.sin(ang) * sign[:, None]).astype(np.float32)
    cbf = np.zeros((128, 768), np.float32)
    cbf[:, 0:128] = np.eye(128, dtype=np.float32)
    cbf[:, 128:256] = 1.0
    partner = np.where((p % 32) < 16, p + 16, p - 16)
    cbf[partner, 256 + p] = 1.0
    kl = np.arange(128)[:, None]
    ql = np.arange(128)[None, :]
    cbf[:, 384:512] = (kl >= ql).astype(np.float32)
    cbf[:, 512:640] = (kl <= ql).astype(np.float32)
    cf32 = np.zeros((128, 960), np.float32)
    cf32[0, 704:832] = 1.0
    cf32[32, 832:960] = 1.0
    cf32[:, 0:512] = np.tile(np.eye(128, dtype=np.float32), (1, 4))
    cf32[:, 512:640] = 1.0
    cf32[:, 640] = EPS
    return {"cosT": cosT, "sinT": sinT, "cbf": cbf, "cf32": cf32}


def host_shared(inp):
    f = lambda a: np.ascontiguousarray(np.asarray(a, dtype=np.float32))
    bc = lambda v: np.ascontiguousarray(np.broadcast_to(np.asarray(v, np.float32).reshape(1, -1), (128, np.asarray(v).size)))
    sh = dict(host_consts())
    sh["ada_w"] = f(inp["ada_w"])
    sh["ada_b_bc"] = np.stack([bc(inp["ada_b"][l]) for l in range(DEPTH)])
    sh["ng_bc"] = np.stack([bc(inp["norm1_g"][0]), bc(inp["norm2_g"][0]), bc(inp["norm1_g"][1]), bc(inp["norm2_g"][1]),
                            bc(inp["final_g"])])
    sh["ffn_w_in"] = f(inp["ffn_w_in"])
    sh["ffn_w_out"] = f(inp["ffn_w_out"])
    sh["even_w_in"] = f(inp["even_w_in"][0])
    sh["even_w_out"] = f(inp["even_w_out"][0])
    sh["sgu_wT"] = f(np.transpose(np.asarray(inp["sgu_w"][0]), (2, 0, 1)).reshape(128, 512))
    sh["sgu_b_bc"] = bc(np.asarray(inp["sgu_b"][0]).reshape(-1))
    sh["lqk_bc"] = bc(np.concatenate([np.asarray(inp[n][0]) for n in ("diff_lq1", "diff_lk1", "diff_lq2", "diff_lk2")]))
    sh["subln_col"] = f(np.asarray(inp["diff_subln_g"][0]).reshape(128, 1))
    sh["odd_w_qkv"] = f(inp["odd_w_qkv"][0])
    sh["odd_w_out"] = f(inp["odd_w_out"][0])
    sk = np.asarray(inp["odd_sink"][0], np.float32)
    sh["sink_col"] = f(np.stack([np.concatenate([np.full(64, sk[2 * c]), np.full(64, sk[2 * c + 1])]) for c in range(8)], 1))
    return sh


def host_core(inp, sh, b):
    m = dict(sh)
    m["x"] = np.ascontiguousarray(np.concatenate([np.asarray(inp["x"][b], np.float32), np.asarray(inp["ctx"][b], np.float32)], 0))
    cc = np.concatenate([np.asarray(inp["c"][b], np.float32).reshape(8, 128).T,
                         np.asarray(inp["c_ctx"], np.float32).reshape(8, 128).T], 1)
    m["c_col"] = np.ascontiguousarray(cc)
    return m


_NC_CACHE = {}


def kernel(**inputs):
    if "nc" not in _NC_CACHE:
        _NC_CACHE["nc"] = build()[0]
    nc = _NC_CACHE["nc"]
    sh = host_shared(inputs)
    n = 8
    in_maps = [host_core(inputs, sh, b) for b in range(n)]
    res = run_bass_kernel_spmd(nc, in_maps, core_ids=list(range(n)))
    return np.stack([np.asarray(r["out"], dtype=np.float32) for r in res.results], 0)
```

```python
import contextlib
import math
import os
import numpy as np
import ml_dtypes
import concourse.bass as bass
import concourse.mybir as mybir
from concourse.bass_utils import run_bass_kernel_spmd

F32 = mybir.dt.float32
BF16 = mybir.dt.bfloat16
AF = mybir.ActivationFunctionType
ALU = mybir.AluOpType
AX = mybir.AxisListType

D = 1024
SEQ = 4096
CTX = 256
NTOK = SEQ + CTX
NT = NTOK // 128
DEPTH = 2
EPS = 1e-6
FFN = 2816
NF = FFN // 128
GRID_W = 64
HEAD_DIM = 64
SCALE = HEAD_DIM ** -0.5
LAM_INIT0 = 0.8 - 0.6 * math.exp(-0.3 * 0)


class Buf:
    __slots__ = ("name", "writer", "readers", "sem", "semval")

    def __init__(self, name):
        self.name = name
        self.writer = None
        self.readers = {}
        self.sem = None
        self.semval = 0


class Ins:
    __slots__ = ("key", "sem", "val", "is_dma")

    def __init__(self, key, sem=None, val=None, is_dma=False):
        self.key = key
        self.sem = sem
        self.val = val
        self.is_dma = is_dma


class Sched:
    CE = ("pe", "act", "dve", "pool")

    def __init__(self, nc, es):
        self.nc = nc
        self.es = es
        self.E = {"pe": nc.tensor, "act": nc.scalar, "dve": nc.vector, "pool": nc.gpsimd, "sp": nc.sync}
        self.sem = {e: es.enter_context(nc.semaphore("sem_" + e)) for e in self.CE}
        self.cnt = {e: 0 for e in self.CE}
        self.waited = {}
        self.pending = {e: [] for e in self.CE}
        self.dma_bufs = []
        self.sem_pool = []
        self.sw_bufs = []
        self.n_dsem = 0
        self.n_ins = 0
        self.n_wait = 0

    def _wait(self, stream, d):
        if d.val is None:
            raise RuntimeError("dependency on unsignalled instruction of " + d.key)
        k = (stream, d.sem.name)
        if self.waited.get(k, 0) >= d.val:
            return
        self.waited[k] = d.val
        self.E[stream].wait_ge(d.sem, d.val)
        self.n_wait += 1

    def _deps(self, stream, r, w, is_dma):
        def skip(d):
            return d.key == "pe" and stream == "pe" and not is_dma
        for b in r:
            d = b.writer
            if d is not None and not skip(d):
                self._wait(stream, d)
        for b in w:
            d = b.writer
            if d is not None and not skip(d):
                self._wait(stream, d)
            for d in b.readers.values():
                if not skip(d):
                    self._wait(stream, d)

    def op(self, eng, fn, r=(), w=(), signal=True):
        self._deps(eng, r, w, False)
        inst = fn()
        self.n_ins += 1
        ins = Ins(eng)
        if signal:
            self.cnt[eng] += 1
            inst.then_inc(self.sem[eng], 1)
            ins.sem = self.sem[eng]
            ins.val = self.cnt[eng]
            for p in self.pending[eng]:
                p.sem = ins.sem
                p.val = ins.val
            self.pending[eng] = []
        else:
            self.pending[eng].append(ins)
        for b in r:
            b.readers[eng] = ins
        for b in w:
            b.writer = ins
            b.readers = {}
        return ins

    def dma(self, q, out, in_, chan, r=(), w=()):
        self._deps(q, r, w, True)
        if q == "pool":
            chan = Buf(chan.name + "_sw%d" % self.n_dsem)
            chan.sem = self.es.enter_context(self.nc.semaphore("w%d" % self.n_dsem))
            self.n_dsem += 1
            self.sw_bufs.append(chan)
        if chan.sem is None:
            if self.sem_pool:
                chan.sem, chan.semval = self.sem_pool.pop()
            else:
                chan.sem = self.es.enter_context(self.nc.semaphore("d%d" % self.n_dsem))
                self.n_dsem += 1
            self.dma_bufs.append(chan)
        chan.semval += 16
        self.E[q].dma_start(out=out, in_=in_).then_inc(chan.sem, 16)
        self.n_ins += 1
        ins = Ins("dma_" + chan.name, chan.sem, chan.semval, True)
        for b in r:
            b.readers[ins.key] = ins
        for b in w:
            b.writer = ins
            b.readers = {}
        return ins

    def finish(self):
        for b in self.dma_bufs + self.sw_bufs:
            self._wait("sp", Ins("x", b.sem, b.semval, True))
        for e in self.CE:
            if self.cnt[e]:
                self._wait("sp", Ins(e, self.sem[e], self.cnt[e]))


class K:
    pass


def _bufs(prefix, n):
    return [Buf("%s%d" % (prefix, i)) for i in range(n)]


def build(stop_after=None, dev=False):
    nc = bass.Bass("TRN2", target_bir_lowering=False)
    k = K()
    k.nc = nc
    k.dev = dev

    def din(name, shape, dt=F32):
        return nc.dram_tensor(name, list(shape), dt, kind="ExternalInput").ap()

    def dscr(name, shape, dt):
        return nc.dram_tensor(name, list(shape), dt, kind="ExternalOutput" if dev else "Internal").ap()

    I = {}
    I["x"] = din("x", [NTOK, D])
    I["c_col"] = din("c_col", [128, 16])
    I["ada_w"] = din("ada_w", [DEPTH, D, 6 * D])
    I["ada_b_bc"] = din("ada_b_bc", [DEPTH, 128, 6 * D])
    I["ng_bc"] = din("ng_bc", [5, 128, D])
    I["ffn_w_in"] = din("ffn_w_in", [DEPTH, D, 2 * FFN])
    I["ffn_w_out"] = din("ffn_w_out", [DEPTH, FFN, D])
    I["even_w_in"] = din("even_w_in", [D, 2560])
    I["even_w_out"] = din("even_w_out", [D, D])
    I["sgu_wT"] = din("sgu_wT", [128, 512])
    I["sgu_b_bc"] = din("sgu_b_bc", [128, 512])
    I["lqk_bc"] = din("lqk_bc", [128, 256])
    I["subln_col"] = din("subln_col", [128, 1])
    I["odd_w_qkv"] = din("odd_w_qkv", [D, 1536])
    I["odd_w_out"] = din("odd_w_out", [D, D])
    I["sink_col"] = din("sink_col", [128, 8])
    I["cosT"] = din("cosT", [128, SEQ])
    I["sinT"] = din("sinT", [128, SEQ])
    I["cbf"] = din("cbf", [128, 768])
    I["cf32"] = din("cf32", [128, 960])
    k.I = I
    k.out = nc.dram_tensor("out", [SEQ, D], F32, kind="ExternalOutput").ap()

    S = {}
    S["h"] = dscr("h_scr", [NTOK, D], F32)
    S["modbc"] = dscr("modbc_scr", [6, 128, D], F32)
    S["q"] = dscr("q_scr", [8, 128, NTOK], BF16)
    S["k"] = dscr("k_scr", [4, 128, NTOK], BF16)
    S["v"] = dscr("v_scr", [NTOK, 512], BF16)
    S["o"] = dscr("o_scr", [8, 128, NTOK], BF16)
    S["hm"] = dscr("hm_scr", [NF, 128, NTOK], BF16)
    if dev:
        S["dbg"] = dscr("dbg_scr", [128, 256], F32)
    k.S = S
    k.Bh = _bufs("Dh", NT)
    k.Bmod = _bufs("Dmod", 6)
    k.Bq = [[Buf("Dq%d_%d" % (c, b)) for b in range(9)] for c in range(8)]
    k.Bk = [[Buf("Dk%d_%d" % (c, b)) for b in range(9)] for c in range(4)]
    k.Bv = _bufs("Dv", NT)
    k.Bo = [[Buf("Do%d_%d" % (c, b)) for b in range(9)] for c in range(8)]
    k.Bhm = [[Buf("Dhm%d_%d" % (f, b)) for b in range(9)] for f in range(NF)]

    with contextlib.ExitStack() as es:
        sc = Sched(nc, es)
        k.sc = sc
        k.es = es
        k.marks = []
        k.marks = []
        phases = [phase_prologue, phase_A0, phase_B0, phase_C1_l0, phase_C2_l0,
                  phase_A1, phase_B1, phase_C1_l1, phase_C2_l1]
        for ph in phases:
            with contextlib.ExitStack() as pes:
                k.pes = pes
                ph(k)
            k.marks.append((ph.__name__, dict(sc.cnt)))
            k.marks.append((ph.__name__, dict(sc.cnt)))
            if stop_after is not None and ph.__name__ == stop_after:
                break
        sc.finish()
    k.n_ins = sc.n_ins
    k.n_wait = sc.n_wait
    return nc, k


def sbt(k, st, name, shape, dt):
    return st.enter_context(k.nc.sbuf_tensor(name, list(shape), dt))


def pst(k, st, name, shape, dt=F32):
    return st.enter_context(k.nc.psum_tensor(name, list(shape), dt))


def mm_group(k, out_ap, outB, pairs, rB, signal_last=True):
    sc, nc = k.sc, k.nc
    n = len(pairs)
    for i, (l, r) in enumerate(pairs):
        sc.op("pe", lambda: nc.tensor.matmul(out_ap, lhsT=l, rhs=r, start=(i == 0), stop=(i == n - 1)),
              r=rB, w=[outB], signal=(signal_last and i == n - 1))


def barrier(k):
    sc = k.sc
    for e in sc.CE:
        assert not sc.pending[e], "unsignalled tail on " + e
    for s in ("pe", "act", "dve", "pool", "sp"):
        for e in sc.CE:
            if sc.cnt[e]:
                sc._wait(s, Ins(e, sc.sem[e], sc.cnt[e]))
        for b in sc.dma_bufs + sc.sw_bufs:
            sc._wait(s, Ins("x", b.sem, b.semval, True))
    sc.sw_bufs = []
    for b in sc.dma_bufs:
        sc.sem_pool.append((b.sem, b.semval))
        b.sem = None
    sc.dma_bufs = []


def modidx(l, s, v):
    return ((l * 2 + s) * 4 + v) * 8


class NormCtx:
    def __init__(self, k, st, tag):
        self.junks = [sbt(k, st, tag + "junk%d" % i, [128, D], BF16) for i in range(2)]
        self.junkB = _bufs(tag + "junk", 2)
        self.nj = 0
        self.ss = sbt(k, st, tag + "ss", [128, 4], F32)
        self.sq = sbt(k, st, tag + "sq", [128, 4], F32)
        self.rs = sbt(k, st, tag + "rs", [128, 4], F32)
        self.ssB, self.sqB, self.rsB = Buf(tag + "ss"), Buf(tag + "sq"), Buf(tag + "rs")
        self.xn = [sbt(k, st, tag + "xn%d" % i, [128, D], BF16) for i in range(4)]
        self.xnB = _bufs(tag + "xn", 4)
        self.np_ = 0
        self.psT = [pst(k, st, tag + "psT%d" % i, [128, 8, 128], BF16) for i in range(2)]
        self.psTB = _bufs(tag + "psT", 2)
        self.n = 0


def norm_stats(k, nx, tiles):
    sc, nc = k.sc, k.nc
    nt = len(tiles)
    for i, (ap, B) in enumerate(tiles):
        jj = nx.nj % 2
        nx.nj += 1
        sc.op("act", lambda: nc.scalar.activation(out=nx.junks[jj][:], in_=ap, func=AF.Square,
                                                  accum_out=nx.ss[:, i:i + 1]), r=[B], w=[nx.ssB, nx.junkB[jj]])
    sc.op("act", lambda: nc.scalar.activation(out=nx.sq[:, 0:nt], in_=nx.ss[:, 0:nt], func=AF.Sqrt,
                                              scale=1.0 / D, bias=k.epsc), r=[nx.ssB, k.Bcf], w=[nx.sqB])
    sc.op("dve", lambda: nc.vector.reciprocal(out=nx.rs[:, 0:nt], in_=nx.sq[:, 0:nt]), r=[nx.sqB], w=[nx.rsB])


def norm_xn(k, nx, ap, B, i):
    sc, nc = k.sc, k.nc
    sc.op("dve", lambda: nc.vector.tensor_scalar(out=nx.xn[i][:], in0=ap, scalar1=nx.rs[:, i:i + 1], scalar2=None,
                                                 op0=ALU.mult), r=[B, nx.rsB], w=[nx.xnB[i]])


def norm_tr(k, nx, i, mbase, aT, aTB, col0, on_act=False):
    sc, nc = k.sc, k.nc
    r = nx.np_ % 2
    nx.np_ += 1
    for j in range(8):
        sc.op("pe", lambda: nc.tensor.transpose(nx.psT[r][:, j, :], nx.xn[i][:, j * 128:(j + 1) * 128], k.ident),
              r=[nx.xnB[i], k.Bc], w=[nx.psTB[r]], signal=(j == 7))
    for j in range(8):
        if on_act:
            sc.op("act", lambda: nc.scalar.activation(out=aT[:, j, col0:col0 + 128], in_=nx.psT[r][:, j, :],
                                                      func=AF.Identity, scale=k.modcol[:, mbase + j:mbase + j + 1],
                                                      bias=k.modcol[:, mbase + 8 + j:mbase + 9 + j]),
                  r=[nx.psTB[r], k.Bmodcol], w=[aTB])
            continue
        sc.op("dve", lambda: nc.vector.tensor_scalar(out=aT[:, j, col0:col0 + 128], in0=nx.psT[r][:, j, :],
                                                     scalar1=k.modcol[:, mbase + j:mbase + j + 1],
                                                     scalar2=k.modcol[:, mbase + 8 + j:mbase + 9 + j],
                                                     op0=ALU.mult, op1=ALU.add),
              r=[nx.psTB[r], k.Bmodcol], w=[aTB])


def norm_apply(k, nx, ap, B, i, mbase, aT, aTB, col0, on_act=False):
    norm_xn(k, nx, ap, B, i)
    norm_tr(k, nx, i, mbase, aT, aTB, col0, on_act)


class RopeCtx:
    def __init__(self, k, st, tag):
        self.qb = [sbt(k, st, tag + "qb%d" % i, [128, 512], BF16) for i in range(2)]
        self.qbB = _bufs(tag + "qb", 2)
        self.psR = pst(k, st, tag + "psR", [128, 512])
        self.psRB = Buf(tag + "psR")
        self.css = [sbt(k, st, tag + "cs%d" % i, [128, 512], F32) for i in range(2)]
        self.sns = [sbt(k, st, tag + "sn%d" % i, [128, 512], F32) for i in range(2)]
        self.csBs = _bufs(tag + "cs", 2)
        self.cs, self.sn, self.csB = self.css[0], self.sns[0], self.csBs[0]
        self.t1 = [sbt(k, st, tag + "t1%d" % i, [128, 512], F32) for i in range(2)]
        self.t2 = [sbt(k, st, tag + "t2%d" % i, [128, 512], F32) for i in range(2)]
        self.t1B, self.t2B = _bufs(tag + "t1", 2), _bufs(tag + "t2", 2)
        self.qst = [sbt(k, st, tag + "qst%d" % i, [128, 512], BF16) for i in range(3)]
        self.qstB = _bufs(tag + "qst", 3)
        self.n = 0
        self.m = 0


def rope_load(k, rc, tok0, N, par=0):
    k.sc.dma("sp", rc.css[par][:, 0:N], k.I["cosT"][:, tok0:tok0 + N], rc.csBs[par], w=[rc.csBs[par]])
    k.sc.dma("sp", rc.sns[par][:, 0:N], k.I["sinT"][:, tok0:tok0 + N], rc.csBs[par], w=[rc.csBs[par]])


def rope_select(rc, par):
    rc.cs, rc.sn, rc.csB = rc.css[par], rc.sns[par], rc.csBs[par]


def rope_store(k, rc, psF, psFB, N, rope, dst_ap, dstB):
    sc, nc = k.sc, k.nc
    s3 = rc.m % 3
    rc.m += 1
    if not rope:
        sc.op("act", lambda: nc.scalar.copy(out=rc.qst[s3][:, 0:N], in_=psF), r=[psFB], w=[rc.qstB[s3]])

        def part2():
            sc.dma("sp", dst_ap, rc.qst[s3][:, 0:N], rc.qstB[s3], r=[rc.qstB[s3]], w=[dstB])
        return part2
    r = rc.n % 2
    rc.n += 1
    sc.op("act", lambda: nc.scalar.copy(out=rc.qb[r][:, 0:N], in_=psF), r=[psFB], w=[rc.qbB[r]])
    sc.op("dve", lambda: nc.vector.tensor_tensor(out=rc.t1[r][:, 0:N], in0=psF, in1=rc.cs[:, 0:N], op=ALU.mult),
          r=[psFB, rc.csB, rc.qbB[r]], w=[rc.t1B[r]])

    def part2():
        sc.op("pe", lambda: nc.tensor.matmul(rc.psR[:, 0:N], lhsT=k.rperm, rhs=rc.qb[r][:, 0:N], start=True, stop=True),
              r=[rc.qbB[r], k.Bc], w=[rc.psRB])
        sc.op("dve", lambda: nc.vector.tensor_tensor(out=rc.t2[r][:, 0:N], in0=rc.psR[:, 0:N], in1=rc.sn[:, 0:N],
                                                     op=ALU.mult), r=[rc.psRB, rc.csB], w=[rc.t2B[r]])
        sc.op("pool", lambda: nc.gpsimd.tensor_tensor(out=rc.qst[s3][:, 0:N], in0=rc.t1[r][:, 0:N],
                                                      in1=rc.t2[r][:, 0:N], op=ALU.add),
              r=[rc.t1B[r], rc.t2B[r]], w=[rc.qstB[s3]])
        sc.dma("sp", dst_ap, rc.qst[s3][:, 0:N], rc.qstB[s3], r=[rc.qstB[s3]], w=[dstB])
    return part2


def load_w(k, dst, src, B, pieces=1):
    v = src.rearrange("(kc p) n -> p kc n", p=128)
    kc = v.shape[1]
    step = (kc + pieces - 1) // pieces
    for a in range(0, kc, step):
        b = min(kc, a + step)
        k.sc.dma("pool", dst[:, a:b, :], v[:, a:b, :], B, w=[B])


def phase_prologue(k):
    nc, sc, es, st, I = k.nc, k.sc, k.es, k.pes, k.I
    cbf = sbt(k, es, "sb_cbf", [128, 768], BF16)
    cf32 = sbt(k, es, "sb_cf32", [128, 960], F32)
    k.Bc, k.Bcf = Buf("cbf"), Buf("cf32")
    sc.dma("pool", cbf[:], I["cbf"], k.Bc, w=[k.Bc])
    sc.dma("sp", cf32[:], I["cf32"], k.Bcf, w=[k.Bcf])
    k.ident, k.ones, k.rperm = cbf[:, 0:128], cbf[:, 128:256], cbf[:, 256:384]
    k.mprev, k.mnext = cbf[:, 384:512], cbf[:, 512:640]
    k.identF4, k.onesF, k.epsc = cf32[:, 0:512], cf32[:, 512:640], cf32[:, 640:641]
    k.sel = [cf32[0:64, 704:832], cf32[0:64, 832:960]]
    k.modcol = sbt(k, es, "modcol", [128, 128], F32)
    k.Bmodcol = Buf("modcol")

    ccol = sbt(k, st, "p_ccol", [128, 16], F32)
    scol = sbt(k, st, "p_scol", [128, 16], F32)
    ccolB, scolB = Buf("p_ccol"), Buf("p_scol")
    sc.dma("sp", ccol[:], I["c_col"], ccolB, w=[ccolB])
    sc.op("act", lambda: nc.scalar.activation(out=scol[:], in_=ccol[:], func=AF.Silu), r=[ccolB], w=[scolB])
    L = sbt(k, st, "p_L", [128, 16, 128], F32)
    LB = Buf("p_L")
    for j in range(16):
        sc.op("dve", lambda: nc.vector.tensor_scalar(out=L[:, j, :], in0=k.onesF,
                                                     scalar1=scol[:, j:j + 1], scalar2=None, op0=ALU.mult),
              r=[scolB, k.Bcf], w=[LB])
    stage = [sbt(k, st, "p_stage%d" % i, [128, 8, 512], F32) for i in range(2)]
    stageB = _bufs("p_stage", 2)
    bias = [sbt(k, st, "p_bias%d" % i, [128, 512], F32) for i in range(2)]
    biasB = _bufs("p_bias", 2)
    ngt = sbt(k, st, "p_ngt", [128, D], F32)
    ngtB = Buf("p_ngt")
    psm = [pst(k, st, "p_ps%d" % i, [128, 512]) for i in range(2)]
    psmB = _bufs("p_ps", 2)
    tA = [sbt(k, st, "p_tA%d" % i, [128, 512], F32) for i in range(2)]
    tAB = _bufs("p_tA", 2)
    tB_ = [sbt(k, st, "p_tB%d" % i, [128, 512], F32) for i in range(2)]
    tBB = _bufs("p_tB", 2)
    tC = [sbt(k, st, "p_tC%d" % i, [128, 4, 128], F32) for i in range(2)]
    tCB = _bufs("p_tC", 2)
    gbc = [sbt(k, st, "p_gbc%d" % i, [128, D], F32) for i in range(2)]
    gbcB = _bufs("p_gbc", 2)
    ngrow = {(0, 1): 0, (0, 4): 1, (1, 1): 2, (1, 4): 3}
    cnt = 0
    gcnt = 0
    for l in range(DEPTH):
        for n in range(12):
            v, half = n // 2, n % 2
            r = cnt % 2
            cnt += 1
            sc.dma("sp", stage[r][:], I["ada_w"][l].rearrange("(kc p) n -> p kc n", p=128)[:, :, n * 512:(n + 1) * 512],
                   stageB[r], w=[stageB[r]])
            sc.dma("sp", bias[r][:], I["ada_b_bc"][l][:, n * 512:(n + 1) * 512], biasB[r], w=[biasB[r]])
            if v in (1, 4) and half == 0:
                sc.dma("sp", ngt[:], I["ng_bc"][ngrow[(l, v)]], ngtB, w=[ngtB])
            for s in (0, 1):
                if s == 1 and l == 1 and v >= 2:
                    continue
                mm_group(k, psm[s][:], psmB[s], [(L[:, s * 8 + kc, :], stage[r][:, kc, :]) for kc in range(8)],
                         [LB, stageB[r]])
                if v in (2, 5):
                    sc.op("dve", lambda: nc.vector.tensor_tensor(out=gbc[s][:, half * 512:(half + 1) * 512], in0=psm[s][:],
                                                                 in1=bias[r][:], op=ALU.add),
                          r=[psmB[s], biasB[r]], w=[gbcB[s]])
                    if half == 1:
                        mi = {(0, 0, 2): 0, (0, 0, 5): 1, (0, 1, 2): 2, (0, 1, 5): 3, (1, 0, 2): 4, (1, 0, 5): 5}[(l, s, v)]
                        sc.dma("sp", k.S["modbc"][mi], gbc[s][:], gbcB[s], r=[gbcB[s]], w=[k.Bmod[mi]])
                    continue
                sc.op("dve", lambda: nc.vector.tensor_tensor(out=tA[s][:], in0=psm[s][:], in1=bias[r][:], op=ALU.add),
                      r=[psmB[s], biasB[r]], w=[tAB[s]])
                src, srcB = tA[s], tAB[s]
                if v in (1, 4):
                    sc.op("dve", lambda: nc.vector.scalar_tensor_tensor(out=tB_[s][:], in0=tA[s][:], scalar=1.0,
                                                                         in1=ngt[:, half * 512:(half + 1) * 512],
                                                                         op0=ALU.add, op1=ALU.mult),
                          r=[tAB[s], ngtB], w=[tBB[s]])
                    src, srcB = tB_[s], tBB[s]
                sc.op("pool", lambda: nc.gpsimd.tensor_tensor(out=tC[s][:].rearrange("p a b -> p (a b)"), in0=src[:],
                                                              in1=k.identF4, op=ALU.mult),
                      r=[srcB, k.Bcf], w=[tCB[s]])
                vi = {0: 1, 1: 0, 3: 3, 4: 2}[v]
                mb = modidx(l, s, vi) + half * 4
                sc.op("dve", lambda: nc.vector.tensor_reduce(out=k.modcol[:, mb:mb + 4], in_=tC[s][:], axis=AX.X,
                                                             op=ALU.add),
                      r=[tCB[s]], w=[k.Bmodcol])
    if k.dev:
        sc.dma("sp", k.S["dbg"][:, 0:112], k.modcol[:, 0:112], k.Bmodcol, r=[k.Bmodcol])
    barrier(k)


def block_tiles(tb):
    if tb < 8:
        return tb * 4, 4, 512, False
    return 32, 2, 256, True


def phase_A0(k):
    nc, sc, st, I, S = k.nc, k.sc, k.pes, k.I, k.S
    Win = sbt(k, st, "a0_Win", [128, 8, 2560], BF16)
    WinB = _bufs("a0_Win", 5)
    wv = I["even_w_in"].rearrange("(kc p) n -> p kc n", p=128)
    for g in (1, 0, 2, 3, 4):
        sc.dma("pool", Win[:, :, g * 512:(g + 1) * 512], wv[:, :, g * 512:(g + 1) * 512], WinB[g], w=[WinB[g]])
    WoA = sbt(k, st, "a0_WoA", [128, 4, D], BF16)
    WoAB = Buf("a0_WoA")
    load_w(k, WoA, I["even_w_out"][0:512, :], WoAB)
    wsT = sbt(k, st, "a0_wsT", [128, 512], BF16)
    wsTB = Buf("a0_wsT")
    sc.dma("pool", wsT[:], I["sgu_wT"], wsTB, w=[wsTB])
    bsbc = sbt(k, st, "a0_bsbc", [128, 512], F32)
    bsB = Buf("a0_bsbc")
    sc.dma("sp", bsbc[:], I["sgu_b_bc"], bsB, w=[bsB])
    gbc = [sbt(k, st, "a0_gbc%d" % i, [128, D], F32) for i in range(2)]
    gbcB = _bufs("a0_gbc", 2)
    sc.dma("sp", gbc[0][:], S["modbc"][0], gbcB[0], r=[k.Bmod[0]], w=[gbcB[0]])
    sc.dma("sp", gbc[1][:], S["modbc"][2], gbcB[1], r=[k.Bmod[2]], w=[gbcB[1]])

    xt = [sbt(k, st, "a0_xt%d" % i, [128, D], F32) for i in range(8)]
    xtB = _bufs("a0_xt", 8)
    nx = NormCtx(k, st, "a0_")
    rc = RopeCtx(k, st, "a0_")
    aT2 = [sbt(k, st, "a0_aT%d" % i, [128, 8, 512], BF16) for i in range(2)]
    aT2B = _bufs("a0_aT", 2)
    guT = sbt(k, st, "a0_guT", [128, 4, 512], BF16)
    guB = Buf("a0_guT")
    vg = [sbt(k, st, "a0_vg%d" % i, [128, 4, 128], F32) for i in range(4)]
    vgB = _bufs("a0_vg", 4)
    lsum = [sbt(k, st, "a0_lsum%d" % i, [128, 4], F32) for i in range(4)]
    lnm = [sbt(k, st, "a0_lnm%d" % i, [128, 4], F32) for i in range(4)]
    lsq = [sbt(k, st, "a0_lsq%d" % i, [128, 4], F32) for i in range(4)]
    lsr = [sbt(k, st, "a0_lsr%d" % i, [128, 4], F32) for i in range(4)]
    lrs = [sbt(k, st, "a0_lrs%d" % i, [128, 4], F32) for i in range(4)]
    lsumB, lnmB, lsqB, lsrB, lrsB = (_bufs("a0_l%s" % n, 4) for n in "abcde")
    vf = [sbt(k, st, "a0_vf%d" % i, [128, 512], BF16) for i in range(4)]
    vfB = _bufs("a0_vf", 4)
    mx = [sbt(k, st, "a0_mx%d" % i, [128, 4, 128], F32) for i in range(2)]
    mxB = _bufs("a0_mx", 2)
    AoT = sbt(k, st, "a0_AoT", [128, 4, 512], BF16)
    AoB = _bufs("a0_AoT", 4)
    yt = [sbt(k, st, "a0_yt%d" % i, [128, 512], F32) for i in range(2)]
    ytB = _bufs("a0_yt", 2)
    vst = [sbt(k, st, "a0_vst%d" % i, [128, 512], BF16) for i in range(2)]
    vstB = _bufs("a0_vst", 2)
    psS = pst(k, st, "a0_psS", [128, 4, 128])
    psSB = Buf("a0_psS")
    psK = [pst(k, st, "a0_psK%d" % i, [128, 512]) for i in range(2)]
    psKB = _bufs("a0_psK", 2)
    psF = [pst(k, st, "a0_psF%d" % i, [128, 512]) for i in range(2)]
    psFB = _bufs("a0_psF", 2)
    cnt = {"K": 0, "F": 0}
    NB = int(os.environ.get("A0_BLOCKS", "9"))

    def xs(tb, i):
        return (tb % 2) * 4 + i

    def stage_load(tb):
        t0, ntl, N, isctx = block_tiles(tb)
        for i in range(ntl):
            j = xs(tb, i)
            sc.dma("sp", xt[j][:], I["x"][(t0 + i) * 128:(t0 + i + 1) * 128, :], xtB[j], w=[xtB[j]])
        if not isctx:
            rope_load(k, rc, t0 * 128, N, tb % 2)

    def stage_norm(tb):
        t0, ntl, N, isctx = block_tiles(tb)
        s = 1 if isctx else 0
        norm_stats(k, nx, [(xt[xs(tb, i)][:], xtB[xs(tb, i)]) for i in range(ntl)])
        for i in range(ntl):
            j = xs(tb, i)
            norm_xn(k, nx, xt[j][:], xtB[j], i)

    def stage_norm2(tb):
        t0, ntl, N, isctx = block_tiles(tb)
        s = 1 if isctx else 0
        for i in range(ntl):
            norm_tr(k, nx, i, modidx(0, s, 0), aT2[tb % 2], aT2B[tb % 2], i * 128)

    def stage_proj(tb):
        t0, ntl, N, isctx = block_tiles(tb)
        tok0 = t0 * 128
        aT, aTB = aT2[tb % 2], aT2B[tb % 2]
        rope_select(rc, tb % 2)
        if tb + 1 < NB:
            stage_load(tb + 1)
        for i in range(ntl):
            kk = cnt["K"] % 2
            cnt["K"] += 1
            r = i
            mm_group(k, psK[kk][:], psKB[kk], [(aT[:, kc, i * 128:(i + 1) * 128], Win[:, kc, 512:1024]) for kc in range(8)],
                     [WinB[1], aTB])
            sc.op("act", lambda: nc.scalar.activation(out=vg[r][:].rearrange("p a b -> p (a b)"), in_=psK[kk][:], func=AF.Gelu),
                  r=[psKB[kk]], w=[vgB[r]])
            sc.op("dve", lambda: nc.vector.tensor_reduce(out=lsum[r][:], in_=vg[r][:], axis=AX.X, op=ALU.add),
                  r=[vgB[r]], w=[lsumB[r]])
            sc.op("dve", lambda: nc.vector.tensor_scalar(out=lnm[r][:], in0=lsum[r][:], scalar1=-1.0 / 128, scalar2=None,
                                                         op0=ALU.mult), r=[lsumB[r]], w=[lnmB[r]])
        for g in range(4):
            f = cnt["F"] % 2
            cnt["F"] += 1
            mm_group(k, psF[f][:, 0:N], psFB[f], [(Win[:, kc, g * 128:(g + 1) * 128], aT[:, kc, 0:N]) for kc in range(8)],
                     [WinB[0], aTB])
            sc.op("act", lambda: nc.scalar.activation(out=guT[:, g, 0:N], in_=psF[f][:, 0:N], func=AF.Gelu),
                  r=[psFB[f]], w=[guB])
        for i in range(ntl):
            r = i
            for g in range(4):
                jj = nx.nj % 2
                nx.nj += 1
                sc.op("act", lambda: nc.scalar.activation(out=nx.junks[jj][:, 0:128], in_=vg[r][:, g, :], func=AF.Square,
                                                          bias=lnm[r][:, g:g + 1], accum_out=lsq[r][:, g:g + 1]),
                      r=[vgB[r], lnmB[r]], w=[lsqB[r], nx.junkB[jj]])
        for i in range(ntl):
            r = i
            sc.op("act", lambda: nc.scalar.activation(out=lsr[r][:], in_=lsq[r][:], func=AF.Sqrt, scale=1.0 / 128,
                                                      bias=k.epsc), r=[lsqB[r], k.Bcf], w=[lsrB[r]])
            sc.op("dve", lambda: nc.vector.reciprocal(out=lrs[r][:], in_=lsr[r][:]), r=[lsrB[r]], w=[lrsB[r]])
            for g in range(4):
                sc.op("dve", lambda: nc.vector.tensor_scalar(out=vf[r][:, g * 128:(g + 1) * 128], in0=vg[r][:, g, :],
                                                             scalar1=lnm[r][:, g:g + 1], scalar2=lrs[r][:, g:g + 1],
                                                             op0=ALU.add, op1=ALU.mult),
                      r=[vgB[r], lnmB[r], lrsB[r]], w=[vfB[r]])
        pend = None
        for c in range(8):
            f = cnt["F"] % 2
            cnt["F"] += 1
            mm_group(k, psF[f][:, 0:N], psFB[f],
                     [(Win[:, kc, 1024 + c * 128:1024 + (c + 1) * 128], aT[:, kc, 0:N]) for kc in range(8)],
                     [WinB[2 + c // 4], aTB])
            if pend is not None:
                pend()
            if c < 4:
                pend = rope_store(k, rc, psF[f][:, 0:N], psFB[f], N, not isctx, S["q"][c][:, tok0:tok0 + N], k.Bq[c][tb])
            else:
                pend = rope_store(k, rc, psF[f][:, 0:N], psFB[f], N, not isctx, S["k"][c - 4][:, tok0:tok0 + N], k.Bk[c - 4][tb])
            if c == 1 and tb + 1 < NB:
                stage_norm(tb + 1)
            if c == 5 and tb + 1 < NB:
                stage_norm2(tb + 1)
        for i in range(ntl):
            kk = cnt["K"] % 2
            cnt["K"] += 1
            mm_group(k, psK[kk][:], psKB[kk], [(aT[:, kc, i * 128:(i + 1) * 128], Win[:, kc, 2048:2560]) for kc in range(8)],
                     [WinB[4], aTB])
            if pend is not None:
                pend()
                pend = None
            v2 = (t0 + i) % 2
            sc.op("dve", lambda: nc.vector.tensor_copy(out=vst[v2][:], in_=psK[kk][:]), r=[psKB[kk]], w=[vstB[v2]])
            sc.dma("sp", S["v"][(t0 + i) * 128:(t0 + i + 1) * 128, :], vst[v2][:], vstB[v2], r=[vstB[v2]], w=[k.Bv[t0 + i]])

    def stage_tail(tb):
        t0, ntl, N, isctx = block_tiles(tb)
        s = 1 if isctx else 0
        for i in range(ntl):
            r = i
            m2 = i % 2
            j = xs(tb, i)
            for g in range(4):
                sc.op("pe", lambda: nc.tensor.matmul(psS[:, g, :], lhsT=vf[r][:, g * 128:(g + 1) * 128],
                                                     rhs=wsT[:, g * 128:(g + 1) * 128], start=True, stop=True),
                      r=[vfB[r], wsTB], w=[psSB], signal=(g == 3))
            sc.op("dve", lambda: nc.vector.tensor_tensor(out=mx[m2][:].rearrange("p a b -> p (a b)"),
                                                         in0=psS[:].rearrange("p a b -> p (a b)"), in1=bsbc[:], op=ALU.add),
                  r=[psSB, bsB], w=[mxB[m2]])
            sc.op("pool", lambda: nc.gpsimd.tensor_tensor(out=AoT[:, :, i * 128:(i + 1) * 128], in0=mx[m2][:],
                                                          in1=guT[:, :, i * 128:(i + 1) * 128], op=ALU.mult),
                  r=[mxB[m2], guB], w=[AoB[i]])
        for i in range(ntl):
            j = xs(tb, i)
            for nh in range(2):
                kk = cnt["K"] % 2
                cnt["K"] += 1
                mm_group(k, psK[kk][:], psKB[kk],
                         [(AoT[:, g, i * 128:(i + 1) * 128], WoA[:, g, nh * 512:(nh + 1) * 512]) for g in range(4)],
                         [AoB[i], WoAB])
                sc.op("dve", lambda: nc.vector.tensor_tensor(out=yt[nh][:], in0=psK[kk][:],
                                                             in1=gbc[s][:, nh * 512:(nh + 1) * 512], op=ALU.mult),
                      r=[psKB[kk], gbcB[s]], w=[ytB[nh]])
                sc.op("pool", lambda: nc.gpsimd.tensor_tensor(out=xt[j][:, nh * 512:(nh + 1) * 512], in0=yt[nh][:],
                                                              in1=xt[j][:, nh * 512:(nh + 1) * 512], op=ALU.add),
                      r=[ytB[nh], xtB[j]], w=[xtB[j]])
            sc.dma("sp", S["h"][(t0 + i) * 128:(t0 + i + 1) * 128, :], xt[j][:], xtB[j], r=[xtB[j]], w=[k.Bh[t0 + i]])

    stage_load(0)
    stage_norm(0)
    stage_norm2(0)
    for tb in range(NB):
        stage_proj(tb)
        stage_tail(tb)
    barrier(k)


def phase_B0(k):
    nc, sc, st, I, S = k.nc, k.sc, k.pes, k.I, k.S
    kT = sbt(k, st, "b0_kT", [128, 4, NTOK], BF16)
    kTB = _bufs("b0_kT", 4)
    for h in range(4):
        sc.dma("sp", kT[:, h, :], S["k"][h], kTB[h], r=[k.Bk[h][b] for b in range(9)], w=[kTB[h]])
    vA = sbt(k, st, "b0_vA", [128, NT, 512], BF16)
    vAB = Buf("b0_vA")
    vv = S["v"].rearrange("(t p) e -> p t e", p=128)
    for a, b in ((0, 17), (17, 34)):
        sc.dma("sp", vA[:, a:b, :], vv[:, a:b, :], vAB, r=[k.Bv[t] for t in range(a, b)], w=[vAB])
    lq = sbt(k, st, "b0_lq", [128, 256], F32)
    lqB = Buf("b0_lq")
    sc.dma("sp", lq[:], I["lqk_bc"], lqB, w=[lqB])
    sg = sbt(k, st, "b0_sg", [128, 1], F32)
    sgB = Buf("b0_sg")
    sc.dma("sp", sg[:], I["subln_col"], sgB, w=[sgB])
    prod = sbt(k, st, "b0_prod", [128, 2, 64], F32)
    s12 = sbt(k, st, "b0_s12", [128, 2], F32)
    e12 = sbt(k, st, "b0_e12", [128, 2], F32)
    dl = sbt(k, st, "b0_dl", [128, 1], F32)
    nl = sbt(k, st, "b0_nl", [128, 1], F32)
    gl = sbt(k, st, "b0_gl", [128, 1], F32)
    prodB, s12B, e12B, dlB, nlB, glB = (Buf("b0_" + n) for n in ("prod", "s12", "e12", "dl", "nl", "gl"))
    for j in range(2):
        sc.op("dve", lambda: nc.vector.tensor_tensor(out=prod[:, j, :], in0=lq[:, j * 128:j * 128 + 64],
                                                     in1=lq[:, j * 128 + 64:j * 128 + 128], op=ALU.mult),
              r=[lqB], w=[prodB])
    sc.op("dve", lambda: nc.vector.tensor_reduce(out=s12[:], in_=prod[:], axis=AX.X, op=ALU.add), r=[prodB], w=[s12B])
    sc.op("act", lambda: nc.scalar.activation(out=e12[:], in_=s12[:], func=AF.Exp), r=[s12B], w=[e12B])
    sc.op("dve", lambda: nc.vector.tensor_tensor(out=dl[:], in0=e12[:, 1:2], in1=e12[:, 0:1], op=ALU.subtract),
          r=[e12B], w=[dlB])
    sc.op("dve", lambda: nc.vector.tensor_scalar(out=nl[:], in0=dl[:], scalar1=-LAM_INIT0, scalar2=None, op0=ALU.add),
          r=[dlB], w=[nlB])
    sc.op("dve", lambda: nc.vector.tensor_scalar(out=gl[:], in0=sg[:], scalar1=1.0 - LAM_INIT0, scalar2=None, op0=ALU.mult),
          r=[sgB], w=[glB])

    psS = [pst(k, st, "b0_psS%d" % i, [128, 2, 512]) for i in range(2)]
    psSB = _bufs("b0_psS", 2)
    psO = [pst(k, st, "b0_psO%d" % i, [128, 512]) for i in range(2)]
    psOB = _bufs("b0_psO", 2)
    psL = [pst(k, st, "b0_psL%d" % i, [128, 512]) for i in range(2)]
    psLB = _bufs("b0_psL", 2)
    E = [sbt(k, st, "b0_E%d" % i, [128, 2, 512], BF16) for i in range(3)]
    EB = _bufs("b0_E", 3)
    qblk = [sbt(k, st, "b0_q%d" % i, [128, 512], BF16) for i in range(2)]
    qblkB = _bufs("b0_q", 2)
    Lsb = sbt(k, st, "b0_Lsb", [128, 512], F32)
    LsbB = Buf("b0_Lsb")
    R = [sbt(k, st, "b0_R%d" % i, [128, 512], F32) for i in range(2)]
    RB = _bufs("b0_R", 2)
    T = [sbt(k, st, "b0_T%d" % i, [128, 512], F32) for i in range(2)]
    TB = _bufs("b0_T", 2)
    ost2 = [sbt(k, st, "b0_ost%d" % i, [128, NTOK], F32) for i in range(2)]
    ostB2 = [_bufs("b0_ost%d_" % i, 9) for i in range(2)]
    sqs2 = [sbt(k, st, "b0_sqs%d" % i, [128, NTOK], BF16) for i in range(2)]
    sqsB2 = [_bufs("b0_sqs%d_" % i, 9) for i in range(2)]
    pend_p2 = []
    SD = [sbt(k, st, "b0_SD%d" % i, [128, 512], F32) for i in range(2)]
    SDB = _bufs("b0_SD", 2)
    RS = [sbt(k, st, "b0_RS%d" % i, [128, 512], F32) for i in range(2)]
    RSB = _bufs("b0_RS", 2)
    osg = [sbt(k, st, "b0_osg%d" % i, [128, 512], BF16) for i in range(2)]
    osgB = _bufs("b0_osg", 2)
    nH = int(os.environ.get("B0_HEADS", "4"))
    nQ = int(os.environ.get("B0_QB", "9"))
    its = [(h, qb) for h in range(nH) for qb in range(9 - nQ, 9)]
    cS = cE = 0
    pend_fin = []

    def qinfo(qb):
        return (512, list(range(NT))) if qb < 8 else (256, [32, 33])

    def load_q(n):
        h, qb = its[n]
        N, _ = qinfo(qb)
        sc.dma("sp", qblk[n % 2][:, 0:N], S["q"][h][:, qb * 512:qb * 512 + N], qblkB[n % 2], r=[k.Bq[h][qb]],
               w=[qblkB[n % 2]])

    load_q(0)
    for n, (h, qb) in enumerate(its):
        N, tiles = qinfo(qb)
        q0 = qb * 512
        qq, qqB = qblk[n % 2], qblkB[n % 2]
        ost, ostB, sqs, sqsB = ost2[h % 2], ostB2[h % 2], sqs2[h % 2], sqsB2[h % 2]
        if n + 1 < len(its):
            load_q(n + 1)
        nt = len(tiles)
        sl = {}

        def emit_S(i):
            nonlocal cS
            s = cS % 2
            cS += 1
            sl[i] = s
            kt = tiles[i]
            for j in range(2):
                sc.op("pe", lambda: nc.tensor.matmul(psS[s][:, j, 0:N], lhsT=kT[j * 64:(j + 1) * 64, h, kt * 128:(kt + 1) * 128],
                                                     rhs=qq[j * 64:(j + 1) * 64, 0:N], start=True, stop=True),
                      r=[kTB[h], qqB], w=[psSB[s]], signal=(j == 1))

        emit_S(0)
        for i in range(nt):
            if i + 1 < nt:
                emit_S(i + 1)
            if pend_fin and (i == 3 or i == nt - 1):
                pend_fin.pop()()
            if pend_p2 and qb < 8 and (i == 12 or (i == 24 and len(pend_p2) > 8 - qb)):
                pend_p2.pop(0)()
            s = sl[i]
            e = cE % 3
            cE += 1
            kt = tiles[i]
            sc.op("act", lambda: nc.scalar.activation(out=E[e][:, :, 0:N], in_=psS[s][:, :, 0:N], func=AF.Exp, scale=SCALE),
                  r=[psSB[s]], w=[EB[e]])
            for j in range(2):
                sc.op("pe", lambda: nc.tensor.matmul(psO[j][:, 0:N], lhsT=vA[:, kt, h * 128:(h + 1) * 128], rhs=E[e][:, j, 0:N],
                                                     start=(i == 0), stop=(i == nt - 1)),
                      r=[vAB, EB[e]], w=[psOB[j]], signal=False)
            for j in range(2):
                sc.op("pe", lambda: nc.tensor.matmul(psL[0][j * 32:(j + 1) * 32, 0:N], lhsT=k.ones[:, 0:32], rhs=E[e][:, j, 0:N],
                                                     start=(i == 0), stop=(i == nt - 1), skip_group_check=True),
                      r=[k.Bc, EB[e]], w=[psLB[0]], signal=(j == 1))
        for j in range(2):
            sc.op("dve", lambda: nc.vector.tensor_copy(out=T[j][:, 0:N], in_=psO[j][:, 0:N]), r=[psOB[j]], w=[TB[j]])
        sc.op("act", lambda: nc.scalar.copy(out=Lsb[0:64, 0:N], in_=psL[0][0:64, 0:N]), r=[psLB[0]], w=[LsbB])

        def fin(N=N, q0=q0, qb=qb, ost=ost, ostB=ostB, sqs=sqs, sqsB=sqsB):
            sc.op("dve", lambda: nc.vector.reciprocal(out=R[0][0:64, 0:N], in_=Lsb[0:64, 0:N]), r=[LsbB], w=[RB[0]])
            for j in (1, 0):
                sc.op("pe", lambda: nc.tensor.matmul(psL[1][:, 0:N], lhsT=k.sel[j], rhs=R[0][0:64, 0:N], start=True, stop=True),
                      r=[k.Bcf, RB[0]], w=[psLB[1]])
                sc.op("dve", lambda: nc.vector.tensor_tensor(out=T[j][:, 0:N], in0=T[j][:, 0:N], in1=psL[1][:, 0:N], op=ALU.mult),
                      r=[psLB[1], TB[j]], w=[TB[j]])
            sc.op("dve", lambda: nc.vector.scalar_tensor_tensor(out=ost[:, q0:q0 + N], in0=T[1][:, 0:N], scalar=nl[:, 0:1],
                                                                in1=T[0][:, 0:N], op0=ALU.mult, op1=ALU.add),
                  r=[TB[0], TB[1], nlB], w=[ostB[qb]])
            sc.op("pool", lambda: nc.gpsimd.tensor_tensor(out=sqs[:, q0:q0 + N], in0=ost[:, q0:q0 + N], in1=ost[:, q0:q0 + N],
                                                          op=ALU.mult), r=[ostB[qb]], w=[sqsB[qb]])
        pend_fin.append(fin)
        if qb == 8 or n + 1 == len(its):
            pend_fin.pop()()
        if qb == 8:
            for q2 in range(9 - nQ, 9):
                def unit(q2=q2, h=h, sqs=sqs, sqsB=sqsB):
                    N2, _ = qinfo(q2)
                    p0 = q2 * 512
                    x2 = q2 % 2
                    sc.op("pe", lambda: nc.tensor.matmul(psL[1][:, 0:N2], lhsT=k.ones, rhs=sqs[:, p0:p0 + N2], start=True, stop=True),
                          r=[k.Bc, sqsB[q2]], w=[psLB[1]])
                    sc.op("act", lambda: nc.scalar.activation(out=SD[x2][:, 0:N2], in_=psL[1][:, 0:N2], func=AF.Ln,
                                                              scale=1.0 / 128, bias=k.epsc), r=[psLB[1], k.Bcf], w=[SDB[x2]])
                    sc.op("act", lambda: nc.scalar.activation(out=RS[x2][:, 0:N2], in_=SD[x2][:, 0:N2], func=AF.Exp, scale=-0.5),
                          r=[SDB[x2]], w=[RSB[x2]])
                    sc.op("dve", lambda: nc.vector.scalar_tensor_tensor(out=osg[x2][:, 0:N2], in0=ost2[h % 2][:, p0:p0 + N2],
                                                                        scalar=gl[:, 0:1], in1=RS[x2][:, 0:N2],
                                                                        op0=ALU.mult, op1=ALU.mult),
                          r=[ostB2[h % 2][q2], glB, RSB[x2]], w=[osgB[x2]])
                    sc.dma("sp", S["o"][h][:, p0:p0 + N2], osg[x2][:, 0:N2], osgB[x2], r=[osgB[x2]], w=[k.Bo[h][q2]])
                pend_p2.append(unit)
    while pend_p2:
        pend_p2.pop(0)()
    barrier(k)


def phase_C1(k, l):
    nc, sc, st, I, S = k.nc, k.sc, k.pes, k.I, k.S
    tg = "c1%d_" % l
    nk = 4 if l == 0 else 8
    nblk = 9 if l == 0 else 8
    Wo = sbt(k, st, tg + "Wo", [128, nk, D], BF16)
    WoB = Buf(tg + "Wo")
    load_w(k, Wo, I["even_w_out"][512:1024, :] if l == 0 else I["odd_w_out"], WoB)
    Wfi = sbt(k, st, tg + "Wfi", [128, 8, 2 * FFN], BF16)
    WfiB = _bufs(tg + "Wfi", 11)
    wv = I["ffn_w_in"][l].rearrange("(kc p) n -> p kc n", p=128)
    for g in range(11):
        sc.dma("pool", Wfi[:, :, g * 512:(g + 1) * 512], wv[:, :, g * 512:(g + 1) * 512], WfiB[g], w=[WfiB[g]])
    gbc = [sbt(k, st, tg + "gbc%d" % i, [128, D], F32) for i in range(2)]
    gbcB = _bufs(tg + "gbc", 2)
    mi = 0 if l == 0 else 4
    sc.dma("sp", gbc[0][:], S["modbc"][mi], gbcB[0], r=[k.Bmod[mi]], w=[gbcB[0]])
    if l == 0:
        sc.dma("sp", gbc[1][:], S["modbc"][2], gbcB[1], r=[k.Bmod[2]], w=[gbcB[1]])
    ht = [sbt(k, st, tg + "ht%d" % i, [128, D], F32) for i in range(4)]
    htB = _bufs(tg + "ht", 4)
    ob = sbt(k, st, tg + "ob", [128, nk, 512], BF16)
    obB = Buf(tg + "ob")
    nx = NormCtx(k, st, tg)
    aT2 = [sbt(k, st, tg + "aT%d" % i, [128, 8, 512], BF16) for i in range(2)]
    aT2B = _bufs(tg + "aT", 2)
    yt = [sbt(k, st, tg + "yt%d" % i, [128, 512], F32) for i in range(2)]
    ytB = _bufs(tg + "yt", 2)
    sg = [sbt(k, st, tg + "sg%d" % i, [128, 512], F32) for i in range(2)]
    sgB = _bufs(tg + "sg", 2)
    hst = [sbt(k, st, tg + "hst%d" % i, [128, 512], BF16) for i in range(3)]
    hstB = _bufs(tg + "hst", 3)
    psK = [pst(k, st, tg + "psK%d" % i, [128, 512]) for i in range(2)]
    psKB = _bufs(tg + "psK", 2)
    psG = [pst(k, st, tg + "psG%d" % i, [128, 512]) for i in range(2)]
    psGB = _bufs(tg + "psG", 2)
    psU = [pst(k, st, tg + "psU%d" % i, [128, 512]) for i in range(2)]
    psUB = _bufs(tg + "psU", 2)
    ov = S["o"][0:nk].rearrange("c p n -> p c n")
    cnt = {"K": 0, "G": 0, "H": 0}

    def preA(tb):
        t0, ntl, N, isctx = block_tiles(tb)
        tok0 = t0 * 128
        s = 1 if isctx else 0
        for i in range(ntl):
            sc.dma("sp", ht[i][:], S["h"][(t0 + i) * 128:(t0 + i + 1) * 128, :], htB[i], r=[k.Bh[t0 + i]], w=[htB[i]])
        sc.dma("sp", ob[:, :, 0:N], ov[:, :, tok0:tok0 + N], obB, r=[k.Bo[c][tb] for c in range(nk)], w=[obB])
        for i in range(ntl):
            for nh in range(2):
                kk = cnt["K"] % 2
                cnt["K"] += 1
                mm_group(k, psK[kk][:], psKB[kk],
                         [(ob[:, c, i * 128:(i + 1) * 128], Wo[:, c, nh * 512:(nh + 1) * 512]) for c in range(nk)], [obB, WoB])
                sc.op("dve", lambda: nc.vector.tensor_tensor(out=yt[nh][:], in0=psK[kk][:], in1=gbc[s][:, nh * 512:(nh + 1) * 512],
                                                             op=ALU.mult), r=[psKB[kk], gbcB[s]], w=[ytB[nh]])
                sc.op("pool", lambda: nc.gpsimd.tensor_tensor(out=ht[i][:, nh * 512:(nh + 1) * 512], in0=yt[nh][:],
                                                              in1=ht[i][:, nh * 512:(nh + 1) * 512], op=ALU.add),
                      r=[ytB[nh], htB[i]], w=[htB[i]])
            sc.dma("sp", S["h"][(t0 + i) * 128:(t0 + i + 1) * 128, :], ht[i][:], htB[i], r=[htB[i]], w=[k.Bh[t0 + i]])
        norm_stats(k, nx, [(ht[i][:], htB[i]) for i in range(ntl)])

    def preB(tb):
        t0, ntl, N, isctx = block_tiles(tb)
        s = 1 if isctx else 0
        for i in range(ntl):
            norm_apply(k, nx, ht[i][:], htB[i], i, modidx(l, s, 2), aT2[tb % 2], aT2B[tb % 2], i * 128, on_act=True)

    preA(0)
    preB(0)
    for tb in range(nblk):
        t0, ntl, N, isctx = block_tiles(tb)
        tok0 = t0 * 128
        aT, aTB = aT2[tb % 2], aT2B[tb % 2]
        for f in range(NF):
            if tb + 1 < nblk and f == 4:
                preA(tb + 1)
            if tb + 1 < nblk and f == 13:
                preB(tb + 1)
            g2 = cnt["G"] % 2
            cnt["G"] += 1
            h3 = cnt["H"] % 3
            cnt["H"] += 1
            c0, c1 = f * 128, FFN + f * 128
            mm_group(k, psG[g2][:, 0:N], psGB[g2], [(Wfi[:, kc, c0:c0 + 128], aT[:, kc, 0:N]) for kc in range(8)],
                     [WfiB[c0 // 512], WfiB[(c0 + 127) // 512], aTB])
            mm_group(k, psU[g2][:, 0:N], psUB[g2], [(Wfi[:, kc, c1:c1 + 128], aT[:, kc, 0:N]) for kc in range(8)],
                     [WfiB[c1 // 512], WfiB[(c1 + 127) // 512], aTB])
            sc.op("act", lambda: nc.scalar.activation(out=sg[g2][:, 0:N], in_=psG[g2][:, 0:N], func=AF.Silu),
                  r=[psGB[g2]], w=[sgB[g2]])
            sc.op("dve", lambda: nc.vector.tensor_tensor(out=hst[h3][:, 0:N], in0=psU[g2][:, 0:N], in1=sg[g2][:, 0:N], op=ALU.mult),
                  r=[psUB[g2], sgB[g2]], w=[hstB[h3]])
            sc.dma("sp", S["hm"][f][:, tok0:tok0 + N], hst[h3][:, 0:N], hstB[h3], r=[hstB[h3]], w=[k.Bhm[f][tb]])
    barrier(k)


def phase_C2(k, l):
    nc, sc, st, I, S = k.nc, k.sc, k.pes, k.I, k.S
    tg = "c2%d_" % l
    nblk = 9 if l == 0 else 8
    last = l == DEPTH - 1
    Wfo = sbt(k, st, tg + "Wfo", [128, NF, D], BF16)
    WfoB = Buf(tg + "Wfo")
    load_w(k, Wfo, I["ffn_w_out"][l], WfoB, pieces=2)
    gbc = [sbt(k, st, tg + "gbc%d" % i, [128, D], F32) for i in range(2)]
    gbcB = _bufs(tg + "gbc", 2)
    mi = 1 if l == 0 else 5
    sc.dma("sp", gbc[0][:], S["modbc"][mi], gbcB[0], r=[k.Bmod[mi]], w=[gbcB[0]])
    if l == 0:
        sc.dma("sp", gbc[1][:], S["modbc"][3], gbcB[1], r=[k.Bmod[3]], w=[gbcB[1]])
    if last:
        fg = sbt(k, st, tg + "fg", [128, D], F32)
        fgB = Buf(tg + "fg")
        sc.dma("sp", fg[:], I["ng_bc"][4], fgB, w=[fgB])
        ot = [sbt(k, st, tg + "ot%d" % i, [128, D], F32) for i in range(2)]
        otB = _bufs(tg + "ot", 2)
    ht = [sbt(k, st, tg + "ht%d" % i, [128, D], F32) for i in range(4)]
    htB = _bufs(tg + "ht", 4)
    hmb = [sbt(k, st, tg + "hmb%d" % i, [128, NF, 512], BF16) for i in range(2)]
    hmbB = _bufs(tg + "hmb", 2)
    nx = NormCtx(k, st, tg)
    yt = [sbt(k, st, tg + "yt%d" % i, [128, 512], F32) for i in range(2)]
    ytB = _bufs(tg + "yt", 2)
    psK = [pst(k, st, tg + "psK%d" % i, [128, 512]) for i in range(4)]
    psKB = _bufs(tg + "psK", 4)
    hv = S["hm"].rearrange("f p n -> p f n")
    nK = nO = 0

    def loads(tb):
        t0, ntl, N, isctx = block_tiles(tb)
        hb = tb % 2
        sc.dma("sp", hmb[hb][:, :, 0:N], hv[:, :, t0 * 128:t0 * 128 + N], hmbB[hb], r=[k.Bhm[f][tb] for f in range(NF)],
               w=[hmbB[hb]])

    loads(0)
    for tb in range(nblk):
        t0, ntl, N, isctx = block_tiles(tb)
        s = 1 if isctx else 0
        hb = tb % 2
        for i in range(ntl):
            sc.dma("sp", ht[i][:], S["h"][(t0 + i) * 128:(t0 + i + 1) * 128, :], htB[i], r=[k.Bh[t0 + i]], w=[htB[i]])
        if tb + 1 < nblk:
            loads(tb + 1)
        for i in range(ntl):
            for nh in range(2):
                kk = nK % 4
                nK += 1
                mm_group(k, psK[kk][:], psKB[kk],
                         [(hmb[hb][:, f, i * 128:(i + 1) * 128], Wfo[:, f, nh * 512:(nh + 1) * 512]) for f in range(NF)],
                         [hmbB[hb], WfoB])
                sc.op("dve", lambda: nc.vector.tensor_tensor(out=yt[nh][:], in0=psK[kk][:], in1=gbc[s][:, nh * 512:(nh + 1) * 512],
                                                             op=ALU.mult), r=[psKB[kk], gbcB[s]], w=[ytB[nh]])
                sc.op("pool", lambda: nc.gpsimd.tensor_tensor(out=ht[i][:, nh * 512:(nh + 1) * 512], in0=yt[nh][:],
                                                              in1=ht[i][:, nh * 512:(nh + 1) * 512], op=ALU.add),
                      r=[ytB[nh], htB[i]], w=[htB[i]])
            if not last:
                sc.dma("sp", S["h"][(t0 + i) * 128:(t0 + i + 1) * 128, :], ht[i][:], htB[i], r=[htB[i]], w=[k.Bh[t0 + i]])
        if last:
            norm_stats(k, nx, [(ht[i][:], htB[i]) for i in range(ntl)])
            for i in range(ntl):
                o2 = nO % 2
                nO += 1
                sc.op("dve", lambda: nc.vector.scalar_tensor_tensor(out=ot[o2][:], in0=ht[i][:], scalar=nx.rs[:, i:i + 1],
                                                                    in1=fg[:], op0=ALU.mult, op1=ALU.mult),
                      r=[htB[i], nx.rsB, fgB], w=[otB[o2]])
                sc.dma("sp", k.out[(t0 + i) * 128:(t0 + i + 1) * 128, :], ot[o2][:], otB[o2], r=[otB[o2]])
    barrier(k)


def phase_C1_l0(k):
    phase_C1(k, 0)


def phase_C2_l0(k):
    phase_C2(k, 0)


def phase_C1_l1(k):
    phase_C1(k, 1)


def phase_C2_l1(k):
    phase_C2(k, 1)


def phase_A1(k):
    nc, sc, st, I, S = k.nc, k.sc, k.pes, k.I, k.S
    Wq = sbt(k, st, "a1_Wq", [128, 8, 1792], BF16)
    WqB = _bufs("a1_Wq", 4)
    wv = I["odd_w_qkv"].rearrange("(kc p) n -> p kc n", p=128)
    for g in range(2):
        sc.dma("pool", Wq[:, :, g * 512:(g + 1) * 512], wv[:, :, g * 512:(g + 1) * 512], WqB[g], w=[WqB[g]])
    for j in range(4):
        for e in range(2):
            sc.dma("pool", Wq[:, :, 1024 + j * 128 + e * 64:1024 + j * 128 + (e + 1) * 64],
                   wv[:, :, 1024 + j * 64:1024 + (j + 1) * 64], WqB[2], w=[WqB[2]])
    sc.dma("pool", Wq[:, :, 1536:1792], wv[:, :, 1280:1536], WqB[3], w=[WqB[3]])
    ht = [sbt(k, st, "a1_ht%d" % i, [128, D], F32) for i in range(4)]
    htB = _bufs("a1_ht", 4)
    nx = NormCtx(k, st, "a1_")
    rc = RopeCtx(k, st, "a1_")
    aT = sbt(k, st, "a1_aT", [128, 8, 512], BF16)
    aTB = Buf("a1_aT")
    vst = [sbt(k, st, "a1_vst%d" % i, [128, 256], BF16) for i in range(2)]
    vstB = _bufs("a1_vst", 2)
    psK = [pst(k, st, "a1_psK%d" % i, [128, 512]) for i in range(2)]
    psKB = _bufs("a1_psK", 2)
    psF = [pst(k, st, "a1_psF%d" % i, [128, 512]) for i in range(2)]
    psFB = _bufs("a1_psF", 2)
    cnt = {"K": 0, "F": 0}
    aT2 = [aT, sbt(k, st, "a1_aTb", [128, 8, 512], BF16)]
    aT2B = [aTB, Buf("a1_aTb")]

    def stage_load(tb):
        t0, ntl, N, isctx = block_tiles(tb)
        for i in range(ntl):
            sc.dma("sp", ht[i][:], S["h"][(t0 + i) * 128:(t0 + i + 1) * 128, :], htB[i], r=[k.Bh[t0 + i]], w=[htB[i]])
        if not isctx:
            rope_load(k, rc, t0 * 128, N, tb % 2)

    def stage_norm(tb):
        t0, ntl, N, isctx = block_tiles(tb)
        norm_stats(k, nx, [(ht[i][:], htB[i]) for i in range(ntl)])
        for i in range(ntl):
            norm_xn(k, nx, ht[i][:], htB[i], i)

    def stage_norm2(tb):
        t0, ntl, N, isctx = block_tiles(tb)
        s = 1 if isctx else 0
        for i in range(ntl):
            norm_tr(k, nx, i, modidx(1, s, 0), aT2[tb % 2], aT2B[tb % 2], i * 128)

    stage_load(0)
    stage_norm(0)
    stage_norm2(0)
    for tb in range(9):
        t0, ntl, N, isctx = block_tiles(tb)
        tok0 = t0 * 128
        aTc, aTcB = aT2[tb % 2], aT2B[tb % 2]
        rope_select(rc, tb % 2)
        if tb + 1 < 9:
            stage_load(tb + 1)
        pend = None
        for c in range(12):
            if isctx and c < 8:
                continue
            f = cnt["F"] % 2
            cnt["F"] += 1
            mm_group(k, psF[f][:, 0:N], psFB[f], [(Wq[:, kc, c * 128:(c + 1) * 128], aTc[:, kc, 0:N]) for kc in range(8)],
                     [WqB[0] if c < 4 else (WqB[1] if c < 8 else WqB[2]), aTcB])
            if pend is not None:
                pend()
            if c < 8:
                pend = rope_store(k, rc, psF[f][:, 0:N], psFB[f], N, True, S["q"][c][:, tok0:tok0 + N], k.Bq[c][tb])
            else:
                pend = rope_store(k, rc, psF[f][:, 0:N], psFB[f], N, not isctx, S["k"][c - 8][:, tok0:tok0 + N], k.Bk[c - 8][tb])
            if c == 1 and tb + 1 < 9:
                stage_norm(tb + 1)
            if c == 7 and tb + 1 < 9:
                stage_norm2(tb + 1)
        for i in range(ntl):
            kk = cnt["K"] % 2
            cnt["K"] += 1
            mm_group(k, psK[kk][:, 0:256], psKB[kk], [(aTc[:, kc, i * 128:(i + 1) * 128], Wq[:, kc, 1536:1792]) for kc in range(8)],
                     [WqB[3], aTcB])
            if pend is not None:
                pend()
                pend = None
            sc.op("dve", lambda: nc.vector.tensor_copy(out=vst[kk][:], in_=psK[kk][:, 0:256]), r=[psKB[kk]], w=[vstB[kk]])
            sc.dma("sp", S["v"][(t0 + i) * 128:(t0 + i + 1) * 128, 0:256], vst[kk][:], vstB[kk], r=[vstB[kk]], w=[k.Bv[t0 + i]])
    barrier(k)


def phase_B1(k):
    nc, sc, st, I, S = k.nc, k.sc, k.pes, k.I, k.S
    kT = sbt(k, st, "b1_kT", [128, 4, NTOK], BF16)
    kTB = _bufs("b1_kT", 4)
    for j in range(4):
        sc.dma("sp", kT[:, j, :], S["k"][j], kTB[j], r=[k.Bk[j][b] for b in range(9)], w=[kTB[j]])
    vA = sbt(k, st, "b1_vA", [128, NT, 256], BF16)
    vAB = Buf("b1_vA")
    vv = S["v"].rearrange("(t p) e -> p t e", p=128)
    for a, b in ((0, 17), (17, 34)):
        sc.dma("sp", vA[:, a:b, :], vv[:, a:b, 0:256], vAB, r=[k.Bv[t] for t in range(a, b)], w=[vAB])
    sk = sbt(k, st, "b1_sk", [128, 8], F32)
    es = sbt(k, st, "b1_es", [128, 8], F32)
    skB, esB = Buf("b1_sk"), Buf("b1_es")
    sc.dma("sp", sk[:], I["sink_col"], skB, w=[skB])
    sc.op("act", lambda: nc.scalar.activation(out=es[:], in_=sk[:], func=AF.Exp), r=[skB], w=[esB])
    psS = [pst(k, st, "b1_psS%d" % i, [128, 2, 512]) for i in range(2)]
    psSB = _bufs("b1_psS", 2)
    psO = [pst(k, st, "b1_psO%d" % i, [128, 512]) for i in range(2)]
    psOB = _bufs("b1_psO", 2)
    psL = [pst(k, st, "b1_psL%d" % i, [128, 512]) for i in range(2)]
    psLB = _bufs("b1_psL", 2)
    E = [sbt(k, st, "b1_E%d" % i, [128, 2, 512], BF16) for i in range(3)]
    EB8 = [[Buf("b1_E%d_%d" % (i, j)) for j in range(8)] for i in range(3)]

    def ebs(er, qa, qe):
        return [EB8[er][e * 4 + qt] for e in range(2) for qt in range(qa, qe)]
    qblk = [sbt(k, st, "b1_q%d" % i, [128, 512], BF16) for i in range(2)]
    qblkB = _bufs("b1_q", 2)
    LT = [sbt(k, st, "b1_LT%d" % i, [128, 512], F32) for i in range(2)]
    LTB = _bufs("b1_LT", 2)
    RR = [sbt(k, st, "b1_RR%d" % i, [128, 512], F32) for i in range(2)]
    RRB = _bufs("b1_RR", 2)
    osg = [sbt(k, st, "b1_osg%d" % i, [128, 512], BF16) for i in range(2)]
    osgB = _bufs("b1_osg", 2)
    its = [(qb, c) for qb in range(int(os.environ.get("B1_QB", "8"))) for c in range(8)]
    cS = cE = 0

    def load_q(n):
        qb, c = its[n]
        sc.dma("sp", qblk[n % 2][:], S["q"][c][:, qb * 512:(qb + 1) * 512], qblkB[n % 2], r=[k.Bq[c][qb]], w=[qblkB[n % 2]])

    load_q(0)
    for n, (qb, c) in enumerate(its):
        j = c // 2
        x2 = n % 2
        qq, qqB = qblk[x2], qblkB[x2]
        if n + 1 < len(its):
            load_q(n + 1)
        tl = [(32, 0, 4), (33, 0, 4)]
        for m in range(-1, 5):
            kt = 4 * qb + m
            if 0 <= kt <= 31:
                tl.append((kt, max(0, m - 1), min(3, m + 1) + 1))
        nt = len(tl)
        sl = {}

        def emit_S(i):
            nonlocal cS
            s = cS % 2
            cS += 1
            sl[i] = s
            kt, qa, qe = tl[i]
            for e in range(2):
                sc.op("pe", lambda: nc.tensor.matmul(psS[s][:, e, qa * 128:qe * 128],
                                                     lhsT=kT[e * 64:(e + 1) * 64, j, kt * 128:(kt + 1) * 128],
                                                     rhs=qq[e * 64:(e + 1) * 64, qa * 128:qe * 128], start=True, stop=True),
                      r=[kTB[j], qqB], w=[psSB[s]], signal=(e == 1))

        emit_S(0)
        for i in range(nt):
            if i + 1 < nt:
                emit_S(i + 1)
            s = sl[i]
            er = cE % 3
            cE += 1
            kt, qa, qe = tl[i]
            lo, hi = qa * 128, qe * 128
            sc.op("act", lambda: nc.scalar.activation(out=E[er][:, :, lo:hi], in_=psS[s][:, :, lo:hi], func=AF.Exp, scale=SCALE),
                  r=[psSB[s]], w=ebs(er, qa, qe))
            if kt < 32:
                for qt in range(qa, qe):
                    rel = kt - (4 * qb + qt)
                    if rel == 0:
                        continue
                    mk = k.mprev if rel == -1 else k.mnext
                    for e in range(2):
                        sc.op("dve", lambda: nc.vector.tensor_tensor(out=E[er][:, e, qt * 128:(qt + 1) * 128],
                                                                     in0=E[er][:, e, qt * 128:(qt + 1) * 128], in1=mk,
                                                                     op=ALU.mult), r=[EB8[er][e * 4 + qt], k.Bc], w=[EB8[er][e * 4 + qt]])
            for e in range(2):
                sc.op("pe", lambda: nc.tensor.matmul(psO[x2][e * 64:(e + 1) * 64, lo:hi], lhsT=vA[:, kt, j * 64:(j + 1) * 64],
                                                     rhs=E[er][:, e, lo:hi], start=(i == 0), stop=(i == nt - 1),
                                                     skip_group_check=True),
                      r=[vAB] + ebs(er, qa, qe), w=[psOB[x2]], signal=False)
            for e in range(2):
                sc.op("pe", lambda: nc.tensor.matmul(psL[x2][e * 64:(e + 1) * 64, lo:hi], lhsT=k.ones[:, 0:64],
                                                     rhs=E[er][:, e, lo:hi], start=(i == 0), stop=(i == nt - 1),
                                                     skip_group_check=True),
                      r=[k.Bc] + ebs(er, qa, qe), w=[psLB[x2]], signal=(e == 1))
        sc.op("act", lambda: nc.scalar.activation(out=LT[x2][:], in_=psL[x2][:], func=AF.Ln, bias=es[:, c:c + 1]),
              r=[psLB[x2], esB], w=[LTB[x2]])
        sc.op("act", lambda: nc.scalar.activation(out=RR[x2][:], in_=LT[x2][:], func=AF.Exp, scale=-1.0),
              r=[LTB[x2]], w=[RRB[x2]])
        sc.op("dve", lambda: nc.vector.tensor_tensor(out=osg[x2][:], in0=psO[x2][:], in1=RR[x2][:], op=ALU.mult),
              r=[psOB[x2], RRB[x2]], w=[osgB[x2]])
        sc.dma("sp", S["o"][c][:, qb * 512:(qb + 1) * 512], osg[x2][:], osgB[x2], r=[osgB[x2]], w=[k.Bo[c][qb]])
    barrier(k)


def host_consts():
    t = np.arange(SEQ)
    row = (t // GRID_W).astype(np.float32)
    col = (t % GRID_W).astype(np.float32)
    nf = HEAD_DIM // 4
    inv = (np.float32(10000.0) ** (-np.arange(nf, dtype=np.float32) / np.float32(nf))).astype(np.float32)
    p = np.arange(128)
    d = p % 64
    pos = np.where((d < 32)[:, None], row[None, :], col[None, :]).astype(np.float32)
    ang = (pos * inv[d % 16][:, None]).astype(np.float32)
    sign = np.where((d % 32) < 16, -1.0, 1.0).astype(np.float32)
    cosT = np.cos(ang).astype(np.float32)
    sinT = (np.sin(ang) * sign[:, None]).astype(np.float32)
    cbf = np.zeros((128, 768), np.float32)
    cbf[:, 0:128] = np.eye(128, dtype=np.float32)
    cbf[:, 128:256] = 1.0
    partner = np.where((p % 32) < 16, p + 16, p - 16)
    cbf[partner, 256 + p] = 1.0
    kl = np.arange(128)[:, None]
    ql = np.arange(128)[None, :]
    cbf[:, 384:512] = (kl >= ql).astype(np.float32)
    cbf[:, 512:640] = (kl <= ql).astype(np.float32)
    cf32 = np.zeros((128, 960), np.float32)
    cf32[0, 704:832] = 1.0
    cf32[32, 832:960] = 1.0
    cf32[:, 0:512] = np.tile(np.eye(128, dtype=np.float32), (1, 4))
    cf32[:, 512:640] = 1.0
    cf32[:, 640] = EPS
    return {"cosT": cosT, "sinT": sinT, "cbf": cbf, "cf32": cf32}


def host_shared(inp):
    f = lambda a: np.ascontiguousarray(np.asarray(a, dtype=np.float32))
    bc = lambda v: np.ascontiguousarray(np.broadcast_to(np.asarray(v, np.float32).reshape(1, -1), (128, np.asarray(v).size)))
    sh = dict(host_consts())
    sh["ada_w"] = f(inp["ada_w"])
    sh["ada_b_bc"] = np.stack([bc(inp["ada_b"][l]) for l in range(DEPTH)])
    sh["ng_bc"] = np.stack([bc(inp["norm1_g"][0]), bc(inp["norm2_g"][0]), bc(inp["norm1_g"][1]), bc(inp["norm2_g"][1]),
                            bc(inp["final_g"])])
    sh["ffn_w_in"] = f(inp["ffn_w_in"])
    sh["ffn_w_out"] = f(inp["ffn_w_out"])
    sh["even_w_in"] = f(inp["even_w_in"][0])
    sh["even_w_out"] = f(inp["even_w_out"][0])
    sh["sgu_wT"] = f(np.transpose(np.asarray(inp["sgu_w"][0]), (2, 0, 1)).reshape(128, 512))
    sh["sgu_b_bc"] = bc(np.asarray(inp["sgu_b"][0]).reshape(-1))
    sh["lqk_bc"] = bc(np.concatenate([np.asarray(inp[n][0]) for n in ("diff_lq1", "diff_lk1", "diff_lq2", "diff_lk2")]))
    sh["subln_col"] = f(np.asarray(inp["diff_subln_g"][0]).reshape(128, 1))
    sh["odd_w_qkv"] = f(inp["odd_w_qkv"][0])
    sh["odd_w_out"] = f(inp["odd_w_out"][0])
    sk = np.asarray(inp["odd_sink"][0], np.float32)
    sh["sink_col"] = f(np.stack([np.concatenate([np.full(64, sk[2 * c]), np.full(64, sk[2 * c + 1])]) for c in range(8)], 1))
    return sh


def host_core(inp, sh, b):
    m = dict(sh)
    m["x"] = np.ascontiguousarray(np.concatenate([np.asarray(inp["x"][b], np.float32), np.asarray(inp["ctx"][b], np.float32)], 0))
    cc = np.concatenate([np.asarray(inp["c"][b], np.float32).reshape(8, 128).T,
                         np.asarray(inp["c_ctx"], np.float32).reshape(8, 128).T], 1)
    m["c_col"] = np.ascontiguousarray(cc)
    return m


_NC_CACHE = {}


def kernel(**inputs):
    if "nc" not in _NC_CACHE:
        _NC_CACHE["nc"] = build()[0]
    nc = _NC_CACHE["nc"]
    sh = host_shared(inputs)
    n = 8
    in_maps = [host_core(inputs, sh, b) for b in range(n)]
    res = run_bass_kernel_spmd(nc, in_maps, core_ids=list(range(n)))
    return np.stack([np.asarray(r["out"], dtype=np.float32) for r in res.results], 0)
```
